# Optimizing a Trainium2 kernel written in Bass

```python
import math
import jax, jax.numpy as jnp
from jax import lax
import numpy as np

D_MODEL = 1024
BATCH = 2
SEQ = 8192
DEPTH = 1
DEC_BATCH = 32
DEC_SEQ = 4
PAST_LEN = 8192
PAGE_SIZE = 128

MIX_WIDTH = D_MODEL
HEAD_DIM = 64
ATT_WIDTH = MIX_WIDTH // 2
N_ATT_HEADS = ATT_WIDTH // HEAD_DIM
DILATED_BRANCHES = ((128, 1), (512, 4), (2048, 16))
MAX_WINDOW = 2048
ROPE_THETA = 10000.0
SSM_WIDTH = MIX_WIDTH - ATT_WIDTH
SSM_HEAD_DIM = 64
N_SSM_HEADS = SSM_WIDTH // SSM_HEAD_DIM
SSM_GROUPS = 2
SSM_STATE = 128
CONV_WIDTH = 4
SSD_CHUNK = 128
CONV_CH = SSM_WIDTH + 2 * SSM_GROUPS * SSM_STATE
IN_SIZES = (ATT_WIDTH, ATT_WIDTH, ATT_WIDTH, SSM_WIDTH, CONV_CH, N_SSM_HEADS)
IN_WIDTH = sum(IN_SIZES)
N_MEM = 256
N_XATT_HEADS = 4
XATT_HEAD_DIM = D_MODEL // N_XATT_HEADS
XATT_WIDTH = N_XATT_HEADS * XATT_HEAD_DIM
D_FF = 4 * D_MODEL
EPS = 1e-6
F32 = jnp.float32

kernel_name = 'dilated_ssd_hybrid_step'


def rmsnorm(x, g):
    xf = x.astype(F32)
    y = xf * lax.rsqrt(jnp.mean(xf * xf, axis=-1, keepdims=True) + EPS)
    return (y * g.astype(F32)).astype(x.dtype)


def rope(t, pos):
    half = HEAD_DIM // 2
    inv = ROPE_THETA ** (-jnp.arange(half, dtype=F32) * 2.0 / HEAD_DIM)
    ang = pos.astype(F32)[:, None] * inv[None, :]
    cos = jnp.cos(ang)[None, :, None, :]
    sin = jnp.sin(ang)[None, :, None, :]
    tf = t.astype(F32)
    t1, t2 = tf[..., :half], tf[..., half:]
    return jnp.concatenate([t1 * cos - t2 * sin, t2 * cos + t1 * sin], axis=-1).astype(t.dtype)


def softmax_stats(s):
    m = jnp.max(s, axis=-1, keepdims=True)
    p = jnp.exp(s - m)
    den = jnp.sum(p, axis=-1, keepdims=True)
    return p / den, (m + jnp.log(den))[..., 0]


def dilated_branch_prompt(q, k, v, window, dil):
    b, s_len, h, dh = q.shape
    n = window // dil
    span = n * dil
    lp = -(-s_len // span) * span
    m_len = lp // dil
    nb = m_len // n

    def to_blocks(t):
        t = jnp.pad(t, ((0, 0), (0, lp - s_len), (0, 0), (0, 0)))
        t = t.reshape(b, m_len, dil, h, dh).transpose(0, 2, 1, 3, 4)
        return t.reshape(b, dil, nb, n, h, dh)

    def with_prev(t):
        prev = jnp.pad(t, ((0, 0), (0, 0), (1, 0), (0, 0), (0, 0), (0, 0)))[:, :, :nb]
        return jnp.concatenate([prev, t], axis=3)

    qb = to_blocks(q)
    kb = with_prev(to_blocks(k))
    vb = with_prev(to_blocks(v))
    s = jnp.einsum('brjqhd,brjkhd->brjhqk', qb, kb) * (HEAD_DIM ** -0.5)
    a = jnp.arange(n)[:, None]
    c = jnp.arange(2 * n)[None, :]
    dist = a + n - c
    band = (dist >= 0) & (dist <= n)
    has_prev = (jnp.arange(nb) > 0)[:, None, None] | (c >= n)[None]
    mask = band[None] & has_prev
    s = jnp.where(mask[None, None, :, None], s, -jnp.inf)
    p, lse = softmax_stats(s)
    o = jnp.einsum('brjhqk,brjkhd->brjqhd', p, vb)
    o = o.reshape(b, dil, m_len, h, dh).transpose(0, 2, 1, 3, 4).reshape(b, lp, h, dh)[:, :s_len]
    lse = lse.transpose(0, 1, 2, 4, 3).reshape(b, dil, m_len, h)
    lse = lse.transpose(0, 2, 1, 3).reshape(b, lp, h)[:, :s_len]
    return o, lse


def dilated_branch_sample(q, k_all, v_all, window, dil):
    t_len = q.shape[1]
    lb = k_all.shape[1] - t_len
    n = window // dil
    idx = lb + jnp.arange(t_len)[:, None] - dil * jnp.arange(n + 1)[None, :]
    valid = idx >= 0
    idx = jnp.maximum(idx, 0)
    kg = k_all[:, idx]
    vg = v_all[:, idx]
    s = jnp.einsum('bthd,btkhd->bthk', q, kg) * (HEAD_DIM ** -0.5)
    s = jnp.where(valid[None, :, None, :], s, -jnp.inf)
    p, lse = softmax_stats(s)
    return jnp.einsum('bthk,btkhd->bthd', p, vg), lse


def combine_branches(outs, lses):
    w = jax.nn.softmax(jnp.stack(lses, axis=0), axis=0)
    return jnp.sum(w[..., None] * jnp.stack(outs, axis=0), axis=0)


def ssd_scan(xs, dt, a_neg, bm, cm, h0):
    b, l_len, nh, hp = xs.shape
    q = min(SSD_CHUNK, l_len)
    lp = -(-l_len // q) * q
    nc = lp // q

    def pad(t):
        return jnp.pad(t, ((0, 0), (0, lp - l_len)) + ((0, 0),) * (t.ndim - 2))

    rep = nh // SSM_GROUPS
    bh = jnp.repeat(pad(bm), rep, axis=2).reshape(b, nc, q, nh, SSM_STATE)
    ch = jnp.repeat(pad(cm), rep, axis=2).reshape(b, nc, q, nh, SSM_STATE)
    dtp = pad(dt)
    xdt = (pad(xs) * dtp[..., None]).reshape(b, nc, q, nh, hp)
    a = (dtp * a_neg).reshape(b, nc, q, nh).transpose(0, 3, 1, 2)
    a_cs = jnp.cumsum(a, axis=-1)
    tri = jnp.tril(jnp.ones((q, q), dtype=bool))
    decay_in = jnp.exp(jnp.where(tri, a_cs[..., :, None] - a_cs[..., None, :], -jnp.inf))
    scores = jnp.einsum('bclhn,bcshn->bhcls', ch, bh) * decay_in
    y_diag = jnp.einsum('bhcls,bcshp->bclhp', scores, xdt)
    to_end = jnp.exp(a_cs[..., -1:] - a_cs).transpose(0, 2, 3, 1)
    chunk_states = jnp.einsum('bclhn,bclhp->bchpn', bh, xdt * to_end[..., None])
    chunk_decay = jnp.exp(a_cs[..., -1])

    def step(h, inp):
        st, dec = inp
        return h * dec[..., None, None] + st, h

    h_last, h_start = lax.scan(step, h0, (chunk_states.transpose(1, 0, 2, 3, 4),
                                          chunk_decay.transpose(2, 0, 1)))
    h_start = h_start.transpose(1, 0, 2, 3, 4)
    from_start = jnp.exp(a_cs).transpose(0, 2, 3, 1)
    y_off = jnp.einsum('bclhn,bchpn->bclhp', ch, h_start) * from_start[..., None]
    y = (y_diag + y_off).reshape(b, lp, nh, hp)[:, :l_len]
    return y, h_last


def gated_rmsnorm(y, z, g):
    b, l_len, w = y.shape
    u = (y * jax.nn.silu(z.astype(F32))).reshape(b, l_len, SSM_GROUPS, w // SSM_GROUPS)
    u = u * lax.rsqrt(jnp.mean(u * u, axis=-1, keepdims=True) + EPS)
    return u.reshape(b, l_len, w) * g.astype(F32)


def mixer(h, pos, k_past, v_past, conv_prev, ssm_prev,
          w_in, conv_w, conv_b, dt_bias, a_log, d_skip, g_ssm, w_out):
    b, l_len, _ = h.shape
    offs = np.cumsum(IN_SIZES)[:-1].tolist()
    q, k, v, z, xbc, dt = jnp.split(h @ w_in, offs, axis=-1)
    q = rope(q.reshape(b, l_len, N_ATT_HEADS, HEAD_DIM), pos).astype(F32)
    k = rope(k.reshape(b, l_len, N_ATT_HEADS, HEAD_DIM), pos)
    v = v.reshape(b, l_len, N_ATT_HEADS, HEAD_DIM)
    outs, lses = [], []
    if k_past is None:
        for window, dil in DILATED_BRANCHES:
            o, l = dilated_branch_prompt(q, k.astype(F32), v.astype(F32), window, dil)
            outs.append(o)
            lses.append(l)
        lw = min(MAX_WINDOW, l_len)
        new_k, new_v = k[:, l_len - lw:], v[:, l_len - lw:]
        conv_prev = jnp.zeros((b, CONV_WIDTH - 1, CONV_CH), xbc.dtype)
        ssm_prev = jnp.zeros((b, N_SSM_HEADS, SSM_HEAD_DIM, SSM_STATE), F32)
    else:
        lb = k_past.shape[1]
        k_all = jnp.concatenate([k_past.astype(k.dtype), k], axis=1)
        v_all = jnp.concatenate([v_past.astype(v.dtype), v], axis=1)
        for window, dil in DILATED_BRANCHES:
            o, l = dilated_branch_sample(q, k_all.astype(F32), v_all.astype(F32), window, dil)
            outs.append(o)
            lses.append(l)
        new_k, new_v = k_all[:, -lb:], v_all[:, -lb:]
    att = combine_branches(outs, lses)

    xp = jnp.concatenate([conv_prev.astype(xbc.dtype), xbc], axis=1)
    new_conv = xp[:, -(CONV_WIDTH - 1):]
    xf = xp.astype(F32)
    conv = conv_b.astype(F32)
    for i in range(CONV_WIDTH):
        conv = conv + xf[:, i:i + l_len] * conv_w[i].astype(F32)
    conv = jax.nn.silu(conv)
    xs, bm, cm = jnp.split(conv, [SSM_WIDTH, SSM_WIDTH + SSM_GROUPS * SSM_STATE], axis=-1)
    xs = xs.reshape(b, l_len, N_SSM_HEADS, SSM_HEAD_DIM)
    bm = bm.reshape(b, l_len, SSM_GROUPS, SSM_STATE)
    cm = cm.reshape(b, l_len, SSM_GROUPS, SSM_STATE)
    dtv = jax.nn.softplus(dt.astype(F32) + dt_bias.astype(F32))
    a_neg = -jnp.exp(a_log.astype(F32))
    y, new_ssm = ssd_scan(xs, dtv, a_neg, bm, cm, ssm_prev.astype(F32))
    y = (y + d_skip.astype(F32)[:, None] * xs).reshape(b, l_len, SSM_WIDTH)
    y = gated_rmsnorm(y, z, g_ssm)
    mixed = jnp.concatenate([att.reshape(b, l_len, ATT_WIDTH), y], axis=-1).astype(h.dtype)
    return mixed @ w_out, (new_k, new_v, new_conv, new_ssm)


def memory_kv(mem, g_mem, w_mk, w_mv):
    b = mem.shape[0]
    m = rmsnorm(mem, g_mem)
    mk = (m @ w_mk).reshape(b, -1, N_XATT_HEADS, XATT_HEAD_DIM)
    mv = (m @ w_mv).reshape(b, -1, N_XATT_HEADS, XATT_HEAD_DIM)
    return mk, mv


def cross_attention(h, mem_k, mem_v, w_xq, w_xo):
    b, l_len, _ = h.shape
    q = (h @ w_xq).reshape(b, l_len, N_XATT_HEADS, XATT_HEAD_DIM).astype(F32)
    s = jnp.einsum('blhd,bmhd->bhlm', q, mem_k.astype(F32)) * (XATT_HEAD_DIM ** -0.5)
    p = jax.nn.softmax(s, axis=-1)
    o = jnp.einsum('bhlm,bmhd->blhd', p, mem_v.astype(F32)).reshape(b, l_len, XATT_WIDTH)
    return o.astype(h.dtype) @ w_xo


def trunk_layer(x, pos, k_past, v_past, conv_prev, ssm_prev, mem_k, mem_v,
                g_mix, w_in, conv_w, conv_b, dt_bias, a_log, d_skip, g_ssm, w_out,
                g_xatt, w_xq, w_xo, g_mlp, w_up, w_down):
    mix, new_state = mixer(rmsnorm(x, g_mix), pos, k_past, v_past, conv_prev, ssm_prev,
                           w_in, conv_w, conv_b, dt_bias, a_log, d_skip, g_ssm, w_out)
    x = x + mix
    x = x + cross_attention(rmsnorm(x, g_xatt), mem_k, mem_v, w_xq, w_xo)
    x = x + jnp.square(jax.nn.relu(rmsnorm(x, g_mlp) @ w_up)) @ w_down
    return x, new_state


def setup_inputs(seed: int = 0) -> dict:
    key = jax.random.key(seed)
    ks = list(jax.random.split(key, 32))

    def nrm(i, shape, scale):
        return jax.random.normal(ks[i], shape, F32) * scale

    lw = min(MAX_WINDOW, PAST_LEN)
    u = jax.random.uniform(ks[20], (DEPTH, N_SSM_HEADS), F32)
    dt0 = jnp.exp(u * (math.log(0.1) - math.log(1e-3)) + math.log(1e-3))
    return {
        'x_prompt': nrm(0, (BATCH, SEQ, D_MODEL), 1.0),
        'x_sample': nrm(1, (DEC_BATCH, DEC_SEQ, D_MODEL), 1.0),
        'cache_win_k': nrm(2, (DEPTH, DEC_BATCH, lw, N_ATT_HEADS, HEAD_DIM), 1.0),
        'cache_win_v': nrm(3, (DEPTH, DEC_BATCH, lw, N_ATT_HEADS, HEAD_DIM), 1.0),
        'state_conv': nrm(4, (DEPTH, DEC_BATCH, CONV_WIDTH - 1, CONV_CH), 1.0),
        'state_ssm': nrm(5, (DEPTH, DEC_BATCH, N_SSM_HEADS, SSM_HEAD_DIM, SSM_STATE), 0.5),
        'cache_mem_k': nrm(6, (DEPTH, DEC_BATCH, N_MEM, N_XATT_HEADS, XATT_HEAD_DIM), 1.0),
        'cache_mem_v': nrm(7, (DEPTH, DEC_BATCH, N_MEM, N_XATT_HEADS, XATT_HEAD_DIM), 1.0),
        'mem_prompt': nrm(8, (BATCH, N_MEM, D_MODEL), 1.0),
        'g_mix': 1.0 + nrm(9, (DEPTH, D_MODEL), 0.01),
        'w_in': nrm(10, (DEPTH, D_MODEL, IN_WIDTH), D_MODEL ** -0.5),
        'conv_w': nrm(11, (DEPTH, CONV_WIDTH, CONV_CH), CONV_WIDTH ** -0.5),
        'conv_b': nrm(12, (DEPTH, CONV_CH), 0.01),
        'dt_bias': dt0 + jnp.log(-jnp.expm1(-dt0)),
        'a_log': jnp.log(jax.random.uniform(ks[21], (DEPTH, N_SSM_HEADS), F32, 1.0, 16.0)),
        'd_skip': 1.0 + nrm(13, (DEPTH, N_SSM_HEADS), 0.01),
        'g_ssm': 1.0 + nrm(14, (DEPTH, SSM_WIDTH), 0.01),
        'w_out': nrm(15, (DEPTH, MIX_WIDTH, D_MODEL), MIX_WIDTH ** -0.5),
        'g_xatt': 1.0 + nrm(16, (DEPTH, D_MODEL), 0.01),
        'g_mem': 1.0 + nrm(17, (DEPTH, D_MODEL), 0.01),
        'w_xq': nrm(18, (DEPTH, D_MODEL, XATT_WIDTH), D_MODEL ** -0.5),
        'w_mk': nrm(19, (DEPTH, D_MODEL, XATT_WIDTH), D_MODEL ** -0.5),
        'w_mv': nrm(22, (DEPTH, D_MODEL, XATT_WIDTH), D_MODEL ** -0.5),
        'w_xo': nrm(23, (DEPTH, XATT_WIDTH, D_MODEL), XATT_WIDTH ** -0.5),
        'g_mlp': 1.0 + nrm(24, (DEPTH, D_MODEL), 0.01),
        'w_up': nrm(25, (DEPTH, D_MODEL, D_FF), D_MODEL ** -0.5),
        'w_down': nrm(26, (DEPTH, D_FF, D_MODEL), D_FF ** -0.5),
        'g_final': 1.0 + nrm(27, (D_MODEL,), 0.01),
    }


def reference(x_prompt, x_sample, cache_win_k, cache_win_v, state_conv, state_ssm,
              cache_mem_k, cache_mem_v, mem_prompt,
              g_mix, w_in, conv_w, conv_b, dt_bias, a_log, d_skip, g_ssm, w_out,
              g_xatt, g_mem, w_xq, w_mk, w_mv, w_xo, g_mlp, w_up, w_down, g_final):
    pos_p = jnp.arange(x_prompt.shape[1], dtype=jnp.int32)
    pos_s = PAST_LEN + jnp.arange(x_sample.shape[1], dtype=jnp.int32)
    hp, hs = x_prompt, x_sample
    wk_p, wv_p, cv_p, ss_p, mk_p, mv_p = [], [], [], [], [], []
    wk_s, wv_s, cv_s, ss_s = [], [], [], []
    for l in range(DEPTH):
        lw = (g_mix[l], w_in[l], conv_w[l], conv_b[l], dt_bias[l], a_log[l], d_skip[l],
              g_ssm[l], w_out[l], g_xatt[l], w_xq[l], w_xo[l], g_mlp[l], w_up[l], w_down[l])
        mk, mv = memory_kv(mem_prompt, g_mem[l], w_mk[l], w_mv[l])
        hp, (nk, nv, nc, ns) = trunk_layer(hp, pos_p, None, None, None, None, mk, mv, *lw)
        wk_p.append(nk)
        wv_p.append(nv)
        cv_p.append(nc)
        ss_p.append(ns)
        mk_p.append(mk)
        mv_p.append(mv)
        hs, (nk, nv, nc, ns) = trunk_layer(hs, pos_s, cache_win_k[l], cache_win_v[l],
                                           state_conv[l], state_ssm[l],
                                           cache_mem_k[l], cache_mem_v[l], *lw)
        wk_s.append(nk)
        wv_s.append(nv)
        cv_s.append(nc)
        ss_s.append(ns)
    y_prompt = rmsnorm(hp, g_final)
    y_sample = rmsnorm(hs, g_final)
    return (y_prompt, y_sample,
            jnp.stack(wk_p), jnp.stack(wv_p), jnp.stack(cv_p), jnp.stack(ss_p),
            jnp.stack(mk_p), jnp.stack(mv_p),
            jnp.stack(wk_s), jnp.stack(wv_s), jnp.stack(cv_s), jnp.stack(ss_s))
```

```python
import numpy as np
from contextlib import ExitStack
import concourse.bass as bass
import concourse.mybir as mybir
from concourse.bass_utils import run_bass_kernel_spmd

F32 = mybir.dt.float32
BF16 = mybir.dt.bfloat16
ALU = mybir.AluOpType
AF = mybir.ActivationFunctionType
AX = mybir.AxisListType

D = 1024
NPRE = 48
NOWN = 16
NT = NPRE + NOWN
ST = 64
NEG = -30000.0
EPS = 1e-6
NDS = 56


class Res:
    __slots__ = ("w", "r")

    def __init__(self):
        self.w = None
        self.r = {}


class TT:
    def __init__(self, h, res=None):
        self.h = h
        self.res = res if res is not None else Res()

    def __getitem__(self, k):
        return self.h[k]


class KB:
    def __init__(self, nc, es):
        self.nc = nc
        self.es = es
        self.eng = {}
        for name, e in (("pe", nc.tensor), ("act", nc.scalar), ("dve", nc.vector),
                        ("pool", nc.gpsimd), ("sp", nc.sync)):
            sem = es.enter_context(nc.semaphore("s_" + name))
            self.eng[name] = dict(e=e, sem=sem, cnt=0, known={})
        self.ds = [dict(sem=es.enter_context(nc.semaphore("d%d" % i)), cnt=0) for i in range(NDS)]
        self.dn = {"sp": 0, "pool": 0}
        self.drange = {"sp": (0, 32), "pool": (32, NDS)}
        self.uid = 0

    def _sem(self, key):
        return self.eng[key]["sem"] if isinstance(key, str) else self.ds[key[1]]["sem"]

    def _wait(self, en, deps):
        E = self.eng[en]
        for key, val in deps:
            if val <= 0:
                continue
            if key == en and en == "pe":
                continue
            if E["known"].get(key, 0) >= val:
                continue
            E["e"].wait_ge(self._sem(key), val)
            E["known"][key] = val

    @staticmethod
    def _deps(r, w):
        deps = []
        for t in r:
            if t.res.w is not None:
                deps.append(t.res.w)
        for t in w:
            if t.res.w is not None:
                deps.append(t.res.w)
            deps.extend(t.res.r.items())
        return deps

    @staticmethod
    def _upd(tok, r, w):
        for t in r:
            t.res.r[tok[0]] = tok[1]
        for t in w:
            t.res.w = tok
            t.res.r = {}

    def op(self, en, fn, r=(), w=()):
        E = self.eng[en]
        self._wait(en, self._deps(r, w))
        ins = fn(E["e"])
        E["cnt"] += 1
        ins.then_inc(E["sem"], 1)
        self._upd((en, E["cnt"]), r, w)

    def dma(self, q, out, in_, r=(), w=()):
        E = self.eng[q]
        lo, hi = self.drange[q]
        i = lo + self.dn[q]
        self.dn[q] = (self.dn[q] + 1) % (hi - lo)
        d = self.ds[i]
        deps = self._deps(r, w)
        deps.append((("d", i), d["cnt"]))
        self._wait(q, deps)
        ins = E["e"].dma_start(out=out, in_=in_)
        d["cnt"] += 16
        ins.then_inc(d["sem"], 16)
        self._upd((("d", i), d["cnt"]), r, w)

    def barrier(self, full=False):
        allv = [(n, e["cnt"]) for n, e in self.eng.items() if n != "sp"]
        allv += [(("d", i), d["cnt"]) for i, d in enumerate(self.ds) if full or i < 32]
        for en in self.eng:
            E = self.eng[en]
            for key, val in allv:
                if val <= 0 or key == en:
                    continue
                if E["known"].get(key, 0) >= val:
                    continue
                E["e"].wait_ge(self._sem(key), val)
                E["known"][key] = val

    def sb(self, es, shape, dt, name=None):
        self.uid += 1
        return TT(es.enter_context(self.nc.sbuf_tensor("%s_%d" % (name or "t", self.uid), list(shape), dt)))


def bc_rows(dram_ap, n):
    return bass.AP(dram_ap.tensor, dram_ap.offset, [[0, 128], [1, n]])


def build():
    import os
    LEVEL = int(os.environ.get("KLEVEL", "99"))
    nc = bass.Bass("TRN2", target_bir_lowering=False)

    def din(name, shape, dt=F32):
        return nc.dram_tensor(name, list(shape), dt, kind="ExternalInput").ap()

    def dout(name, shape, dt=F32):
        return nc.dram_tensor(name, list(shape), dt, kind="ExternalOutput").ap()

    def dscr(name, shape, dt=BF16):
        return nc.dram_tensor(name, list(shape), dt, kind="Internal").ap()

    xcat = din("xcat", [NT * 128, D])
    xsin = din("xs", [128, D])
    posd = din("pos", [128, 65])
    roped = din("rope", [65, 128, 64])
    ck = din("ck", [4, 2048, 512])
    cv = din("cv", [4, 2048, 512])
    sconv = din("sconv", [4, 3, D])
    sssm = din("sssm", [4, 512, 128])
    cmk = din("cmk", [4, 256, D])
    cmv = din("cmv", [4, 256, D])
    memd = din("mem", [256, D])
    w_in = din("w_in", [D, 3080])
    w_out = din("w_out", [D, D])
    w_xq = din("w_xq", [D, D])
    w_mk = din("w_mk", [D, D])
    w_mv = din("w_mv", [D, D])
    w_xo = din("w_xo", [D, D])
    w_up = din("w_up", [D, 4096])
    w_down = din("w_down", [4096, D])
    gcols_d = din("gcols", [128, 32])
    gfin_d = din("gfin", [1, D])
    convw_d = din("convw", [4, D])
    convb_d = din("convb", [1, D])
    small_d = din("small", [3, 8])
    gssm_d = din("gssm", [1, 512])
    c_ident = din("c_ident", [128, 128])
    c_tri = din("c_tri", [128, 128])
    c_mneg = din("c_mneg", [128, 128])
    c_cm = din("c_cm", [128, 17 * 128])
    c_sh = din("c_sh", [128, 7 * 128])
    c_shs = din("c_shs", [128, 4 * 128])
    c_cs = din("c_cs", [128, 64 * 16])
    c_cn = din("c_cn", [128, 16])
    c_cc = din("c_cc", [128, 4 * 16])
    c_rm = din("c_rm", [128, 4])
    y_o = dout("y", [NOWN * 128, D])
    ys_o = dout("ys", [128, D])
    wk_o = dout("wk", [NOWN * 128, 512])
    wv_o = dout("wv", [NOWN * 128, 512])
    convp_o = dout("convp", [3, D])
    ssmp_o = dout("ssmp", [512, 128])
    memk_o = dout("memk", [256, D])
    memv_o = dout("memv", [256, D])
    wks_o = dout("wks", [4, 2048, 512])
    wvs_o = dout("wvs", [4, 2048, 512])
    convs_o = dout("convs", [4, 3, D])
    ssms_o = dout("ssms", [4, 512, 128])
    wb_in = dscr("wb_in", [D, 3080])
    wb_out = dscr("wb_out", [D, D])
    wb_xq = dscr("wb_xq", [D, D])
    wb_mk = dscr("wb_mk", [D, D])
    wb_mv = dscr("wb_mv", [D, D])
    wb_xo = dscr("wb_xo", [D, D])
    wb_up = dscr("wb_up", [D, 4096])
    wb_down = dscr("wb_down", [4096, D])

    with ExitStack() as es:
        K = KB(nc, es)
        wres = {}

        def convert(items):
            for src, dst, rows in items:
                if dst.name not in wres:
                    wres[dst.name] = TT(None)
                step = 512
                for r0 in range(0, rows, step):
                    K.dma("pool", dst[r0:r0 + step, :], src[r0:r0 + step, :], w=[wres[dst.name]])

        wres["in_a"] = TT(None)
        wres["in_b"] = TT(None)
        for r0 in range(0, D, 512):
            K.dma("pool", wb_in[r0:r0 + 512, 0:1536], w_in[r0:r0 + 512, 0:1536], w=[wres["in_a"]])
        convert([(w_mk, wb_mk, D), (w_mv, wb_mv, D)])
        for r0 in range(0, D, 512):
            K.dma("pool", wb_in[r0:r0 + 512, 1536:3080], w_in[r0:r0 + 512, 1536:3080], w=[wres["in_b"]])
        for _w in (wb_out, wb_xq, wb_xo, wb_up, wb_down):
            wres[_w.name] = TT(None)

        late_q = []
        for _src, _dst, _rows in ((w_out, wb_out, D), (w_xq, wb_xq, D), (w_xo, wb_xo, D), (w_up, wb_up, D),
                                  (w_down, wb_down, 4096)):
            for r0 in range(0, _rows, 512):
                late_q.append(lambda _src=_src, _dst=_dst, r0=r0: K.dma(
                    "pool", _dst[r0:r0 + 512, :], _src[r0:r0 + 512, :], w=[wres[_dst.name]]))
        for bl in range(4):
            late_q.append(lambda bl=bl: K.dma("pool", wks_o[bl, 0:2044, :], ck[bl, 4:2048, :]))
            late_q.append(lambda bl=bl: K.dma("pool", wvs_o[bl, 0:2044, :], cv[bl, 4:2048, :]))

        def late_dmas(n=None):
            k = len(late_q) if n is None else min(n, len(late_q))
            for _ in range(k):
                late_q.pop(0)()
        pf = [TT(es.enter_context(nc.psum_tensor("pf%d" % i, [128, 512], F32))) for i in range(6)]
        pb = [TT(es.enter_context(nc.psum_tensor("pb%d" % i, [128, 1024], BF16))) for i in range(2)]
        st_ = dict(pf=0, pb=0)

        def npf():
            st_["pf"] = (st_["pf"] + 1) % 4
            return pf[st_["pf"]]

        def npb():
            st_["pb"] = (st_["pb"] + 1) % 2
            return pb[st_["pb"]]

        def cload(src_ap, shape, dt=F32, q="sp", stk=None):
            stk = stk if stk is not None else es
            t = K.sb(stk, shape, F32, "cst")
            K.dma(q, t[:], src_ap, w=[t])
            if dt == F32:
                return t
            tb = K.sb(stk, shape, dt, "cstb")
            K.op("dve", lambda e: e.tensor_copy(tb[:], t[:]), r=[t], w=[tb])
            return tb

        def cload_bf(stk, specs):
            outs = [K.sb(stk, shape, BF16, "cbf") for _, shape in specs]
            with ExitStack() as tmp:
                for (src_ap, shape), o in zip(specs, outs):
                    t = K.sb(tmp, shape, F32, "cst")
                    K.dma("sp", t[:], src_ap, w=[t])
                    K.op("dve", lambda e, o=o, t=t: e.tensor_copy(o[:], t[:]), r=[t], w=[o])
                K.barrier()
            return outs

        identf = cload(c_ident, [128, 128])
        identb = K.sb(es, [128, 128], BF16, "identb")
        K.op("dve", lambda e: e.tensor_copy(identb[:], identf[:]), r=[identf], w=[identb])
        onesf = K.sb(es, [128, 128], F32, "onesf")
        K.op("dve", lambda e: e.memset(onesf[:], 1.0), w=[onesf])
        epsT = K.sb(es, [128, 1], F32, "epsT")
        K.op("dve", lambda e: e.memset(epsT[:], EPS), w=[epsT])
        onesb = K.sb(es, [128, 128], BF16, "onesb")
        K.op("dve", lambda e: e.memset(onesb[:], 1.0), w=[onesb])
        gcols = cload(gcols_d, [128, 32])
        pos = cload(posd, [128, 65])
        valid = K.sb(es, [128, 65], F32, "valid")
        K.op("dve", lambda e: e.tensor_single_scalar(valid[:], pos[:], 0.0, ALU.is_ge), r=[pos], w=[valid])
        kbias = K.sb(es, [128, 65], F32, "kbias")
        K.op("dve", lambda e: e.tensor_scalar(kbias[:], valid[:], -1.0, -NEG, ALU.add, ALU.mult),
             r=[valid], w=[kbias])
        attT = K.sb(es, [128, 4, (NOWN + 1) * 128], BF16, "attT")
        yT = K.sb(es, [128, 4, (NOWN + 1) * 128], BF16, "yT")
        K.barrier()

        def wload(es2, src_b, kchunks, ncols, col0=0, name="w", q="sp", res=None):
            t = K.sb(es2, [128, kchunks, ncols], BF16, name)
            v = src_b.rearrange("(kc p) n -> p kc n", p=128)
            for kc0 in range(0, kchunks, 4):
                K.dma(q, t[:, kc0:kc0 + 4, :], v[:, kc0:kc0 + 4, col0:col0 + ncols], r=[wres[res or src_b.name]], w=[t])
            return t

        def norm_T(wk, x, gi, dst, dst_ap):
            junk, ss, rstd, xn = wk["junk"], wk["ss"], wk["rstd"], wk["xn"]
            K.op("act", lambda e: e.activation(junk[:], x[:], AF.Square, accum_out=ss[:, 0:1]), r=[x], w=[junk, ss])
            K.op("act", lambda e: e.activation(rstd[:], ss[:], AF.Ln, scale=1.0 / D, bias=epsT[:, 0:1]), r=[ss, epsT], w=[rstd])
            K.op("act", lambda e: e.activation(rstd[:], rstd[:], AF.Exp, scale=-0.5), r=[rstd], w=[rstd])
            if wk.get("xn_eng", "act") == "dve":
                K.op("dve", lambda e: e.tensor_scalar(xn[:], x[:], rstd[:, 0:1], None, ALU.mult), r=[x, rstd], w=[xn])
            else:
                K.op("act", lambda e: e.activation(xn[:], x[:], AF.Copy, scale=rstd[:, 0:1]), r=[x, rstd], w=[xn])
            p = npb()
            for j in range(8):
                K.op("pe", lambda e, j=j: e.transpose(p[:, j * 128:(j + 1) * 128], xn[:, j * 128:(j + 1) * 128], identb[:]),
                     r=[xn, identb], w=[p])
            g = gcols[:, gi * 8:(gi + 1) * 8].unsqueeze(2).to_broadcast([128, 8, 128])
            K.op("dve", lambda e: e.tensor_tensor(dst_ap, p[:, :].rearrange("p (k t) -> p k t", t=128), g, ALU.mult),
                 r=[p, gcols], w=[dst])
            return rstd

        def proj(ps, ncols, hT, hT_ap, W, col0):
            for kc in range(8):
                K.op("pe", lambda e, kc=kc: e.matmul(ps[:, 0:ncols], hT_ap[:, kc, :], W[:, kc, col0:col0 + ncols],
                                                       start=(kc == 0), stop=(kc == 7)),
                     r=[hT, W], w=[ps])

        def transpose4(src, src_ap_fn, dst, dst_ap):
            p = npb()
            for j in range(4):
                K.op("pe", lambda e, j=j: e.transpose(p[:, j * 128:(j + 1) * 128], src_ap_fn(j), identb[:]),
                     r=[src, identb], w=[p])
            K.op("act", lambda e: e.copy(dst_ap, p[:, 0:512]), r=[p], w=[dst])

        def mem_prepare(es2, wk, ktok, vtok, KTm, Vm):
            kb = wk["kb"]
            K.op("act", lambda e: e.copy(kb[:], ktok[:]), r=[ktok], w=[kb])
            K.op("pool", lambda e: e.tensor_copy(Vm[:], vtok[:]), r=[vtok], w=[Vm])
            for mb in range(2):
                for half in range(2):
                    p = npb()
                    for j in range(4):
                        c = half * 4 + j
                        K.op("pe", lambda e, j=j, c=c, mb=mb: e.transpose(p[:, j * 128:(j + 1) * 128],
                                                                         kb[:, mb, c * 128:(c + 1) * 128], identb[:]),
                             r=[kb, identb], w=[p])
                    K.op("dve", lambda e, mb=mb, half=half: e.tensor_copy(
                        KTm[:, half * 4:half * 4 + 4, mb * 128:(mb + 1) * 128],
                        p[:, 0:512].rearrange("p (c m) -> p c m", m=128)), r=[p], w=[KTm])

        def xattn(wk, qT, qT_ap, n, KTm, Vm, OT, OT_ap):
            def st1(h):
                Ps = []
                for mb in range(2):
                    s = npf()
                    for dc in range(2):
                        K.op("pe", lambda e, dc=dc, mb=mb, s=s: e.matmul(
                            s[:, 0:n], KTm[:, 2 * h + dc, mb * 128:(mb + 1) * 128], qT_ap[:, 2 * h + dc, :],
                            start=(dc == 0), stop=(dc == 1)), r=[KTm, qT], w=[s])
                    P = wk["xP"][2 * (h % 2) + mb]
                    K.op("act", lambda e, s=s, P=P: e.activation(P[:, 0:n], s[:, 0:n], AF.Exp), r=[s], w=[P])
                    Ps.append(P)
                return Ps

            def st2(h, Ps):
                dn = npf()
                for mb in range(2):
                    K.op("pe", lambda e, mb=mb: e.matmul(dn[:, 0:n], onesb[:], Ps[mb][:, 0:n],
                                                         start=(mb == 0), stop=(mb == 1)), r=[onesb, Ps[mb]], w=[dn])
                rec = wk["xrec"][h % 2]
                K.op("dve", lambda e: e.reciprocal(rec[:, 0:n], dn[:, 0:n]), r=[dn], w=[rec])
                for dc in range(2):
                    o = npf()
                    for mb in range(2):
                        K.op("pe", lambda e, mb=mb, dc=dc, o=o: e.matmul(
                            o[:, 0:n], Vm[:, mb, h * 256 + dc * 128:h * 256 + dc * 128 + 128], Ps[mb][:, 0:n],
                            start=(mb == 0), stop=(mb == 1)), r=[Vm, Ps[mb]], w=[o])
                    K.op("dve", lambda e, dc=dc, o=o: e.tensor_tensor(OT_ap[:, 2 * h + dc, :], o[:, 0:n], rec[:, 0:n], ALU.mult),
                         r=[o, rec], w=[OT])

            prev = None
            for h in range(4):
                Ps = st1(h)
                if prev is not None:
                    st2(*prev)
                prev = (h, Ps)
            st2(*prev)

        KTm_p = K.sb(es, [128, 8, 256], BF16, "KTm_p")
        Vm_p = K.sb(es, [128, 2, D], BF16, "Vm_p")

        with ExitStack() as e2:
          if LEVEL >= 2:
            NB = 3
            LAG = NB - 1
            NR = 18
            junk_ = K.sb(e2, [128, D], F32)
            wks_ = [dict(junk=junk_, ss=K.sb(e2, [128, 1], F32), rstd=K.sb(e2, [128, 1], F32),
                         xn=K.sb(e2, [128, D], BF16)) for _ in range(2)]
            cm, ccs, ccn, ccc = cload_bf(e2, [(c_cm, [128, 17 * 128]), (c_cs, [128, 64 * 16]), (c_cn, [128, 16]),
                                              (c_cc, [128, 4 * 16])])
            wqkv = wload(e2, wb_in, 8, 1536, name="wqkv", res="in_a")
            KT = [K.sb(e2, [128, 4, 128], BF16, "KT") for _ in range(NR)]
            VX = [K.sb(e2, [128, 8, 72], BF16, "VX") for _ in range(NR)]
            for v_ in VX:
                K.op("pool", lambda e, v_=v_: e.memset(v_[:], 1.0), w=[v_])
            xts = [K.sb(e2, [128, D], F32, "xt") for _ in range(2)]
            hTs = [K.sb(e2, [128, 8, 128], BF16) for _ in range(2)]
            css = [K.sb(e2, [128, 64], F32) for _ in range(2)]
            qks = [K.sb(e2, [128, 16, 64], F32) for _ in range(2)]
            ros = [K.sb(e2, [128, 16, 64], F32) for _ in range(2)]
            tas = [K.sb(e2, [128, 16, 32], F32)] * 2
            tbs = [K.sb(e2, [128, 16, 32], F32)] * 2
            qkbs = [K.sb(e2, [128, 16, 64], BF16) for _ in range(2)]
            QTs = [K.sb(e2, [128, 4, 128], BF16) for _ in range(2)]
            vfs = [K.sb(e2, [128, 512], F32) for _ in range(2)]
            Eb = [[K.sb(e2, [128, 512], BF16, "E") for _ in range(2)] for _ in range(NB)]
            Pb = [[K.sb(e2, [128, 512], BF16, "P") for _ in range(2)] for _ in range(NB)]
            rec8 = K.sb(e2, [128, 8], F32)
            att = K.sb(e2, [128, 512], BF16)
            Oacc = [pf[4], pf[5]]

            def qkv_tile(j, xsrc, full, slot, parts="ab"):
                par = j % 2
                xt, hT, cs, qk, ro, ta, tb, qkb, QT, vf = (xts[par], hTs[par], css[par], qks[par], ros[par], tas[par],
                                                           tbs[par], qkbs[par], QTs[par], vfs[par])
                if "a" in parts:
                    K.dma("sp", xt[:], xsrc, w=[xt])
                    K.dma("sp", cs[:], roped[j], w=[cs])
                    norm_T(wks_[par], xt, 0, hT, hT[:, :, :])
                if "b" not in parts:
                    return None, None
                for idx, c0 in ((0, 0), (1, 512)):
                    if idx == 0 and not full:
                        continue
                    ps = npf()
                    proj(ps, 512, hT, hT[:, :, :], wqkv, c0)
                    K.op("act", lambda e, ps=ps, idx=idx: e.copy(
                        qk[:, idx * 8:(idx + 1) * 8, :], ps[:, :].rearrange("p (h d) -> p h d", d=64)), r=[ps], w=[qk])
                h0 = 0 if full else 8
                nh = 16 - h0
                cosb = cs[:, 0:32].unsqueeze(1).to_broadcast([128, nh, 32])
                sinb = cs[:, 32:64].unsqueeze(1).to_broadcast([128, nh, 32])
                t1 = qk[:, h0:16, 0:32]
                t2 = qk[:, h0:16, 32:64]
                K.op("dve", lambda e: e.tensor_tensor(ta[:, h0:16, :], t1, cosb, ALU.mult), r=[qk, cs], w=[ta])
                K.op("dve", lambda e: e.tensor_tensor(tb[:, h0:16, :], t2, sinb, ALU.mult), r=[qk, cs], w=[tb])
                K.op("dve", lambda e: e.tensor_tensor(ro[:, h0:16, 0:32], ta[:, h0:16, :], tb[:, h0:16, :], ALU.subtract),
                     r=[ta, tb], w=[ro])
                K.op("dve", lambda e: e.tensor_tensor(ta[:, h0:16, :], t2, cosb, ALU.mult), r=[qk, cs], w=[ta])
                K.op("dve", lambda e: e.tensor_tensor(tb[:, h0:16, :], t1, sinb, ALU.mult), r=[qk, cs], w=[tb])
                K.op("dve", lambda e: e.tensor_tensor(ro[:, h0:16, 32:64], ta[:, h0:16, :], tb[:, h0:16, :], ALU.add),
                     r=[ta, tb], w=[ro])
                K.op("act", lambda e: e.copy(qkb[:, 8:16, :], ro[:, 8:16, :]), r=[ro], w=[qkb])
                transpose4(qkb, lambda jj: qkb[:, 8 + 2 * jj:10 + 2 * jj, :].rearrange("p h d -> p (h d)"),
                           KT[slot], KT[slot][:, :, :].rearrange("p c t -> p (c t)"))
                if full:
                    K.op("act", lambda e: e.mul(qkb[:, 0:8, :], ro[:, 0:8, :], 0.125), r=[ro], w=[qkb])
                    transpose4(qkb, lambda jj: qkb[:, 2 * jj:2 * jj + 2, :].rearrange("p h d -> p (h d)"),
                               QT, QT[:, :, :].rearrange("p c t -> p (c t)"))
                ps = npf()
                proj(ps, 512, hT, hT[:, :, :], wqkv, 1024)
                K.op("act", lambda e: e.copy(vf[:], ps[:, :]), r=[ps], w=[vf])
                K.op("dve", lambda e: e.tensor_copy(VX[slot][:, :, 0:64], vf[:].rearrange("p (h d) -> p h d", d=64)),
                     r=[vf], w=[VX[slot]])
                return ro, vf

            def unit_front(u, QT, KTt, nq, mask_ap_fn, bias_ap):
                for hg in range(2):
                    s = npf()
                    for hh in range(4):
                        K.op("pe", lambda e, hh=hh, hg=hg, s=s: e.matmul(
                            s[:, hh * nq:(hh + 1) * nq], KTt[hg * 64:hg * 64 + 64, hh, :],
                            QT[hg * 64:hg * 64 + 64, hh, 0:nq], start=True, stop=True), r=[KTt, QT], w=[s])
                    E = Eb[u % NB][hg]
                    if bias_ap is not None:
                        K.op("act", lambda e, E=E, s=s: e.activation(E[:, 0:4 * nq], s[:, 0:4 * nq], AF.Exp, bias=bias_ap),
                             r=[s, kbias], w=[E])
                    else:
                        K.op("act", lambda e, E=E, s=s: e.activation(E[:, 0:4 * nq], s[:, 0:4 * nq], AF.Exp), r=[s], w=[E])
                    P = Pb[u % NB][hg]
                    K.op("dve", lambda e, E=E, P=P: e.tensor_tensor(
                        P[:, 0:4 * nq].rearrange("p (h q) -> p h q", q=nq),
                        E[:, 0:4 * nq].rearrange("p (h q) -> p h q", q=nq), mask_ap_fn(), ALU.mult),
                         r=[E, cm, ccs, ccn, ccc], w=[P])

            def unit_back(u, VXt, nq, first, last):
                for hg in range(2):
                    P = Pb[u % NB][hg]
                    for hh in range(4):
                        h = 2 * hh + hg
                        K.op("pe", lambda e, hh=hh, h=h, hg=hg, P=P: e.matmul(
                            Oacc[hg][0:nq, hh * 128:hh * 128 + 65], P[:, hh * nq:(hh + 1) * nq], VXt[:, h, 0:65],
                            start=(first and hh == 0), stop=(last and hh == 3)), r=[P, VXt], w=[Oacc[hg]])

            def run_units(units, QT, nq):
                n = len(units)
                for i in range(n + LAG):
                    if i < n:
                        prep, ktf, vxf, mfn, bias = units[i]
                        if prep is not None:
                            prep()
                        unit_front(i, QT, ktf(), nq, mfn, bias)
                    k = i - LAG
                    if k >= 0:
                        unit_back(k, units[k][2](), nq, k == 0, k == n - 1)

            def attn_finish(nq, col0):
                for hg in range(2):
                    o3 = Oacc[hg][:, 0:512].rearrange("p (h d) -> p h d", d=128)
                    K.op("dve", lambda e: e.reciprocal(rec8[:, hg * 4:hg * 4 + 4].unsqueeze(2), o3[:, :, 64:65]),
                         r=[Oacc[hg]], w=[rec8])
                    K.op("dve", lambda e: e.tensor_tensor(
                        att[:, :].rearrange("p (pr hf d) -> p pr hf d", hf=2, d=64)[:, :, hg, :], o3[:, :, 0:64],
                        rec8[:, hg * 4:hg * 4 + 4].unsqueeze(2).to_broadcast([128, 4, 64]), ALU.mult),
                         r=[Oacc[hg], rec8], w=[att])
                transpose4(att, lambda jj: att[:, jj * 128:(jj + 1) * 128], attT,
                           attT[:, :, col0:col0 + 128].rearrange("p c t -> p c t"))

            fast = bool(os.environ.get("KFAST"))
            jlist = list(range(32, NT)) if not fast else list(range(46, 52))
            jmin = jlist[0]

            def do_attn(j):
                jj = j - NPRE
                units = []
                for o in range(16, -1, -1):
                    kt = j - o
                    if kt < jmin:
                        continue
                    sl = kt % NR
                    units.append((None, lambda sl=sl: KT[sl], lambda sl=sl: VX[sl],
                                  lambda o=o: cm[:, o * 128:(o + 1) * 128].unsqueeze(1).to_broadcast([128, 4, 128]),
                                  kbias[:, kt:kt + 1]))
                run_units(units, QTs[j % 2], 128)
                attn_finish(128, jj * 128)

            prev_full = None
            qkv_tile(jlist[0], xcat[jlist[0] * 128:(jlist[0] + 1) * 128, :], jlist[0] >= NPRE, jlist[0] % NR, parts="a")
            for ji, j in enumerate(jlist):
                full = j >= NPRE
                if ji + 1 < len(jlist):
                    jn = jlist[ji + 1]
                    qkv_tile(jn, xcat[jn * 128:(jn + 1) * 128, :], jn >= NPRE, jn % NR, parts="a")
                ro, vf = qkv_tile(j, xcat[j * 128:(j + 1) * 128, :], full, j % NR, parts="b")
                if full:
                    late_dmas(1)
                if full:
                    jj = j - NPRE
                    K.dma("sp", wk_o[jj * 128:(jj + 1) * 128, :], ro[:, 8:16, :].rearrange("p h d -> p (h d)"), r=[ro])
                    K.dma("sp", wv_o[jj * 128:(jj + 1) * 128, :], vf[:], r=[vf])
                if prev_full is not None:
                    do_attn(prev_full)
                prev_full = j if full else None
            if prev_full is not None:
                do_attn(prev_full)

            if LEVEL >= 3:
                NBK = NB + 1
                KTs = [K.sb(e2, [128, 4, 128], BF16) for _ in range(NBK)]
                VXs = [K.sb(e2, [128, 8, 72], BF16) for _ in range(NBK)]
                for v_ in VXs:
                    K.op("pool", lambda e, v_=v_: e.memset(v_[:], 1.0), w=[v_])
                ckf = [K.sb(e2, [128, 512], F32, "ckf") for _ in range(3)]
                cvf = [K.sb(e2, [128, 512], F32, "cvf") for _ in range(3)]
                ckb = [K.sb(e2, [128, 512], BF16) for _ in range(2)]
                sslot = 0
                ro, vf = qkv_tile(ST, xsin, True, sslot)
                for bl in range(4):
                    K.dma("sp", wks_o[bl, 2044:2048, :], ro[4 * bl:4 * bl + 4, 8:16, :].rearrange("p h d -> p (h d)"), r=[ro])
                    K.dma("sp", wvs_o[bl, 2044:2048, :], vf[4 * bl:4 * bl + 4, :], r=[vf])
                units = []
                n_ = 0

                def finish_prep(n_, a, b_):
                    kb_ = ckb[n_ % 2]
                    kts_, vxs_ = KTs[n_ % NBK], VXs[n_ % NBK]
                    K.op("pool", lambda e: e.tensor_copy(kb_[:], a[:]), r=[a], w=[kb_])
                    transpose4(kb_, lambda jj: kb_[:, jj * 128:(jj + 1) * 128], kts_,
                               kts_[:, :, :].rearrange("p c t -> p (c t)"))
                    K.op("pool", lambda e: e.tensor_copy(vxs_[:, :, 0:64], b_[:].rearrange("p (h d) -> p h d", d=64)),
                         r=[b_], w=[vxs_])

                for bl in range(4):
                    for g in range(3):
                        def prep(bl=bl, g=g, n_=n_):
                            a, b_ = ckf[n_ % 3], cvf[n_ % 3]
                            for tq in range(4):
                                K.dma("sp", a[32 * tq:32 * tq + 32, :], ck[bl, 512 * g + tq:512 * (g + 1):16, :], w=[a])
                                K.dma("sp", b_[32 * tq:32 * tq + 32, :], cv[bl, 512 * g + tq:512 * (g + 1):16, :], w=[b_])
                            finish_prep(n_, a, b_)
                        units.append((prep, lambda n_=n_: KTs[n_ % NBK], lambda n_=n_: VXs[n_ % NBK],
                                      lambda bl=bl: ccc[:, bl * 16:(bl + 1) * 16].unsqueeze(1).to_broadcast([128, 4, 16]),
                                      None))
                        n_ += 1
                    for kt in range(12, 16):
                        def prep(bl=bl, kt=kt, n_=n_):
                            a, b_ = ckf[n_ % 3], cvf[n_ % 3]
                            K.dma("sp", a[:], ck[bl, kt * 128:(kt + 1) * 128, :], w=[a])
                            K.dma("sp", b_[:], cv[bl, kt * 128:(kt + 1) * 128, :], w=[b_])
                            finish_prep(n_, a, b_)
                        units.append((prep, lambda n_=n_: KTs[n_ % NBK], lambda n_=n_: VXs[n_ % NBK],
                                      lambda bl=bl, kt=kt: ccs[:, (bl * 16 + kt) * 16:(bl * 16 + kt + 1) * 16].unsqueeze(1).to_broadcast([128, 4, 16]),
                                      None))
                        n_ += 1
                units.append((None, lambda: KT[sslot], lambda: VX[sslot],
                              lambda: ccn[:, 0:16].unsqueeze(1).to_broadcast([128, 4, 16]), None))
                run_units(units, QTs[ST % 2], 16)
                attn_finish(16, NOWN * 128)
            K.barrier()

        with ExitStack() as e2:
            wk = dict(junk=K.sb(e2, [128, D], F32), ss=K.sb(e2, [128, 1], F32), rstd=K.sb(e2, [128, 1], F32),
                      xn=K.sb(e2, [128, D], BF16), kb=K.sb(e2, [128, 2, D], BF16))
            wmk = wload(e2, wb_mk, 8, D, name="wmk")
            wmv = wload(e2, wb_mv, 8, D, name="wmv")
            mT = K.sb(e2, [128, 8, 256], BF16)
            ktok = K.sb(e2, [128, 2, D], F32)
            vtok = K.sb(e2, [128, 2, D], F32)
            for mb in range(2):
                xt = K.sb(e2, [128, D], F32)
                K.dma("sp", xt[:], memd[mb * 128:(mb + 1) * 128, :], w=[xt])
                norm_T(wk, xt, 3, mT, mT[:, :, mb * 128:(mb + 1) * 128])
            for mb in range(2):
                for W, tok in ((wmk, ktok), (wmv, vtok)):
                    for cg in range(2):
                        ps = npf()
                        proj(ps, 512, mT, mT[:, :, mb * 128:(mb + 1) * 128], W, cg * 512)
                        K.op("act", lambda e, ps=ps, tok=tok, cg=cg, mb=mb: e.copy(tok[:, mb, cg * 512:(cg + 1) * 512], ps[:, :]),
                             r=[ps], w=[tok])
            K.dma("sp", memk_o.rearrange("(m p) n -> p m n", p=128), ktok[:], r=[ktok])
            K.dma("sp", memv_o.rearrange("(m p) n -> p m n", p=128), vtok[:], r=[vtok])
            mem_prepare(e2, wk, ktok, vtok, KTm_p, Vm_p)
            K.barrier()

        STf = K.sb(es, [128, 512], F32, "STf")
        with ExitStack() as e2:
          if LEVEL >= 4:
            wk = dict(junk=K.sb(e2, [128, D], F32), ss=K.sb(e2, [128, 1], F32), rstd=K.sb(e2, [128, 1], F32),
                      xn=K.sb(e2, [128, D], BF16), xn_eng="dve")
            tri = cload(c_tri, [128, 128], stk=e2)
            mneg = cload(c_mneg, [128, 128], stk=e2)
            sh = cload(c_sh, [128, 7 * 128], BF16, stk=e2)
            shs = cload(c_shs, [128, 4 * 128], BF16, stk=e2)
            rm = cload(c_rm, [128, 4], stk=e2)
            cw = K.sb(e2, [128, 4, D], F32)
            for i in range(4):
                K.dma("sp", cw[:, i, :], bc_rows(convw_d[i:i + 1, :], D), w=[cw])
            cbf = K.sb(e2, [1, D], F32, "cbf")
            K.dma("sp", cbf[:], convb_d, w=[cbf])
            cbrow = K.sb(e2, [1, D], BF16, "cbrow")
            K.op("dve", lambda e: e.tensor_copy(cbrow[:], cbf[:]), r=[cbf], w=[cbrow])
            gs = cload(bc_rows(gssm_d, 512), [128, 512], stk=e2)
            sm = K.sb(e2, [128, 3, 8], F32)
            for i in range(3):
                K.dma("sp", sm[:, i, :], bc_rows(small_d[i:i + 1, :], 8), w=[sm])
            Aneg = K.sb(e2, [128, 8], F32)
            K.op("act", lambda e: e.activation(Aneg[:], sm[:, 1, :], AF.Exp), r=[sm], w=[Aneg])
            K.op("dve", lambda e: e.tensor_scalar(Aneg[:], Aneg[:], -1.0, None, ALU.mult), r=[Aneg], w=[Aneg])
            wz = wload(e2, wb_in, 8, 1544, col0=1536, name="wz", res="in_b")
            xts = [K.sb(e2, [128, D], F32, "xt") for _ in range(2)]
            hTs_ = [K.sb(e2, [128, 8, 128], BF16) for _ in range(2)]
            xbcs_ = [K.sb(e2, [128, D], F32) for _ in range(3)]
            hT, xbc = hTs_[0], xbcs_[0]
            xw = [[K.sb(e2, [128, D], BF16, "xw") for _ in range(4)] for _ in range(2)]
            for t_ in xw[0] + xw[1]:
                K.op("pool", lambda e, t_=t_: e.memset(t_[:], 0.0), w=[t_])
            xcs_ = [K.sb(e2, [128, D], F32) for _ in range(3)]
            xc = xcs_[0]
            xcb = K.sb(e2, [128, 512], BF16)
            dtrs_ = [K.sb(e2, [128, 8], F32) for _ in range(5)]
            dtms_ = [K.sb(e2, [128, 8], F32) for _ in range(5)]
            dtr, dtm = dtrs_[0], dtms_[0]
            av = K.sb(e2, [128, 8], F32)
            acs = K.sb(e2, [128, 8], F32)
            nacs = K.sb(e2, [128, 8], F32)
            toe = K.sb(e2, [128, 8], F32)
            dec = K.sb(e2, [128, 8], F32)
            fs = K.sb(e2, [128, 8], F32)
            xdt = K.sb(e2, [128, 512], BF16)
            xw2 = K.sb(e2, [128, 512], BF16)
            STb = K.sb(e2, [128, 512], BF16)
            abc = K.sb(e2, [128, 8, 128], F32)
            T1 = K.sb(e2, [128, 8, 128], F32)
            Lm = K.sb(e2, [128, 8, 128], F32)
            scT = K.sb(e2, [128, 8, 128], BF16)
            BCT = K.sb(e2, [128, 4, 128], BF16)
            ysum = K.sb(e2, [128, 512], F32)
            ytmp = K.sb(e2, [128, 512], F32)
            yS = K.sb(e2, [128, 512], F32)
            zss_ = [K.sb(e2, [128, 512], F32) for _ in range(5)]
            zs = zss_[0]
            ss2 = K.sb(e2, [128, 2], F32)
            yb = K.sb(e2, [128, 512], BF16)
            X7 = K.sb(e2, [128, D], F32)
            stl = K.sb(e2, [128, 4, 128], F32)
            K.op("dve", lambda e: e.memset(STf[:], 0.0), w=[STf])
            K.op("dve", lambda e: e.memset(STb[:], 0.0), w=[STb])

            def conv_dt(j, cur, prev, shm, nsh_prev, src_x, full):
                full = full or j == NPRE - 1
                ncol = D if full else 768
                for i in range(4):
                    K.op("pool", lambda e, i=i: e.tensor_tensor(cur[i][:, 0:ncol], src_x[:, 0:ncol], cw[:, i, 0:ncol], ALU.mult),
                         r=[src_x, cw], w=[cur[i]])
                for cg in range(2):
                    ps = npf()
                    nw = 512 if (full or cg == 0) else 256
                    for i in range(4):
                        K.op("pe", lambda e, i=i, cg=cg, ps=ps, nw=nw: e.matmul(
                            ps[:, 0:nw], shm[:, i * 128:(i + 1) * 128], cur[i][:, cg * 512:cg * 512 + nw],
                            start=(i == 0), stop=False), r=[shm, cur[i]], w=[ps])
                    for i in range(nsh_prev):
                        K.op("pe", lambda e, i=i, cg=cg, ps=ps, nw=nw: e.matmul(
                            ps[:, 0:nw], shm[:, (4 + i) * 128:(5 + i) * 128], prev[i][:, cg * 512:cg * 512 + nw],
                            start=False, stop=False), r=[shm, prev[i]], w=[ps])
                    K.op("pe", lambda e, cg=cg, ps=ps, nw=nw: e.matmul(
                        ps[:, 0:nw], onesb[0:1, :], cbrow[0:1, cg * 512:cg * 512 + nw], start=False, stop=True),
                         r=[onesb, cbrow], w=[ps])
                    K.op("act", lambda e, cg=cg, ps=ps, nw=nw: e.activation(xc[:, cg * 512:cg * 512 + nw], ps[:, 0:nw], AF.Silu),
                         r=[ps], w=[xc])

            def ssd_chunk(full, dtm_ap_t, ST_f, ST_b, parts="abc"):
                nbc = 512 if full else 256
                if "a" in parts:
                    K.op("dve", lambda e: e.tensor_tensor(av[:], dtm_ap_t[:], Aneg[:], ALU.mult), r=[dtm_ap_t, Aneg], w=[av])
                    p1 = npf()
                    K.op("pe", lambda e: e.matmul(p1[:, 0:8], onesf[:], av[:], start=True, stop=True), r=[onesf, av], w=[p1])
                    K.op("pe", lambda e: e.matmul(p1[:, 8:16], tri[:], av[:], start=True, stop=True), r=[tri, av], w=[p1])
                    K.op("dve", lambda e: e.tensor_copy(acs[:], p1[:, 8:16]), r=[p1], w=[acs])
                    K.op("dve", lambda e: e.tensor_tensor(toe[:], p1[:, 0:8], acs[:], ALU.subtract), r=[p1, acs], w=[toe])
                    K.op("act", lambda e: e.activation(toe[:], toe[:], AF.Exp), r=[toe], w=[toe])
                    K.op("act", lambda e: e.activation(dec[:], p1[:, 0:8], AF.Exp), r=[p1], w=[dec])
                    xs3 = xc[:, 0:512].rearrange("p (h d) -> p h d", d=64)
                    K.op("dve", lambda e: e.tensor_tensor(xdt[:].rearrange("p (h d) -> p h d", d=64), xs3,
                                                          dtm_ap_t[:].unsqueeze(2).to_broadcast([128, 8, 64]), ALU.mult),
                         r=[xc, dtm_ap_t], w=[xdt])
                    K.op("dve", lambda e: e.tensor_tensor(xw2[:].rearrange("p (h d) -> p h d", d=64),
                                                          xdt[:].rearrange("p (h d) -> p h d", d=64),
                                                          toe[:].unsqueeze(2).to_broadcast([128, 8, 64]), ALU.mult),
                         r=[xdt, toe], w=[xw2])
                    nbc = 512 if full else 256
                    K.op("dve", lambda e: e.tensor_copy(xcb[:, 0:nbc], xc[:, 512:512 + nbc]), r=[xc], w=[xcb])
                if full and "b" in parts:
                    transpose4(xcb, lambda jj: xcb[:, jj * 128:(jj + 1) * 128], BCT, BCT[:, :, :].rearrange("p c t -> p (c t)"))
                    K.op("dve", lambda e: e.tensor_scalar(nacs[:], acs[:], -1.0, None, ALU.mult), r=[acs], w=[nacs])
                    K.op("act", lambda e: e.activation(fs[:], acs[:], AF.Exp), r=[acs], w=[fs])
                    K.op("pool", lambda e: e.tensor_tensor(abc[:], av[:].unsqueeze(2).to_broadcast([128, 8, 128]),
                                                          tri[:].unsqueeze(1).to_broadcast([128, 8, 128]), ALU.mult),
                         r=[av, tri], w=[abc])
                    g_ = npf()
                    for g in range(2):
                        K.op("pe", lambda e, g=g: e.matmul(g_[:, g * 128:(g + 1) * 128], BCT[:, g, :], BCT[:, 2 + g, :],
                                                           start=True, stop=True), r=[BCT], w=[g_])
                    for half in range(2):
                        R = npf()
                        K.op("pe", lambda e, half=half, R=R: e.matmul(
                            R[:, :], onesf[:], abc[:, half * 4:half * 4 + 4, :].rearrange("p h l -> p (h l)"),
                            start=True, stop=True), r=[onesf, abc], w=[R])
                        K.op("dve", lambda e, half=half, R=R: e.tensor_tensor(
                            T1[:, half * 4:half * 4 + 4, :], R[:, :].rearrange("p (h l) -> p h l", l=128),
                            mneg[:].unsqueeze(1).to_broadcast([128, 4, 128]), ALU.add), r=[R, mneg], w=[T1])
                    for h in range(8):
                        K.op("act", lambda e, h=h: e.activation(Lm[:, h, :], T1[:, h, :], AF.Exp, bias=nacs[:, h:h + 1]),
                             r=[T1, nacs], w=[Lm])
                    for g in range(2):
                        K.op("dve", lambda e, g=g: e.tensor_tensor(
                            scT[:, g * 4:g * 4 + 4, :], Lm[:, g * 4:g * 4 + 4, :],
                            g_[:, g * 128:(g + 1) * 128].unsqueeze(1).to_broadcast([128, 4, 128]), ALU.mult),
                             r=[Lm, g_], w=[scT])
                    yd = npf()
                    for h in range(8):
                        K.op("pe", lambda e, h=h: e.matmul(yd[:, h * 64:(h + 1) * 64], scT[:, h, :],
                                                           xdt[:, h * 64:(h + 1) * 64], start=True, stop=True),
                             r=[scT, xdt], w=[yd])
                    K.op("act", lambda e: e.copy(ysum[:], yd[:, :]), r=[yd], w=[ysum])
                if full and "c" in parts:
                    yo = npf()
                    for h in range(8):
                        K.op("pe", lambda e, h=h: e.matmul(yo[:, h * 64:(h + 1) * 64], BCT[:, 2 + h // 4, :],
                                                           ST_b[:, h * 64:(h + 1) * 64], start=True, stop=True),
                             r=[BCT, ST_b], w=[yo])
                    K.op("dve", lambda e: e.tensor_tensor(ytmp[:].rearrange("p (h d) -> p h d", d=64),
                                                          yo[:, :].rearrange("p (h d) -> p h d", d=64),
                                                          fs[:].unsqueeze(2).to_broadcast([128, 8, 64]), ALU.mult),
                         r=[yo, fs], w=[ytmp])
                    K.op("dve", lambda e: e.tensor_tensor(ysum[:], ysum[:], ytmp[:], ALU.add), r=[ysum, ytmp], w=[ysum])
                if "c" in parts:
                    cs_ = npf()
                    for g in range(2):
                        K.op("pe", lambda e, g=g: e.matmul(cs_[:, g * 256:(g + 1) * 256], xcb[:, g * 128:(g + 1) * 128],
                                                           xw2[:, g * 256:(g + 1) * 256], start=True, stop=True),
                             r=[xcb, xw2], w=[cs_])
                    K.op("dve", lambda e: e.tensor_tensor(ST_f[:].rearrange("p (h d) -> p h d", d=64),
                                                          ST_f[:].rearrange("p (h d) -> p h d", d=64),
                                                          dec[:].unsqueeze(2).to_broadcast([128, 8, 64]), ALU.mult),
                         r=[ST_f, dec], w=[ST_f])
                    K.op("dve", lambda e: e.tensor_tensor(ST_f[:], ST_f[:], cs_[:, :], ALU.add), r=[ST_f, cs_], w=[ST_f])
                    K.op("dve", lambda e: e.tensor_copy(ST_b[:], ST_f[:]), r=[ST_f], w=[ST_b])

            def ssd_post(ysrc, zps, col0):
                K.op("dve", lambda e: e.tensor_tensor(ytmp[:].rearrange("p (h d) -> p h d", d=64),
                                                      xc[:, 0:512].rearrange("p (h d) -> p h d", d=64),
                                                      sm[:, 2, :].unsqueeze(2).to_broadcast([128, 8, 64]), ALU.mult),
                     r=[xc, sm], w=[ytmp])
                K.op("dve", lambda e: e.tensor_tensor(ytmp[:], ytmp[:], ysrc[:], ALU.add), r=[ytmp, ysrc], w=[ytmp])
                K.op("dve", lambda e: e.tensor_tensor(ytmp[:], ytmp[:], zs[:], ALU.mult), r=[ytmp, zs], w=[ytmp])
                for g in range(2):
                    K.op("act", lambda e, g=g: e.activation(zs[:, g * 256:(g + 1) * 256], ytmp[:, g * 256:(g + 1) * 256], AF.Square,
                                                            accum_out=ss2[:, g:g + 1]), r=[ytmp], w=[zs, ss2])
                K.op("act", lambda e: e.activation(ss2[:], ss2[:], AF.Ln, scale=1.0 / 256, bias=epsT[:, 0:1]), r=[ss2, epsT], w=[ss2])
                K.op("act", lambda e: e.activation(ss2[:], ss2[:], AF.Exp, scale=-0.5), r=[ss2], w=[ss2])
                K.op("dve", lambda e: e.tensor_tensor(ytmp[:].rearrange("p (g d) -> p g d", d=256),
                                                      ytmp[:].rearrange("p (g d) -> p g d", d=256),
                                                      ss2[:].unsqueeze(2).to_broadcast([128, 2, 256]), ALU.mult),
                     r=[ytmp, ss2], w=[ytmp])
                K.op("dve", lambda e: e.tensor_tensor(yb[:], ytmp[:], gs[:], ALU.mult), r=[ytmp, gs], w=[yb])
                transpose4(yb, lambda jj: yb[:, jj * 128:(jj + 1) * 128], yT, yT[:, :, col0:col0 + 128].rearrange("p c t -> p c t"))

            def dt_chain(ps8, vcol):
                K.op("dve", lambda e: e.tensor_tensor(dtr[:], ps8, sm[:, 0, :], ALU.add), r=[sm], w=[dtr])
                K.op("act", lambda e: e.activation(dtr[:], dtr[:], AF.Exp), r=[dtr], w=[dtr])
                K.op("act", lambda e: e.activation(dtr[:], dtr[:], AF.Ln, bias=1.0), r=[dtr], w=[dtr])
                K.op("dve", lambda e: e.tensor_scalar(dtm[:], dtr[:], vcol, None, ALU.mult), r=[dtr, valid, rm], w=[dtm])

            def state_out(ST_f, dst):
                p = npf()
                for c in range(4):
                    K.op("pe", lambda e, c=c: e.transpose(p[:, c * 128:(c + 1) * 128], ST_f[:, c * 128:(c + 1) * 128], identf[:]),
                         r=[ST_f, identf], w=[p])
                K.op("act", lambda e: e.copy(stl[:].rearrange("p c n -> p (c n)"), p[:, :]), r=[p], w=[stl])
                K.dma("sp", dst.rearrange("(c p) n -> p c n", p=128), stl[:], r=[stl])

            def front(j, xsrc, full, parts="ab"):
                if "a" in parts:
                    xt = xts[j % 2]
                    K.dma("sp", xt[:], xsrc, w=[xt])
                    norm_T(wk, xt, 0, hT, hT[:, :, :])
                if "b" not in parts:
                    return None, None
                for cg in range(2):
                    ps = npf()
                    nw = 512 if (full or j == NPRE - 1 or cg == 0) else 256
                    proj(ps, nw, hT, hT[:, :, :], wz, 512 + cg * 512)
                    K.op("dve", lambda e, ps=ps, cg=cg, nw=nw: e.tensor_copy(xbc[:, cg * 512:cg * 512 + nw], ps[:, 0:nw]), r=[ps], w=[xbc])
                pd = npf()
                proj(pd, 8, hT, hT[:, :, :], wz, 1536)
                pz = None
                if full:
                    pz = npf()
                    proj(pz, 512, hT, hT[:, :, :], wz, 0)
                    K.op("act", lambda e: e.activation(zs[:], pz[:, :], AF.Silu), r=[pz], w=[zs])
                return pd, pz

            def bind(j):
                nonlocal hT, xbc, xc, dtr, dtm, zs
                hT, xbc, xc = hTs_[j % 2], xbcs_[j % 3], xcs_[j % 3]
                dtr, dtm, zs = dtrs_[j % 5], dtms_[j % 5], zss_[j % 5]

            def stage_F1(j):
                bind(j)
                front(j, xcat[j * 128:(j + 1) * 128, :], j >= NPRE, parts="a")

            def stage_F2(j):
                bind(j)
                full = j >= NPRE
                pd, pz = front(j, xcat[j * 128:(j + 1) * 128, :], full, parts="b")
                pd_ap = pd[:, 0:8]
                K.op("dve", lambda e: e.tensor_tensor(dtr[:], pd_ap, sm[:, 0, :], ALU.add), r=[pd, sm], w=[dtr])
                K.op("act", lambda e: e.activation(dtr[:], dtr[:], AF.Exp), r=[dtr], w=[dtr])
                K.op("act", lambda e: e.activation(dtr[:], dtr[:], AF.Ln, bias=1.0), r=[dtr], w=[dtr])
                K.op("dve", lambda e: e.tensor_scalar(dtm[:], dtr[:], valid[:, j:j + 1], None, ALU.mult), r=[dtr, valid], w=[dtm])

            def stage_C(j):
                bind(j)
                conv_dt(j, xw[j % 2], xw[(j + 1) % 2], sh, 3, xbc, j >= NPRE)
                if j == NT - 1:
                    K.dma("sp", convp_o, xbc[125:128, :], r=[xbc])

            def stage_S(j, part):
                bind(j)
                full = j >= NPRE
                ssd_chunk(full, dtm, STf, STb, parts=part)
                if full and part == "c":
                    ssd_post(ysum, None, (j - NPRE) * 128)

            jl = list(range(NT)) if not os.environ.get("KFAST") else list(range(46, 52))
            nj = len(jl)
            stage_F1(jl[0])
            LC, LS = 1, 3
            for it in range(nj + LS):
                if it % 4 == 0:
                    late_dmas(1)
                if it + 1 < nj:
                    stage_F1(jl[it + 1])
                if it >= LS:
                    stage_S(jl[it - LS], "a")
                if LC <= it < nj + LC:
                    stage_C(jl[it - LC])
                if it >= LS:
                    stage_S(jl[it - LS], "b")
                if it < nj:
                    stage_F2(jl[it])
                if it >= LS:
                    stage_S(jl[it - LS], "c")
            late_dmas()
            bind(0)
            state_out(STf, ssmp_o)

            if LEVEL >= 5:
                pd, pz = front(ST, xsin, True)
                pd_ap = pd[:, 0:8]
                K.op("dve", lambda e: e.tensor_tensor(dtr[:], pd_ap, sm[:, 0, :], ALU.add), r=[pd, sm], w=[dtr])
                K.op("act", lambda e: e.activation(dtr[:], dtr[:], AF.Exp), r=[dtr], w=[dtr])
                K.op("act", lambda e: e.activation(dtr[:], dtr[:], AF.Ln, bias=1.0), r=[dtr], w=[dtr])
                K.op("dve", lambda e: e.memset(X7[:], 0.0), w=[X7])
                for bl in range(4):
                    K.dma("sp", X7[7 * bl:7 * bl + 3, :], sconv[bl], w=[X7])
                    K.dma("sp", X7[7 * bl + 3:7 * bl + 7, :], xbc[4 * bl:4 * bl + 4, :], r=[xbc], w=[X7])
                for bl in range(4):
                    K.dma("sp", convs_o[bl], X7[7 * bl + 4:7 * bl + 7, :], r=[X7])
                conv_dt(ST, xw[0], xw[1], shs, 0, X7, True)
                K.op("dve", lambda e: e.memset(yS[:], 0.0), w=[yS])
                STs = K.sb(e2, [128, 512], F32)
                STsb = K.sb(e2, [128, 512], BF16)
                for bl in range(4):
                    K.dma("sp", stl[:], sssm[bl].rearrange("(c p) n -> p c n", p=128), w=[stl])
                    p = npf()
                    for c in range(4):
                        K.op("pe", lambda e, c=c: e.transpose(p[:, c * 128:(c + 1) * 128], stl[:, c, :], identf[:]),
                             r=[stl, identf], w=[p])
                    K.op("act", lambda e: e.copy(STs[:], p[:, :]), r=[p], w=[STs])
                    K.op("dve", lambda e: e.tensor_copy(STsb[:], STs[:]), r=[STs], w=[STsb])
                    K.op("dve", lambda e, bl=bl: e.tensor_scalar(dtm[:], dtr[:], rm[:, bl:bl + 1], None, ALU.mult),
                         r=[dtr, rm], w=[dtm])
                    ssd_chunk(True, dtm, STs, STsb)
                    K.op("dve", lambda e, bl=bl: e.scalar_tensor_tensor(yS[:], ysum[:], rm[:, bl:bl + 1], yS[:], ALU.mult, ALU.add),
                         r=[ysum, rm, yS], w=[yS])
                    state_out(STs, ssms_o[bl])
                ssd_post(yS, None, NOWN * 128)
            K.barrier()

        gfin = cload(bc_rows(gfin_d, D), [128, D])
        KTm_s = K.sb(es, [128, 8, 256], BF16, "KTm_s")
        Vm_s = K.sb(es, [128, 2, D], BF16, "Vm_s")
        wo = wload(es, wb_out, 8, D, name="wo")
        wq = wload(es, wb_xq, 8, D, name="wq")
        wxo = wload(es, wb_xo, 8, D, name="wxo")
        wup = [K.sb(es, [128, 8, 512], BF16, "wup") for _ in range(2)]
        wdn = [K.sb(es, [128, 4, D], BF16, "wdn") for _ in range(2)]
        upv = wb_up.rearrange("(kc p) n -> p kc n", p=128)
        dnv = wb_down.rearrange("(fc p) n -> p fc n", p=128)

        def load_up(ch):
            W = wup[ch % 2]
            K.dma("sp", W[:, 0:4, :], upv[:, 0:4, ch * 512:(ch + 1) * 512], r=[wres[wb_up.name]], w=[W])
            K.dma("sp", W[:, 4:8, :], upv[:, 4:8, ch * 512:(ch + 1) * 512], r=[wres[wb_up.name]], w=[W])

        def load_dn(ch):
            W = wdn[ch % 2]
            K.dma("sp", W[:, :, :], dnv[:, ch * 4:ch * 4 + 4, :], r=[wres[wb_down.name]], w=[W])

        groups = [(g * 4, 4) for g in range(4)] + [(NOWN, 1)]
        if LEVEL < 6:
            groups = []
        elif LEVEL == 6:
            groups = groups[:1]
        elif LEVEL == 7:
            groups = [groups[0], groups[4]]
        for gi_, (t0, ntl) in enumerate(groups):
            n = ntl * 128
            is_s = ntl == 1
            with ExitStack() as eg:
                resid = [K.sb(eg, [128, D], F32, "resid") for _ in range(ntl)]
                hmT = K.sb(eg, [128, 8, n], BF16, "hmT")
                with ExitStack() as e2:
                    wk = dict(junk=K.sb(e2, [128, D], F32), ss=K.sb(e2, [128, 1], F32), rstd=K.sb(e2, [128, 1], F32),
                              xn=K.sb(e2, [128, D], BF16), kb=(K.sb(e2, [128, 2, D], BF16) if is_s else None),
                              xP=[K.sb(e2, [128, 512], BF16, "xP") for _ in range(4)],
                              xrec=[K.sb(e2, [128, 512], F32, "xrec") for _ in range(2)])
                    load_up(0)
                    load_up(1)
                    load_dn(0)
                    load_dn(1)
                    hxT = K.sb(e2, [128, 8, n], BF16)
                    qT = K.sb(e2, [128, 8, n], BF16)
                    OT = K.sb(e2, [128, 8, n], BF16)
                    for tl in range(ntl):
                        src = xsin if is_s else xcat[(NPRE + t0 + tl) * 128:(NPRE + t0 + tl + 1) * 128, :]
                        K.dma("sp", resid[tl][:], src, w=[resid[tl]])
                    for tl in range(ntl):
                        c0 = (t0 + tl) * 128
                        for cg in range(2):
                            ps = npf()
                            for kc in range(8):
                                srcT = attT if kc < 4 else yT
                                K.op("pe", lambda e, kc=kc, cg=cg, ps=ps, srcT=srcT: e.matmul(
                                    ps[:, :], srcT[:, kc % 4, c0:c0 + 128], wo[:, kc, cg * 512:(cg + 1) * 512],
                                    start=(kc == 0), stop=(kc == 7)), r=[srcT, wo], w=[ps])
                            K.op("dve", lambda e, cg=cg, ps=ps, tl=tl: e.tensor_tensor(
                                resid[tl][:, cg * 512:(cg + 1) * 512], ps[:, :], resid[tl][:, cg * 512:(cg + 1) * 512], ALU.add),
                                 r=[ps, resid[tl]], w=[resid[tl]])
                    for tl in range(ntl):
                        norm_T(wk, resid[tl], 1, hxT, hxT[:, :, tl * 128:(tl + 1) * 128])
                    for fc in range(8):
                        ps = npf()
                        for kc in range(8):
                            K.op("pe", lambda e, kc=kc, fc=fc, ps=ps: e.matmul(
                                ps[:, 0:n], wq[:, kc, fc * 128:(fc + 1) * 128], hxT[:, kc, :],
                                start=(kc == 0), stop=(kc == 7)), r=[wq, hxT], w=[ps])
                        K.op("act", lambda e, fc=fc, ps=ps: e.mul(qT[:, fc, :], ps[:, 0:n], 1.0 / 16), r=[ps], w=[qT])
                    if not is_s:
                        xattn(wk, qT, qT[:, :, :], n, KTm_p, Vm_p, OT, OT[:, :, :])
                    else:
                        K.op("dve", lambda e: e.memset(OT[:], 0.0), w=[OT])
                        ktoks = [K.sb(e2, [128, 2, D], F32) for _ in range(2)]
                        vtoks = [K.sb(e2, [128, 2, D], F32) for _ in range(2)]
                        KTm2 = [KTm_s, KTm_s]
                        Vm2 = [Vm_s, Vm_s]

                        def mload(bl):
                            K.dma("sp", ktoks[bl % 2][:], cmk[bl].rearrange("(m p) n -> p m n", p=128), w=[ktoks[bl % 2]])
                            K.dma("sp", vtoks[bl % 2][:], cmv[bl].rearrange("(m p) n -> p m n", p=128), w=[vtoks[bl % 2]])

                        mload(0)
                        mload(1)
                        for bl in range(4):
                            mem_prepare(e2, wk, ktoks[bl % 2], vtoks[bl % 2], KTm2[bl % 2], Vm2[bl % 2])
                            if bl + 2 < 4:
                                mload(bl + 2)
                            xattn(wk, qT, qT[:, :, 4 * bl:4 * bl + 4], 4, KTm2[bl % 2], Vm2[bl % 2], OT, OT[:, :, 4 * bl:4 * bl + 4])
                    for tl in range(ntl):
                        for cg in range(2):
                            ps = npf()
                            for kc in range(8):
                                K.op("pe", lambda e, kc=kc, cg=cg, ps=ps, tl=tl: e.matmul(
                                    ps[:, :], OT[:, kc, tl * 128:(tl + 1) * 128], wxo[:, kc, cg * 512:(cg + 1) * 512],
                                    start=(kc == 0), stop=(kc == 7)), r=[OT, wxo], w=[ps])
                            K.op("dve", lambda e, cg=cg, ps=ps, tl=tl: e.tensor_tensor(
                                resid[tl][:, cg * 512:(cg + 1) * 512], ps[:, :], resid[tl][:, cg * 512:(cg + 1) * 512], ALU.add),
                                 r=[ps, resid[tl]], w=[resid[tl]])
                        norm_T(wk, resid[tl], 2, hmT, hmT[:, :, tl * 128:(tl + 1) * 128])
                    K.barrier()
                with ExitStack() as e2:
                    wk = dict(junk=K.sb(e2, [128, D], F32), ss=K.sb(e2, [128, 1], F32), rstd=K.sb(e2, [128, 1], F32),
                              xn=K.sb(e2, [128, D], BF16))
                    hidT = K.sb(e2, [128, 32, n], BF16)
                    yt = wk["junk"]
                    rls = [K.sb(e2, [128, 512], F32, "rl") for _ in range(2)]
                    for ch in range(8):
                        W = wup[ch % 2]
                        for f4 in range(4):
                            fc = ch * 4 + f4
                            ps = npf()
                            for kc in range(8):
                                K.op("pe", lambda e, kc=kc, f4=f4, ps=ps, W=W: e.matmul(
                                    ps[:, 0:n], W[:, kc, f4 * 128:(f4 + 1) * 128], hmT[:, kc, :],
                                    start=(kc == 0), stop=(kc == 7)), r=[W, hmT], w=[ps])
                            rl = rls[fc % 2]
                            K.op("act", lambda e, ps=ps, rl=rl: e.activation(rl[:, 0:n], ps[:, 0:n], AF.Relu), r=[ps], w=[rl])
                            if fc % 2:
                                K.op("act", lambda e, fc=fc, rl=rl: e.activation(hidT[:, fc, :], rl[:, 0:n], AF.Square), r=[rl], w=[hidT])
                            else:
                                K.op("dve", lambda e, fc=fc, rl=rl: e.tensor_tensor(
                                    hidT[:, fc, :], rl[:, 0:n], rl[:, 0:n], ALU.mult), r=[rl], w=[hidT])
                        if ch + 2 < 8:
                            load_up(ch + 2)
                    for ch in range(8):
                        W = wdn[ch % 2]
                        for tl in range(ntl):
                            for cg in range(2):
                                ps = npf()
                                for f4 in range(4):
                                    K.op("pe", lambda e, f4=f4, cg=cg, ps=ps, tl=tl, W=W, ch=ch: e.matmul(
                                        ps[:, :], hidT[:, ch * 4 + f4, tl * 128:(tl + 1) * 128], W[:, f4, cg * 512:(cg + 1) * 512],
                                        start=(f4 == 0), stop=(f4 == 3)), r=[hidT, W], w=[ps])
                                K.op("dve", lambda e, cg=cg, ps=ps, tl=tl: e.tensor_tensor(
                                    resid[tl][:, cg * 512:(cg + 1) * 512], ps[:, :], resid[tl][:, cg * 512:(cg + 1) * 512], ALU.add),
                                     r=[ps, resid[tl]], w=[resid[tl]])
                        if ch + 2 < 8:
                            load_dn(ch + 2)
                    junk, ss, rstd = wk["junk"], wk["ss"], wk["rstd"]
                    for tl in range(ntl):
                        x = resid[tl]
                        K.op("act", lambda e, x=x: e.activation(junk[:], x[:], AF.Square, accum_out=ss[:, 0:1]), r=[x], w=[junk, ss])
                        K.op("act", lambda e: e.activation(rstd[:], ss[:], AF.Ln, scale=1.0 / D, bias=epsT[:, 0:1]), r=[ss, epsT], w=[rstd])
                        K.op("act", lambda e: e.activation(rstd[:], rstd[:], AF.Exp, scale=-0.5), r=[rstd], w=[rstd])
                        K.op("dve", lambda e, x=x: e.scalar_tensor_tensor(yt[:], x[:], rstd[:, 0:1], gfin[:], ALU.mult, ALU.mult),
                             r=[x, rstd, gfin], w=[yt])
                        dst = ys_o if is_s else y_o[(t0 + tl) * 128:(t0 + tl + 1) * 128, :]
                        K.dma("sp", dst, yt[:], r=[yt])
                    K.barrier()
        K.barrier(full=True)
    return nc


def _mult(delta):
    d = delta
    c = ((d >= 0) & (d <= 128)).astype(np.float32)
    c += ((d >= 0) & (d % 4 == 0) & (d <= 512)).astype(np.float32)
    c += ((d >= 0) & (d % 16 == 0) & (d <= 2048)).astype(np.float32)
    return c


def _consts():
    i = np.arange(128)
    c = {}
    c["c_ident"] = np.eye(128, dtype=np.float32)
    c["c_tri"] = (i[:, None] <= i[None, :]).astype(np.float32)
    c["c_mneg"] = np.where(i[:, None] <= i[None, :], 0.0, NEG).astype(np.float32)
    cm = np.zeros((128, 17 * 128), np.float32)
    for o in range(17):
        cm[:, o * 128:(o + 1) * 128] = _mult(128 * o + i[None, :] - i[:, None])
    c["c_cm"] = cm
    sh = np.zeros((128, 7 * 128), np.float32)
    for t in range(4):
        m = (i[:, None] == i[None, :] - 3 + t)
        sh[:, t * 128:(t + 1) * 128] = m
    for t in range(3):
        m = (i[:, None] - 128 == i[None, :] - 3 + t)
        sh[:, (4 + t) * 128:(5 + t) * 128] = m
    c["c_sh"] = sh
    shs = np.zeros((128, 4 * 128), np.float32)
    for t in range(4):
        for bl in range(4):
            for tt in range(4):
                shs[7 * bl + tt + t, t * 128 + 4 * bl + tt] = 1.0
    c["c_shs"] = shs
    cs = np.zeros((128, 64 * 16), np.float32)
    for bl in range(4):
        for kt in range(16):
            blk = np.zeros((128, 16), np.float32)
            for t in range(4):
                blk[:, 4 * bl + t] = _mult(2048 + t - 128 * kt - i)
            cs[:, (bl * 16 + kt) * 16:(bl * 16 + kt + 1) * 16] = blk
    c["c_cs"] = cs
    cn = np.zeros((128, 16), np.float32)
    for bl in range(4):
        for tk in range(4):
            for tq in range(4):
                if tq >= tk:
                    cn[4 * bl + tk, 4 * bl + tq] = _mult(np.array(tq - tk))
    c["c_cn"] = cn
    cc = np.zeros((128, 4 * 16), np.float32)
    for bl in range(4):
        for t in range(4):
            cc[32 * t:32 * t + 32, bl * 16 + 4 * bl + t] = 1.0
    c["c_cc"] = cc
    rm = np.zeros((128, 4), np.float32)
    for bl in range(4):
        rm[4 * bl:4 * bl + 4, bl] = 1.0
    c["c_rm"] = rm
    return c


_NC = None


def kernel(x_prompt, x_sample, cache_win_k, cache_win_v, state_conv, state_ssm,
           cache_mem_k, cache_mem_v, mem_prompt,
           g_mix, w_in, conv_w, conv_b, dt_bias, a_log, d_skip, g_ssm, w_out,
           g_xatt, g_mem, w_xq, w_mk, w_mv, w_xo, g_mlp, w_up, w_down, g_final):
    global _NC
    f = lambda a: np.ascontiguousarray(np.asarray(a, dtype=np.float32))
    x_prompt, x_sample = f(x_prompt), f(x_sample)
    consts = _consts()
    gc = np.concatenate([f(g)[0].reshape(8, 128).T for g in (g_mix, g_xatt, g_mlp, g_mem)], axis=1)
    shared = dict(w_in=f(w_in)[0], w_out=f(w_out)[0], w_xq=f(w_xq)[0], w_mk=f(w_mk)[0], w_mv=f(w_mv)[0],
                  w_xo=f(w_xo)[0], w_up=f(w_up)[0], w_down=f(w_down)[0], gcols=np.ascontiguousarray(gc),
                  gfin=f(g_final).reshape(1, D), convw=f(conv_w)[0], convb=f(conv_b)[0].reshape(1, D),
                  small=np.stack([f(dt_bias)[0], f(a_log)[0], f(d_skip)[0]]), gssm=f(g_ssm)[0].reshape(1, 512))
    shared.update(consts)
    half = 32
    inv = (10000.0 ** (-np.arange(half, dtype=np.float32) * 2.0 / 64)).astype(np.float32)
    in_maps = []
    for core in range(8):
        b, c = core // 4, core % 4
        start = 2048 * c
        p0 = start - NPRE * 128
        xc = np.zeros((NT * 128, D), np.float32)
        lo = max(p0, 0)
        xc[lo - p0:] = x_prompt[b, lo:start + 2048]
        pos = np.zeros((128, 65), np.float32)
        pos[:, :NT] = p0 + 128 * np.arange(NT)[None, :] + np.arange(128)[:, None]
        pos[:, 64] = -1.0
        pos[:16, 64] = 8192 + (np.arange(16) % 4)
        pp = np.maximum(pos, 0.0).T.astype(np.float32)
        ang = pp[:, :, None] * inv[None, None, :]
        rope = np.concatenate([np.cos(ang), np.sin(ang)], axis=-1).astype(np.float32)
        xs = np.zeros((128, D), np.float32)
        xs[:16] = x_sample[4 * core:4 * core + 4].reshape(16, D)
        m = dict(shared)
        m.update(xcat=xc, xs=xs, pos=pos, rope=np.ascontiguousarray(rope),
                 ck=f(cache_win_k)[0, 4 * core:4 * core + 4].reshape(4, 2048, 512),
                 cv=f(cache_win_v)[0, 4 * core:4 * core + 4].reshape(4, 2048, 512),
                 sconv=f(state_conv)[0, 4 * core:4 * core + 4],
                 sssm=f(state_ssm)[0, 4 * core:4 * core + 4].reshape(4, 512, 128),
                 cmk=f(cache_mem_k)[0, 4 * core:4 * core + 4].reshape(4, 256, D),
                 cmv=f(cache_mem_v)[0, 4 * core:4 * core + 4].reshape(4, 256, D),
                 mem=f(mem_prompt)[b])
        in_maps.append({k: np.ascontiguousarray(v) for k, v in m.items()})
    if _NC is None:
        _NC = build()
    import os
    ncores = int(os.environ.get("KCORES", "8"))
    if ncores < 8:
        c0 = int(os.environ.get("KCORE", "3"))
        res = run_bass_kernel_spmd(_NC, in_maps[c0:c0 + 1], core_ids=[0])
        return res.results[0]
    res = run_bass_kernel_spmd(_NC, in_maps, core_ids=list(range(8)))
    R = res.results
    y_prompt = np.stack([np.concatenate([R[4 * b + c]["y"] for c in range(4)], axis=0) for b in range(2)])
    y_sample = np.concatenate([R[k]["ys"][:16].reshape(4, 4, D) for k in range(8)], axis=0)
    wkp = np.stack([R[4 * b + 3]["wk"].reshape(2048, 8, 64) for b in range(2)])[None]
    wvp = np.stack([R[4 * b + 3]["wv"].reshape(2048, 8, 64) for b in range(2)])[None]
    cvp = np.stack([R[4 * b + 3]["convp"] for b in range(2)])[None]
    ssp = np.stack([R[4 * b + 3]["ssmp"].reshape(8, 64, 128) for b in range(2)])[None]
    mkp = np.stack([R[4 * b]["memk"].reshape(256, 4, 256) for b in range(2)])[None]
    mvp = np.stack([R[4 * b]["memv"].reshape(256, 4, 256) for b in range(2)])[None]
    wks = np.concatenate([R[k]["wks"].reshape(4, 2048, 8, 64) for k in range(8)], axis=0)[None]
    wvs = np.concatenate([R[k]["wvs"].reshape(4, 2048, 8, 64) for k in range(8)], axis=0)[None]
    cvs = np.concatenate([R[k]["convs"] for k in range(8)], axis=0)[None]
    sss = np.concatenate([R[k]["ssms"].reshape(4, 8, 64, 128) for k in range(8)], axis=0)[None]
    return tuple(np.ascontiguousarray(a.astype(np.float32)) for a in
                 (y_prompt, y_sample, wkp, wvp, cvp, ssp, mkp, mvp, wks, wvs, cvs, sss))
```

```python
import numpy as np
from contextlib import ExitStack
import concourse.bass as bass
import concourse.mybir as mybir
from concourse.bass_utils import run_bass_kernel_spmd

F32 = mybir.dt.float32
BF16 = mybir.dt.bfloat16
ALU = mybir.AluOpType
AF = mybir.ActivationFunctionType
AX = mybir.AxisListType

D = 1024
NPRE = 48
NOWN = 16
NT = NPRE + NOWN
ST = 64
NEG = -30000.0
EPS = 1e-6
NDS = 56


class Res:
    __slots__ = ("w", "r")

    def __init__(self):
        self.w = None
        self.r = {}


class TT:
    def __init__(self, h, res=None):
        self.h = h
        self.res = res if res is not None else Res()

    def __getitem__(self, k):
        return self.h[k]


class KB:
    def __init__(self, nc, es):
        self.nc = nc
        self.es = es
        self.eng = {}
        for name, e in (("pe", nc.tensor), ("act", nc.scalar), ("dve", nc.vector),
                        ("pool", nc.gpsimd), ("sp", nc.sync)):
            sem = es.enter_context(nc.semaphore("s_" + name))
            self.eng[name] = dict(e=e, sem=sem, cnt=0, known={})
        self.ds = [dict(sem=es.enter_context(nc.semaphore("d%d" % i)), cnt=0) for i in range(NDS)]
        self.dn = {"sp": 0, "pool": 0}
        self.drange = {"sp": (0, 32), "pool": (32, NDS)}
        self.uid = 0

    def _sem(self, key):
        return self.eng[key]["sem"] if isinstance(key, str) else self.ds[key[1]]["sem"]

    def _wait(self, en, deps):
        E = self.eng[en]
        for key, val in deps:
            if val <= 0:
                continue
            if key == en and en == "pe":
                continue
            if E["known"].get(key, 0) >= val:
                continue
            E["e"].wait_ge(self._sem(key), val)
            E["known"][key] = val

    @staticmethod
    def _deps(r, w):
        deps = []
        for t in r:
            if t.res.w is not None:
                deps.append(t.res.w)
        for t in w:
            if t.res.w is not None:
                deps.append(t.res.w)
            deps.extend(t.res.r.items())
        return deps

    @staticmethod
    def _upd(tok, r, w):
        for t in r:
            t.res.r[tok[0]] = tok[1]
        for t in w:
            t.res.w = tok
            t.res.r = {}

    def op(self, en, fn, r=(), w=()):
        E = self.eng[en]
        self._wait(en, self._deps(r, w))
        ins = fn(E["e"])
        E["cnt"] += 1
        ins.then_inc(E["sem"], 1)
        self._upd((en, E["cnt"]), r, w)

    def dma(self, q, out, in_, r=(), w=()):
        E = self.eng[q]
        lo, hi = self.drange[q]
        i = lo + self.dn[q]
        self.dn[q] = (self.dn[q] + 1) % (hi - lo)
        d = self.ds[i]
        deps = self._deps(r, w)
        deps.append((("d", i), d["cnt"]))
        self._wait(q, deps)
        ins = E["e"].dma_start(out=out, in_=in_)
        d["cnt"] += 16
        ins.then_inc(d["sem"], 16)
        self._upd((("d", i), d["cnt"]), r, w)

    def barrier(self, full=False):
        allv = [(n, e["cnt"]) for n, e in self.eng.items() if n != "sp"]
        allv += [(("d", i), d["cnt"]) for i, d in enumerate(self.ds) if full or i < 32]
        for en in self.eng:
            E = self.eng[en]
            for key, val in allv:
                if val <= 0 or key == en:
                    continue
                if E["known"].get(key, 0) >= val:
                    continue
                E["e"].wait_ge(self._sem(key), val)
                E["known"][key] = val

    def sb(self, es, shape, dt, name=None):
        self.uid += 1
        return TT(es.enter_context(self.nc.sbuf_tensor("%s_%d" % (name or "t", self.uid), list(shape), dt)))


def bc_rows(dram_ap, n):
    return bass.AP(dram_ap.tensor, dram_ap.offset, [[0, 128], [1, n]])


def build():
    import os
    LEVEL = int(os.environ.get("KLEVEL", "99"))
    nc = bass.Bass("TRN2", target_bir_lowering=False)

    def din(name, shape, dt=F32):
        return nc.dram_tensor(name, list(shape), dt, kind="ExternalInput").ap()

    def dout(name, shape, dt=F32):
        return nc.dram_tensor(name, list(shape), dt, kind="ExternalOutput").ap()

    def dscr(name, shape, dt=BF16):
        return nc.dram_tensor(name, list(shape), dt, kind="Internal").ap()

    xcat = din("xcat", [NT * 128, D])
    xsin = din("xs", [128, D])
    posd = din("pos", [128, 65])
    roped = din("rope", [65, 128, 64])
    ck = din("ck", [4, 2048, 512])
    cv = din("cv", [4, 2048, 512])
    sconv = din("sconv", [4, 3, D])
    sssm = din("sssm", [4, 512, 128])
    cmk = din("cmk", [4, 256, D])
    cmv = din("cmv", [4, 256, D])
    memd = din("mem", [256, D])
    w_in = din("w_in", [D, 3080])
    w_out = din("w_out", [D, D])
    w_xq = din("w_xq", [D, D])
    w_mk = din("w_mk", [D, D])
    w_mv = din("w_mv", [D, D])
    w_xo = din("w_xo", [D, D])
    w_up = din("w_up", [D, 4096])
    w_down = din("w_down", [4096, D])
    gcols_d = din("gcols", [128, 32])
    gfin_d = din("gfin", [1, D])
    convw_d = din("convw", [4, D])
    convb_d = din("convb", [1, D])
    small_d = din("small", [3, 8])
    gssm_d = din("gssm", [1, 512])
    c_ident = din("c_ident", [128, 128])
    c_tri = din("c_tri", [128, 128])
    c_mneg = din("c_mneg", [128, 128])
    c_cm = din("c_cm", [128, 17 * 128])
    c_sh = din("c_sh", [128, 7 * 128])
    c_shs = din("c_shs", [128, 4 * 128])
    c_cs = din("c_cs", [128, 64 * 16])
    c_cn = din("c_cn", [128, 16])
    c_cc = din("c_cc", [128, 4 * 16])
    c_rm = din("c_rm", [128, 4])
    y_o = dout("y", [NOWN * 128, D])
    ys_o = dout("ys", [128, D])
    wk_o = dout("wk", [NOWN * 128, 512])
    wv_o = dout("wv", [NOWN * 128, 512])
    convp_o = dout("convp", [3, D])
    ssmp_o = dout("ssmp", [512, 128])
    memk_o = dout("memk", [256, D])
    memv_o = dout("memv", [256, D])
    wks_o = dout("wks", [4, 2048, 512])
    wvs_o = dout("wvs", [4, 2048, 512])
    convs_o = dout("convs", [4, 3, D])
    ssms_o = dout("ssms", [4, 512, 128])
    wb_in = dscr("wb_in", [D, 3080])
    wb_out = dscr("wb_out", [D, D])
    wb_xq = dscr("wb_xq", [D, D])
    wb_mk = dscr("wb_mk", [D, D])
    wb_mv = dscr("wb_mv", [D, D])
    wb_xo = dscr("wb_xo", [D, D])
    wb_up = dscr("wb_up", [D, 4096])
    wb_down = dscr("wb_down", [4096, D])

    with ExitStack() as es:
        K = KB(nc, es)
        wres = {}

        def convert(items):
            for src, dst, rows in items:
                if dst.name not in wres:
                    wres[dst.name] = TT(None)
                step = 512
                for r0 in range(0, rows, step):
                    K.dma("pool", dst[r0:r0 + step, :], src[r0:r0 + step, :], w=[wres[dst.name]])

        wres["in_a"] = TT(None)
        wres["in_b"] = TT(None)
        for r0 in range(0, D, 512):
            K.dma("pool", wb_in[r0:r0 + 512, 0:1536], w_in[r0:r0 + 512, 0:1536], w=[wres["in_a"]])
        convert([(w_mk, wb_mk, D), (w_mv, wb_mv, D)])
        for r0 in range(0, D, 512):
            K.dma("pool", wb_in[r0:r0 + 512, 1536:3080], w_in[r0:r0 + 512, 1536:3080], w=[wres["in_b"]])
        for _w in (wb_out, wb_xq, wb_xo, wb_up, wb_down):
            wres[_w.name] = TT(None)

        late_q = []
        for _src, _dst, _rows in ((w_out, wb_out, D), (w_xq, wb_xq, D), (w_xo, wb_xo, D), (w_up, wb_up, D),
                                  (w_down, wb_down, 4096)):
            for r0 in range(0, _rows, 512):
                late_q.append(lambda _src=_src, _dst=_dst, r0=r0: K.dma(
                    "pool", _dst[r0:r0 + 512, :], _src[r0:r0 + 512, :], w=[wres[_dst.name]]))
        for bl in range(4):
            late_q.append(lambda bl=bl: K.dma("pool", wks_o[bl, 0:2044, :], ck[bl, 4:2048, :]))
            late_q.append(lambda bl=bl: K.dma("pool", wvs_o[bl, 0:2044, :], cv[bl, 4:2048, :]))

        def late_dmas(n=None):
            k = len(late_q) if n is None else min(n, len(late_q))
            for _ in range(k):
                late_q.pop(0)()
        pf = [TT(es.enter_context(nc.psum_tensor("pf%d" % i, [128, 512], F32))) for i in range(6)]
        pb = [TT(es.enter_context(nc.psum_tensor("pb%d" % i, [128, 1024], BF16))) for i in range(2)]
        st_ = dict(pf=0, pb=0)

        def npf():
            st_["pf"] = (st_["pf"] + 1) % 4
            return pf[st_["pf"]]

        def npb():
            st_["pb"] = (st_["pb"] + 1) % 2
            return pb[st_["pb"]]

        def cload(src_ap, shape, dt=F32, q="sp", stk=None):
            stk = stk if stk is not None else es
            t = K.sb(stk, shape, F32, "cst")
            K.dma(q, t[:], src_ap, w=[t])
            if dt == F32:
                return t
            tb = K.sb(stk, shape, dt, "cstb")
            K.op("dve", lambda e: e.tensor_copy(tb[:], t[:]), r=[t], w=[tb])
            return tb

        def cload_bf(stk, specs):
            outs = [K.sb(stk, shape, BF16, "cbf") for _, shape in specs]
            with ExitStack() as tmp:
                for (src_ap, shape), o in zip(specs, outs):
                    t = K.sb(tmp, shape, F32, "cst")
                    K.dma("sp", t[:], src_ap, w=[t])
                    K.op("dve", lambda e, o=o, t=t: e.tensor_copy(o[:], t[:]), r=[t], w=[o])
                K.barrier()
            return outs

        identf = cload(c_ident, [128, 128])
        identb = K.sb(es, [128, 128], BF16, "identb")
        K.op("dve", lambda e: e.tensor_copy(identb[:], identf[:]), r=[identf], w=[identb])
        onesf = K.sb(es, [128, 128], F32, "onesf")
        K.op("dve", lambda e: e.memset(onesf[:], 1.0), w=[onesf])
        epsT = K.sb(es, [128, 1], F32, "epsT")
        K.op("dve", lambda e: e.memset(epsT[:], EPS), w=[epsT])
        onesb = K.sb(es, [128, 128], BF16, "onesb")
        K.op("dve", lambda e: e.memset(onesb[:], 1.0), w=[onesb])
        gcols = cload(gcols_d, [128, 32])
        pos = cload(posd, [128, 65])
        valid = K.sb(es, [128, 65], F32, "valid")
        K.op("dve", lambda e: e.tensor_single_scalar(valid[:], pos[:], 0.0, ALU.is_ge), r=[pos], w=[valid])
        kbias = K.sb(es, [128, 65], F32, "kbias")
        K.op("dve", lambda e: e.tensor_scalar(kbias[:], valid[:], -1.0, -NEG, ALU.add, ALU.mult),
             r=[valid], w=[kbias])
        attT = K.sb(es, [128, 4, (NOWN + 1) * 128], BF16, "attT")
        yT = K.sb(es, [128, 4, (NOWN + 1) * 128], BF16, "yT")
        K.barrier()

        def wload(es2, src_b, kchunks, ncols, col0=0, name="w", q="sp", res=None):
            t = K.sb(es2, [128, kchunks, ncols], BF16, name)
            v = src_b.rearrange("(kc p) n -> p kc n", p=128)
            for kc0 in range(0, kchunks, 4):
                K.dma(q, t[:, kc0:kc0 + 4, :], v[:, kc0:kc0 + 4, col0:col0 + ncols], r=[wres[res or src_b.name]], w=[t])
            return t

        def norm_T(wk, x, gi, dst, dst_ap):
            junk, ss, rstd, xn = wk["junk"], wk["ss"], wk["rstd"], wk["xn"]
            K.op("act", lambda e: e.activation(junk[:], x[:], AF.Square, accum_out=ss[:, 0:1]), r=[x], w=[junk, ss])
            K.op("act", lambda e: e.activation(rstd[:], ss[:], AF.Ln, scale=1.0 / D, bias=epsT[:, 0:1]), r=[ss, epsT], w=[rstd])
            K.op("act", lambda e: e.activation(rstd[:], rstd[:], AF.Exp, scale=-0.5), r=[rstd], w=[rstd])
            K.op("act", lambda e: e.activation(xn[:], x[:], AF.Copy, scale=rstd[:, 0:1]), r=[x, rstd], w=[xn])
            p = npb()
            for j in range(8):
                K.op("pe", lambda e, j=j: e.transpose(p[:, j * 128:(j + 1) * 128], xn[:, j * 128:(j + 1) * 128], identb[:]),
                     r=[xn, identb], w=[p])
            g = gcols[:, gi * 8:(gi + 1) * 8].unsqueeze(2).to_broadcast([128, 8, 128])
            K.op("dve", lambda e: e.tensor_tensor(dst_ap, p[:, :].rearrange("p (k t) -> p k t", t=128), g, ALU.mult),
                 r=[p, gcols], w=[dst])
            return rstd

        def proj(ps, ncols, hT, hT_ap, W, col0):
            for kc in range(8):
                K.op("pe", lambda e, kc=kc: e.matmul(ps[:, 0:ncols], hT_ap[:, kc, :], W[:, kc, col0:col0 + ncols],
                                                       start=(kc == 0), stop=(kc == 7)),
                     r=[hT, W], w=[ps])

        def transpose4(src, src_ap_fn, dst, dst_ap):
            p = npb()
            for j in range(4):
                K.op("pe", lambda e, j=j: e.transpose(p[:, j * 128:(j + 1) * 128], src_ap_fn(j), identb[:]),
                     r=[src, identb], w=[p])
            K.op("act", lambda e: e.copy(dst_ap, p[:, 0:512]), r=[p], w=[dst])

        def mem_prepare(es2, wk, ktok, vtok, KTm, Vm):
            kb = wk["kb"]
            K.op("act", lambda e: e.copy(kb[:], ktok[:]), r=[ktok], w=[kb])
            K.op("pool", lambda e: e.tensor_copy(Vm[:], vtok[:]), r=[vtok], w=[Vm])
            for mb in range(2):
                for half in range(2):
                    p = npb()
                    for j in range(4):
                        c = half * 4 + j
                        K.op("pe", lambda e, j=j, c=c, mb=mb: e.transpose(p[:, j * 128:(j + 1) * 128],
                                                                         kb[:, mb, c * 128:(c + 1) * 128], identb[:]),
                             r=[kb, identb], w=[p])
                    K.op("dve", lambda e, mb=mb, half=half: e.tensor_copy(
                        KTm[:, half * 4:half * 4 + 4, mb * 128:(mb + 1) * 128],
                        p[:, 0:512].rearrange("p (c m) -> p c m", m=128)), r=[p], w=[KTm])

        def xattn(wk, qT, qT_ap, n, KTm, Vm, OT, OT_ap):
            def st1(h):
                Ps = []
                for mb in range(2):
                    s = npf()
                    for dc in range(2):
                        K.op("pe", lambda e, dc=dc, mb=mb, s=s: e.matmul(
                            s[:, 0:n], KTm[:, 2 * h + dc, mb * 128:(mb + 1) * 128], qT_ap[:, 2 * h + dc, :],
                            start=(dc == 0), stop=(dc == 1)), r=[KTm, qT], w=[s])
                    P = wk["xP"][2 * (h % 2) + mb]
                    K.op("act", lambda e, s=s, P=P: e.activation(P[:, 0:n], s[:, 0:n], AF.Exp), r=[s], w=[P])
                    Ps.append(P)
                return Ps

            def st2(h, Ps):
                dn = npf()
                for mb in range(2):
                    K.op("pe", lambda e, mb=mb: e.matmul(dn[:, 0:n], onesb[:], Ps[mb][:, 0:n],
                                                         start=(mb == 0), stop=(mb == 1)), r=[onesb, Ps[mb]], w=[dn])
                rec = wk["xrec"][h % 2]
                K.op("dve", lambda e: e.reciprocal(rec[:, 0:n], dn[:, 0:n]), r=[dn], w=[rec])
                for dc in range(2):
                    o = npf()
                    for mb in range(2):
                        K.op("pe", lambda e, mb=mb, dc=dc, o=o: e.matmul(
                            o[:, 0:n], Vm[:, mb, h * 256 + dc * 128:h * 256 + dc * 128 + 128], Ps[mb][:, 0:n],
                            start=(mb == 0), stop=(mb == 1)), r=[Vm, Ps[mb]], w=[o])
                    K.op("dve", lambda e, dc=dc, o=o: e.tensor_tensor(OT_ap[:, 2 * h + dc, :], o[:, 0:n], rec[:, 0:n], ALU.mult),
                         r=[o, rec], w=[OT])

            prev = None
            for h in range(4):
                Ps = st1(h)
                if prev is not None:
                    st2(*prev)
                prev = (h, Ps)
            st2(*prev)

        KTm_p = K.sb(es, [128, 8, 256], BF16, "KTm_p")
        Vm_p = K.sb(es, [128, 2, D], BF16, "Vm_p")

        with ExitStack() as e2:
          if LEVEL >= 2:
            NB = 3
            LAG = NB - 1
            NR = 18
            junk_ = K.sb(e2, [128, D], F32)
            wks_ = [dict(junk=junk_, ss=K.sb(e2, [128, 1], F32), rstd=K.sb(e2, [128, 1], F32),
                         xn=K.sb(e2, [128, D], BF16)) for _ in range(2)]
            cm, ccs, ccn, ccc = cload_bf(e2, [(c_cm, [128, 17 * 128]), (c_cs, [128, 64 * 16]), (c_cn, [128, 16]),
                                              (c_cc, [128, 4 * 16])])
            wqkv = wload(e2, wb_in, 8, 1536, name="wqkv", res="in_a")
            KT = [K.sb(e2, [128, 4, 128], BF16, "KT") for _ in range(NR)]
            VX = [K.sb(e2, [128, 8, 72], BF16, "VX") for _ in range(NR)]
            for v_ in VX:
                K.op("pool", lambda e, v_=v_: e.memset(v_[:], 1.0), w=[v_])
            xts = [K.sb(e2, [128, D], F32, "xt") for _ in range(2)]
            hTs = [K.sb(e2, [128, 8, 128], BF16) for _ in range(2)]
            css = [K.sb(e2, [128, 64], F32) for _ in range(2)]
            qks = [K.sb(e2, [128, 16, 64], F32) for _ in range(2)]
            ros = [K.sb(e2, [128, 16, 64], F32) for _ in range(2)]
            tas = [K.sb(e2, [128, 16, 32], F32)] * 2
            tbs = [K.sb(e2, [128, 16, 32], F32)] * 2
            qkbs = [K.sb(e2, [128, 16, 64], BF16) for _ in range(2)]
            QTs = [K.sb(e2, [128, 4, 128], BF16) for _ in range(2)]
            vfs = [K.sb(e2, [128, 512], F32) for _ in range(2)]
            Eb = [[K.sb(e2, [128, 512], BF16, "E") for _ in range(2)] for _ in range(NB)]
            Pb = [[K.sb(e2, [128, 512], BF16, "P") for _ in range(2)] for _ in range(NB)]
            rec8 = K.sb(e2, [128, 8], F32)
            att = K.sb(e2, [128, 512], BF16)
            Oacc = [pf[4], pf[5]]

            def qkv_tile(j, xsrc, full, slot, parts="ab"):
                par = j % 2
                xt, hT, cs, qk, ro, ta, tb, qkb, QT, vf = (xts[par], hTs[par], css[par], qks[par], ros[par], tas[par],
                                                           tbs[par], qkbs[par], QTs[par], vfs[par])
                if "a" in parts:
                    K.dma("sp", xt[:], xsrc, w=[xt])
                    K.dma("sp", cs[:], roped[j], w=[cs])
                    norm_T(wks_[par], xt, 0, hT, hT[:, :, :])
                if "b" not in parts:
                    return None, None
                for idx, c0 in ((0, 0), (1, 512)):
                    if idx == 0 and not full:
                        continue
                    ps = npf()
                    proj(ps, 512, hT, hT[:, :, :], wqkv, c0)
                    K.op("act", lambda e, ps=ps, idx=idx: e.copy(
                        qk[:, idx * 8:(idx + 1) * 8, :], ps[:, :].rearrange("p (h d) -> p h d", d=64)), r=[ps], w=[qk])
                ps = npf()
                proj(ps, 512, hT, hT[:, :, :], wqkv, 1024)
                K.op("act", lambda e: e.copy(vf[:], ps[:, :]), r=[ps], w=[vf])
                K.op("dve", lambda e: e.tensor_copy(VX[slot][:, :, 0:64], vf[:].rearrange("p (h d) -> p h d", d=64)),
                     r=[vf], w=[VX[slot]])
                h0 = 0 if full else 8
                nh = 16 - h0
                cosb = cs[:, 0:32].unsqueeze(1).to_broadcast([128, nh, 32])
                sinb = cs[:, 32:64].unsqueeze(1).to_broadcast([128, nh, 32])
                t1 = qk[:, h0:16, 0:32]
                t2 = qk[:, h0:16, 32:64]
                K.op("dve", lambda e: e.tensor_tensor(ta[:, h0:16, :], t1, cosb, ALU.mult), r=[qk, cs], w=[ta])
                K.op("dve", lambda e: e.tensor_tensor(tb[:, h0:16, :], t2, sinb, ALU.mult), r=[qk, cs], w=[tb])
                K.op("dve", lambda e: e.tensor_tensor(ro[:, h0:16, 0:32], ta[:, h0:16, :], tb[:, h0:16, :], ALU.subtract),
                     r=[ta, tb], w=[ro])
                K.op("dve", lambda e: e.tensor_tensor(ta[:, h0:16, :], t2, cosb, ALU.mult), r=[qk, cs], w=[ta])
                K.op("dve", lambda e: e.tensor_tensor(tb[:, h0:16, :], t1, sinb, ALU.mult), r=[qk, cs], w=[tb])
                K.op("dve", lambda e: e.tensor_tensor(ro[:, h0:16, 32:64], ta[:, h0:16, :], tb[:, h0:16, :], ALU.add),
                     r=[ta, tb], w=[ro])
                K.op("act", lambda e: e.copy(qkb[:, 8:16, :], ro[:, 8:16, :]), r=[ro], w=[qkb])
                transpose4(qkb, lambda jj: qkb[:, 8 + 2 * jj:10 + 2 * jj, :].rearrange("p h d -> p (h d)"),
                           KT[slot], KT[slot][:, :, :].rearrange("p c t -> p (c t)"))
                if full:
                    K.op("act", lambda e: e.mul(qkb[:, 0:8, :], ro[:, 0:8, :], 0.125), r=[ro], w=[qkb])
                    transpose4(qkb, lambda jj: qkb[:, 2 * jj:2 * jj + 2, :].rearrange("p h d -> p (h d)"),
                               QT, QT[:, :, :].rearrange("p c t -> p (c t)"))
                return ro, vf

            def unit_front(u, QT, KTt, nq, mask_ap_fn, bias_ap):
                for hg in range(2):
                    s = npf()
                    for hh in range(4):
                        K.op("pe", lambda e, hh=hh, hg=hg, s=s: e.matmul(
                            s[:, hh * nq:(hh + 1) * nq], KTt[hg * 64:hg * 64 + 64, hh, :],
                            QT[hg * 64:hg * 64 + 64, hh, 0:nq], start=True, stop=True), r=[KTt, QT], w=[s])
                    E = Eb[u % NB][hg]
                    if bias_ap is not None:
                        K.op("act", lambda e, E=E, s=s: e.activation(E[:, 0:4 * nq], s[:, 0:4 * nq], AF.Exp, bias=bias_ap),
                             r=[s, kbias], w=[E])
                    else:
                        K.op("act", lambda e, E=E, s=s: e.activation(E[:, 0:4 * nq], s[:, 0:4 * nq], AF.Exp), r=[s], w=[E])
                    P = Pb[u % NB][hg]
                    K.op("dve", lambda e, E=E, P=P: e.tensor_tensor(
                        P[:, 0:4 * nq].rearrange("p (h q) -> p h q", q=nq),
                        E[:, 0:4 * nq].rearrange("p (h q) -> p h q", q=nq), mask_ap_fn(), ALU.mult),
                         r=[E, cm, ccs, ccn, ccc], w=[P])

            def unit_back(u, VXt, nq, first, last):
                for hg in range(2):
                    P = Pb[u % NB][hg]
                    for hh in range(4):
                        h = 2 * hh + hg
                        K.op("pe", lambda e, hh=hh, h=h, hg=hg, P=P: e.matmul(
                            Oacc[hg][0:nq, hh * 128:hh * 128 + 65], P[:, hh * nq:(hh + 1) * nq], VXt[:, h, 0:65],
                            start=(first and hh == 0), stop=(last and hh == 3)), r=[P, VXt], w=[Oacc[hg]])

            def run_units(units, QT, nq):
                n = len(units)
                for i in range(n + LAG):
                    if i < n:
                        prep, ktf, vxf, mfn, bias = units[i]
                        if prep is not None:
                            prep()
                        unit_front(i, QT, ktf(), nq, mfn, bias)
                    k = i - LAG
                    if k >= 0:
                        unit_back(k, units[k][2](), nq, k == 0, k == n - 1)

            def attn_finish(nq, col0):
                for hg in range(2):
                    o3 = Oacc[hg][:, 0:512].rearrange("p (h d) -> p h d", d=128)
                    K.op("dve", lambda e: e.reciprocal(rec8[:, hg * 4:hg * 4 + 4].unsqueeze(2), o3[:, :, 64:65]),
                         r=[Oacc[hg]], w=[rec8])
                    K.op("dve", lambda e: e.tensor_tensor(
                        att[:, :].rearrange("p (pr hf d) -> p pr hf d", hf=2, d=64)[:, :, hg, :], o3[:, :, 0:64],
                        rec8[:, hg * 4:hg * 4 + 4].unsqueeze(2).to_broadcast([128, 4, 64]), ALU.mult),
                         r=[Oacc[hg], rec8], w=[att])
                transpose4(att, lambda jj: att[:, jj * 128:(jj + 1) * 128], attT,
                           attT[:, :, col0:col0 + 128].rearrange("p c t -> p c t"))

            fast = bool(os.environ.get("KFAST"))
            jlist = list(range(32, NT)) if not fast else list(range(46, 52))
            jmin = jlist[0]

            def do_attn(j):
                jj = j - NPRE
                units = []
                for o in range(16, -1, -1):
                    kt = j - o
                    if kt < jmin:
                        continue
                    sl = kt % NR
                    units.append((None, lambda sl=sl: KT[sl], lambda sl=sl: VX[sl],
                                  lambda o=o: cm[:, o * 128:(o + 1) * 128].unsqueeze(1).to_broadcast([128, 4, 128]),
                                  kbias[:, kt:kt + 1]))
                run_units(units, QTs[j % 2], 128)
                attn_finish(128, jj * 128)

            prev_full = None
            qkv_tile(jlist[0], xcat[jlist[0] * 128:(jlist[0] + 1) * 128, :], jlist[0] >= NPRE, jlist[0] % NR, parts="a")
            for ji, j in enumerate(jlist):
                full = j >= NPRE
                if ji + 1 < len(jlist):
                    jn = jlist[ji + 1]
                    qkv_tile(jn, xcat[jn * 128:(jn + 1) * 128, :], jn >= NPRE, jn % NR, parts="a")
                ro, vf = qkv_tile(j, xcat[j * 128:(j + 1) * 128, :], full, j % NR, parts="b")
                if full:
                    late_dmas(1)
                if full:
                    jj = j - NPRE
                    K.dma("sp", wk_o[jj * 128:(jj + 1) * 128, :], ro[:, 8:16, :].rearrange("p h d -> p (h d)"), r=[ro])
                    K.dma("sp", wv_o[jj * 128:(jj + 1) * 128, :], vf[:], r=[vf])
                if prev_full is not None:
                    do_attn(prev_full)
                prev_full = j if full else None
            if prev_full is not None:
                do_attn(prev_full)

            if LEVEL >= 3:
                NBK = NB + 1
                KTs = [K.sb(e2, [128, 4, 128], BF16) for _ in range(NBK)]
                VXs = [K.sb(e2, [128, 8, 72], BF16) for _ in range(NBK)]
                for v_ in VXs:
                    K.op("pool", lambda e, v_=v_: e.memset(v_[:], 1.0), w=[v_])
                ckf = [K.sb(e2, [128, 512], F32, "ckf") for _ in range(3)]
                cvf = [K.sb(e2, [128, 512], F32, "cvf") for _ in range(3)]
                ckb = [K.sb(e2, [128, 512], BF16) for _ in range(2)]
                sslot = 0
                ro, vf = qkv_tile(ST, xsin, True, sslot)
                for bl in range(4):
                    K.dma("sp", wks_o[bl, 2044:2048, :], ro[4 * bl:4 * bl + 4, 8:16, :].rearrange("p h d -> p (h d)"), r=[ro])
                    K.dma("sp", wvs_o[bl, 2044:2048, :], vf[4 * bl:4 * bl + 4, :], r=[vf])
                units = []
                n_ = 0

                def finish_prep(n_, a, b_):
                    kb_ = ckb[n_ % 2]
                    kts_, vxs_ = KTs[n_ % NBK], VXs[n_ % NBK]
                    K.op("pool", lambda e: e.tensor_copy(kb_[:], a[:]), r=[a], w=[kb_])
                    transpose4(kb_, lambda jj: kb_[:, jj * 128:(jj + 1) * 128], kts_,
                               kts_[:, :, :].rearrange("p c t -> p (c t)"))
                    K.op("pool", lambda e: e.tensor_copy(vxs_[:, :, 0:64], b_[:].rearrange("p (h d) -> p h d", d=64)),
                         r=[b_], w=[vxs_])

                for bl in range(4):
                    for g in range(3):
                        def prep(bl=bl, g=g, n_=n_):
                            a, b_ = ckf[n_ % 3], cvf[n_ % 3]
                            for tq in range(4):
                                K.dma("sp", a[32 * tq:32 * tq + 32, :], ck[bl, 512 * g + tq:512 * (g + 1):16, :], w=[a])
                                K.dma("sp", b_[32 * tq:32 * tq + 32, :], cv[bl, 512 * g + tq:512 * (g + 1):16, :], w=[b_])
                            finish_prep(n_, a, b_)
                        units.append((prep, lambda n_=n_: KTs[n_ % NBK], lambda n_=n_: VXs[n_ % NBK],
                                      lambda bl=bl: ccc[:, bl * 16:(bl + 1) * 16].unsqueeze(1).to_broadcast([128, 4, 16]),
                                      None))
                        n_ += 1
                    for kt in range(12, 16):
                        def prep(bl=bl, kt=kt, n_=n_):
                            a, b_ = ckf[n_ % 3], cvf[n_ % 3]
                            K.dma("sp", a[:], ck[bl, kt * 128:(kt + 1) * 128, :], w=[a])
                            K.dma("sp", b_[:], cv[bl, kt * 128:(kt + 1) * 128, :], w=[b_])
                            finish_prep(n_, a, b_)
                        units.append((prep, lambda n_=n_: KTs[n_ % NBK], lambda n_=n_: VXs[n_ % NBK],
                                      lambda bl=bl, kt=kt: ccs[:, (bl * 16 + kt) * 16:(bl * 16 + kt + 1) * 16].unsqueeze(1).to_broadcast([128, 4, 16]),
                                      None))
                        n_ += 1
                units.append((None, lambda: KT[sslot], lambda: VX[sslot],
                              lambda: ccn[:, 0:16].unsqueeze(1).to_broadcast([128, 4, 16]), None))
                run_units(units, QTs[ST % 2], 16)
                attn_finish(16, NOWN * 128)
            K.barrier()

        with ExitStack() as e2:
            wk = dict(junk=K.sb(e2, [128, D], F32), ss=K.sb(e2, [128, 1], F32), rstd=K.sb(e2, [128, 1], F32),
                      xn=K.sb(e2, [128, D], BF16), kb=K.sb(e2, [128, 2, D], BF16))
            wmk = wload(e2, wb_mk, 8, D, name="wmk")
            wmv = wload(e2, wb_mv, 8, D, name="wmv")
            mT = K.sb(e2, [128, 8, 256], BF16)
            ktok = K.sb(e2, [128, 2, D], F32)
            vtok = K.sb(e2, [128, 2, D], F32)
            for mb in range(2):
                xt = K.sb(e2, [128, D], F32)
                K.dma("sp", xt[:], memd[mb * 128:(mb + 1) * 128, :], w=[xt])
                norm_T(wk, xt, 3, mT, mT[:, :, mb * 128:(mb + 1) * 128])
            for mb in range(2):
                for W, tok in ((wmk, ktok), (wmv, vtok)):
                    for cg in range(2):
                        ps = npf()
                        proj(ps, 512, mT, mT[:, :, mb * 128:(mb + 1) * 128], W, cg * 512)
                        K.op("act", lambda e, ps=ps, tok=tok, cg=cg, mb=mb: e.copy(tok[:, mb, cg * 512:(cg + 1) * 512], ps[:, :]),
                             r=[ps], w=[tok])
            K.dma("sp", memk_o.rearrange("(m p) n -> p m n", p=128), ktok[:], r=[ktok])
            K.dma("sp", memv_o.rearrange("(m p) n -> p m n", p=128), vtok[:], r=[vtok])
            mem_prepare(e2, wk, ktok, vtok, KTm_p, Vm_p)
            K.barrier()

        STf = K.sb(es, [128, 512], F32, "STf")
        with ExitStack() as e2:
          if LEVEL >= 4:
            wk = dict(junk=K.sb(e2, [128, D], F32), ss=K.sb(e2, [128, 1], F32), rstd=K.sb(e2, [128, 1], F32),
                      xn=K.sb(e2, [128, D], BF16))
            tri = cload(c_tri, [128, 128], stk=e2)
            mneg = cload(c_mneg, [128, 128], stk=e2)
            sh = cload(c_sh, [128, 7 * 128], BF16, stk=e2)
            shs = cload(c_shs, [128, 4 * 128], BF16, stk=e2)
            rm = cload(c_rm, [128, 4], stk=e2)
            cw = K.sb(e2, [128, 4, D], F32)
            for i in range(4):
                K.dma("sp", cw[:, i, :], bc_rows(convw_d[i:i + 1, :], D), w=[cw])
            cbf = K.sb(e2, [1, D], F32, "cbf")
            K.dma("sp", cbf[:], convb_d, w=[cbf])
            cbrow = K.sb(e2, [1, D], BF16, "cbrow")
            K.op("dve", lambda e: e.tensor_copy(cbrow[:], cbf[:]), r=[cbf], w=[cbrow])
            gs = cload(bc_rows(gssm_d, 512), [128, 512], stk=e2)
            sm = K.sb(e2, [128, 3, 8], F32)
            for i in range(3):
                K.dma("sp", sm[:, i, :], bc_rows(small_d[i:i + 1, :], 8), w=[sm])
            Aneg = K.sb(e2, [128, 8], F32)
            K.op("act", lambda e: e.activation(Aneg[:], sm[:, 1, :], AF.Exp), r=[sm], w=[Aneg])
            K.op("dve", lambda e: e.tensor_scalar(Aneg[:], Aneg[:], -1.0, None, ALU.mult), r=[Aneg], w=[Aneg])
            wz = wload(e2, wb_in, 8, 1544, col0=1536, name="wz", res="in_b")
            xts = [K.sb(e2, [128, D], F32, "xt") for _ in range(2)]
            hTs_ = [K.sb(e2, [128, 8, 128], BF16) for _ in range(2)]
            xbcs_ = [K.sb(e2, [128, D], F32) for _ in range(3)]
            hT, xbc = hTs_[0], xbcs_[0]
            xw = [[K.sb(e2, [128, D], BF16, "xw") for _ in range(4)] for _ in range(2)]
            for t_ in xw[0] + xw[1]:
                K.op("pool", lambda e, t_=t_: e.memset(t_[:], 0.0), w=[t_])
            xcs_ = [K.sb(e2, [128, D], F32) for _ in range(3)]
            xc = xcs_[0]
            xcb = K.sb(e2, [128, 512], BF16)
            dtrs_ = [K.sb(e2, [128, 8], F32) for _ in range(5)]
            dtms_ = [K.sb(e2, [128, 8], F32) for _ in range(5)]
            dtr, dtm = dtrs_[0], dtms_[0]
            av = K.sb(e2, [128, 8], F32)
            acs = K.sb(e2, [128, 8], F32)
            nacs = K.sb(e2, [128, 8], F32)
            toe = K.sb(e2, [128, 8], F32)
            dec = K.sb(e2, [128, 8], F32)
            fs = K.sb(e2, [128, 8], F32)
            xdt = K.sb(e2, [128, 512], BF16)
            xw2 = K.sb(e2, [128, 512], BF16)
            STb = K.sb(e2, [128, 512], BF16)
            abc = K.sb(e2, [128, 8, 128], F32)
            T1 = K.sb(e2, [128, 8, 128], F32)
            Lm = K.sb(e2, [128, 8, 128], F32)
            scT = K.sb(e2, [128, 8, 128], BF16)
            BCT = K.sb(e2, [128, 4, 128], BF16)
            ysum = K.sb(e2, [128, 512], F32)
            ytmp = K.sb(e2, [128, 512], F32)
            yS = K.sb(e2, [128, 512], F32)
            zss_ = [K.sb(e2, [128, 512], F32) for _ in range(5)]
            zs = zss_[0]
            ss2 = K.sb(e2, [128, 2], F32)
            yb = K.sb(e2, [128, 512], BF16)
            X7 = K.sb(e2, [128, D], F32)
            stl = K.sb(e2, [128, 4, 128], F32)
            K.op("dve", lambda e: e.memset(STf[:], 0.0), w=[STf])
            K.op("dve", lambda e: e.memset(STb[:], 0.0), w=[STb])

            def conv_dt(j, cur, prev, shm, nsh_prev, src_x, full):
                full = full or j == NPRE - 1
                ncol = D if full else 768
                for i in range(4):
                    K.op("pool", lambda e, i=i: e.tensor_tensor(cur[i][:, 0:ncol], src_x[:, 0:ncol], cw[:, i, 0:ncol], ALU.mult),
                         r=[src_x, cw], w=[cur[i]])
                for cg in range(2):
                    ps = npf()
                    nw = 512 if (full or cg == 0) else 256
                    for i in range(4):
                        K.op("pe", lambda e, i=i, cg=cg, ps=ps, nw=nw: e.matmul(
                            ps[:, 0:nw], shm[:, i * 128:(i + 1) * 128], cur[i][:, cg * 512:cg * 512 + nw],
                            start=(i == 0), stop=False), r=[shm, cur[i]], w=[ps])
                    for i in range(nsh_prev):
                        K.op("pe", lambda e, i=i, cg=cg, ps=ps, nw=nw: e.matmul(
                            ps[:, 0:nw], shm[:, (4 + i) * 128:(5 + i) * 128], prev[i][:, cg * 512:cg * 512 + nw],
                            start=False, stop=False), r=[shm, prev[i]], w=[ps])
                    K.op("pe", lambda e, cg=cg, ps=ps, nw=nw: e.matmul(
                        ps[:, 0:nw], onesb[0:1, :], cbrow[0:1, cg * 512:cg * 512 + nw], start=False, stop=True),
                         r=[onesb, cbrow], w=[ps])
                    K.op("act", lambda e, cg=cg, ps=ps, nw=nw: e.activation(xc[:, cg * 512:cg * 512 + nw], ps[:, 0:nw], AF.Silu),
                         r=[ps], w=[xc])

            def ssd_chunk(full, dtm_ap_t, ST_f, ST_b, parts="abc"):
                nbc = 512 if full else 256
                if "a" in parts:
                    K.op("dve", lambda e: e.tensor_tensor(av[:], dtm_ap_t[:], Aneg[:], ALU.mult), r=[dtm_ap_t, Aneg], w=[av])
                    p1 = npf()
                    K.op("pe", lambda e: e.matmul(p1[:, 0:8], onesf[:], av[:], start=True, stop=True), r=[onesf, av], w=[p1])
                    K.op("pe", lambda e: e.matmul(p1[:, 8:16], tri[:], av[:], start=True, stop=True), r=[tri, av], w=[p1])
                    K.op("act", lambda e: e.copy(acs[:], p1[:, 8:16]), r=[p1], w=[acs])
                    K.op("dve", lambda e: e.tensor_tensor(toe[:], p1[:, 0:8], acs[:], ALU.subtract), r=[p1, acs], w=[toe])
                    K.op("act", lambda e: e.activation(toe[:], toe[:], AF.Exp), r=[toe], w=[toe])
                    K.op("act", lambda e: e.activation(dec[:], p1[:, 0:8], AF.Exp), r=[p1], w=[dec])
                    xs3 = xc[:, 0:512].rearrange("p (h d) -> p h d", d=64)
                    K.op("dve", lambda e: e.tensor_tensor(xdt[:].rearrange("p (h d) -> p h d", d=64), xs3,
                                                          dtm_ap_t[:].unsqueeze(2).to_broadcast([128, 8, 64]), ALU.mult),
                         r=[xc, dtm_ap_t], w=[xdt])
                    K.op("dve", lambda e: e.tensor_tensor(xw2[:].rearrange("p (h d) -> p h d", d=64),
                                                          xdt[:].rearrange("p (h d) -> p h d", d=64),
                                                          toe[:].unsqueeze(2).to_broadcast([128, 8, 64]), ALU.mult),
                         r=[xdt, toe], w=[xw2])
                    nbc = 512 if full else 256
                    K.op("dve", lambda e: e.tensor_copy(xcb[:, 0:nbc], xc[:, 512:512 + nbc]), r=[xc], w=[xcb])
                if full and "b" in parts:
                    transpose4(xcb, lambda jj: xcb[:, jj * 128:(jj + 1) * 128], BCT, BCT[:, :, :].rearrange("p c t -> p (c t)"))
                    K.op("act", lambda e: e.mul(nacs[:], acs[:], -1.0), r=[acs], w=[nacs])
                    K.op("act", lambda e: e.activation(fs[:], acs[:], AF.Exp), r=[acs], w=[fs])
                    K.op("pool", lambda e: e.tensor_tensor(abc[:], av[:].unsqueeze(2).to_broadcast([128, 8, 128]),
                                                          tri[:].unsqueeze(1).to_broadcast([128, 8, 128]), ALU.mult),
                         r=[av, tri], w=[abc])
                    g_ = npf()
                    for g in range(2):
                        K.op("pe", lambda e, g=g: e.matmul(g_[:, g * 128:(g + 1) * 128], BCT[:, g, :], BCT[:, 2 + g, :],
                                                           start=True, stop=True), r=[BCT], w=[g_])
                    for half in range(2):
                        R = npf()
                        K.op("pe", lambda e, half=half, R=R: e.matmul(
                            R[:, :], onesf[:], abc[:, half * 4:half * 4 + 4, :].rearrange("p h l -> p (h l)"),
                            start=True, stop=True), r=[onesf, abc], w=[R])
                        K.op("dve", lambda e, half=half, R=R: e.tensor_tensor(
                            T1[:, half * 4:half * 4 + 4, :], R[:, :].rearrange("p (h l) -> p h l", l=128),
                            mneg[:].unsqueeze(1).to_broadcast([128, 4, 128]), ALU.add), r=[R, mneg], w=[T1])
                    for h in range(8):
                        K.op("act", lambda e, h=h: e.activation(Lm[:, h, :], T1[:, h, :], AF.Exp, bias=nacs[:, h:h + 1]),
                             r=[T1, nacs], w=[Lm])
                    for g in range(2):
                        K.op("dve", lambda e, g=g: e.tensor_tensor(
                            scT[:, g * 4:g * 4 + 4, :], Lm[:, g * 4:g * 4 + 4, :],
                            g_[:, g * 128:(g + 1) * 128].unsqueeze(1).to_broadcast([128, 4, 128]), ALU.mult),
                             r=[Lm, g_], w=[scT])
                    yd = npf()
                    for h in range(8):
                        K.op("pe", lambda e, h=h: e.matmul(yd[:, h * 64:(h + 1) * 64], scT[:, h, :],
                                                           xdt[:, h * 64:(h + 1) * 64], start=True, stop=True),
                             r=[scT, xdt], w=[yd])
                    K.op("act", lambda e: e.copy(ysum[:], yd[:, :]), r=[yd], w=[ysum])
                if full and "c" in parts:
                    yo = npf()
                    for h in range(8):
                        K.op("pe", lambda e, h=h: e.matmul(yo[:, h * 64:(h + 1) * 64], BCT[:, 2 + h // 4, :],
                                                           ST_b[:, h * 64:(h + 1) * 64], start=True, stop=True),
                             r=[BCT, ST_b], w=[yo])
                    K.op("dve", lambda e: e.tensor_tensor(ytmp[:].rearrange("p (h d) -> p h d", d=64),
                                                          yo[:, :].rearrange("p (h d) -> p h d", d=64),
                                                          fs[:].unsqueeze(2).to_broadcast([128, 8, 64]), ALU.mult),
                         r=[yo, fs], w=[ytmp])
                    K.op("dve", lambda e: e.tensor_tensor(ysum[:], ysum[:], ytmp[:], ALU.add), r=[ysum, ytmp], w=[ysum])
                if "c" in parts:
                    cs_ = npf()
                    for g in range(2):
                        K.op("pe", lambda e, g=g: e.matmul(cs_[:, g * 256:(g + 1) * 256], xcb[:, g * 128:(g + 1) * 128],
                                                           xw2[:, g * 256:(g + 1) * 256], start=True, stop=True),
                             r=[xcb, xw2], w=[cs_])
                    K.op("dve", lambda e: e.tensor_tensor(ST_f[:].rearrange("p (h d) -> p h d", d=64),
                                                          ST_f[:].rearrange("p (h d) -> p h d", d=64),
                                                          dec[:].unsqueeze(2).to_broadcast([128, 8, 64]), ALU.mult),
                         r=[ST_f, dec], w=[ST_f])
                    K.op("dve", lambda e: e.tensor_tensor(ST_f[:], ST_f[:], cs_[:, :], ALU.add), r=[ST_f, cs_], w=[ST_f])
                    K.op("act", lambda e: e.copy(ST_b[:], ST_f[:]), r=[ST_f], w=[ST_b])

            def ssd_post(ysrc, zps, col0):
                K.op("dve", lambda e: e.tensor_tensor(ytmp[:].rearrange("p (h d) -> p h d", d=64),
                                                      xc[:, 0:512].rearrange("p (h d) -> p h d", d=64),
                                                      sm[:, 2, :].unsqueeze(2).to_broadcast([128, 8, 64]), ALU.mult),
                     r=[xc, sm], w=[ytmp])
                K.op("dve", lambda e: e.tensor_tensor(ytmp[:], ytmp[:], ysrc[:], ALU.add), r=[ytmp, ysrc], w=[ytmp])
                K.op("dve", lambda e: e.tensor_tensor(ytmp[:], ytmp[:], zs[:], ALU.mult), r=[ytmp, zs], w=[ytmp])
                for g in range(2):
                    K.op("act", lambda e, g=g: e.activation(zs[:, g * 256:(g + 1) * 256], ytmp[:, g * 256:(g + 1) * 256], AF.Square,
                                                            accum_out=ss2[:, g:g + 1]), r=[ytmp], w=[zs, ss2])
                K.op("act", lambda e: e.activation(ss2[:], ss2[:], AF.Ln, scale=1.0 / 256, bias=epsT[:, 0:1]), r=[ss2, epsT], w=[ss2])
                K.op("act", lambda e: e.activation(ss2[:], ss2[:], AF.Exp, scale=-0.5), r=[ss2], w=[ss2])
                K.op("dve", lambda e: e.tensor_tensor(ytmp[:].rearrange("p (g d) -> p g d", d=256),
                                                      ytmp[:].rearrange("p (g d) -> p g d", d=256),
                                                      ss2[:].unsqueeze(2).to_broadcast([128, 2, 256]), ALU.mult),
                     r=[ytmp, ss2], w=[ytmp])
                K.op("dve", lambda e: e.tensor_tensor(yb[:], ytmp[:], gs[:], ALU.mult), r=[ytmp, gs], w=[yb])
                transpose4(yb, lambda jj: yb[:, jj * 128:(jj + 1) * 128], yT, yT[:, :, col0:col0 + 128].rearrange("p c t -> p c t"))

            def dt_chain(ps8, vcol):
                K.op("dve", lambda e: e.tensor_tensor(dtr[:], ps8, sm[:, 0, :], ALU.add), r=[sm], w=[dtr])
                K.op("act", lambda e: e.activation(dtr[:], dtr[:], AF.Exp), r=[dtr], w=[dtr])
                K.op("act", lambda e: e.activation(dtr[:], dtr[:], AF.Ln, bias=1.0), r=[dtr], w=[dtr])
                K.op("dve", lambda e: e.tensor_scalar(dtm[:], dtr[:], vcol, None, ALU.mult), r=[dtr, valid, rm], w=[dtm])

            def state_out(ST_f, dst):
                p = npf()
                for c in range(4):
                    K.op("pe", lambda e, c=c: e.transpose(p[:, c * 128:(c + 1) * 128], ST_f[:, c * 128:(c + 1) * 128], identf[:]),
                         r=[ST_f, identf], w=[p])
                K.op("act", lambda e: e.copy(stl[:].rearrange("p c n -> p (c n)"), p[:, :]), r=[p], w=[stl])
                K.dma("sp", dst.rearrange("(c p) n -> p c n", p=128), stl[:], r=[stl])

            def front(j, xsrc, full, parts="ab"):
                if "a" in parts:
                    xt = xts[j % 2]
                    K.dma("sp", xt[:], xsrc, w=[xt])
                    norm_T(wk, xt, 0, hT, hT[:, :, :])
                if "b" not in parts:
                    return None, None
                for cg in range(2):
                    ps = npf()
                    nw = 512 if (full or j == NPRE - 1 or cg == 0) else 256
                    proj(ps, nw, hT, hT[:, :, :], wz, 512 + cg * 512)
                    K.op("dve", lambda e, ps=ps, cg=cg, nw=nw: e.tensor_copy(xbc[:, cg * 512:cg * 512 + nw], ps[:, 0:nw]), r=[ps], w=[xbc])
                pd = npf()
                proj(pd, 8, hT, hT[:, :, :], wz, 1536)
                pz = None
                if full:
                    pz = npf()
                    proj(pz, 512, hT, hT[:, :, :], wz, 0)
                    K.op("act", lambda e: e.activation(zs[:], pz[:, :], AF.Silu), r=[pz], w=[zs])
                return pd, pz

            def bind(j):
                nonlocal hT, xbc, xc, dtr, dtm, zs
                hT, xbc, xc = hTs_[j % 2], xbcs_[j % 3], xcs_[j % 3]
                dtr, dtm, zs = dtrs_[j % 5], dtms_[j % 5], zss_[j % 5]

            def stage_F1(j):
                bind(j)
                front(j, xcat[j * 128:(j + 1) * 128, :], j >= NPRE, parts="a")

            def stage_F2(j):
                bind(j)
                full = j >= NPRE
                pd, pz = front(j, xcat[j * 128:(j + 1) * 128, :], full, parts="b")
                pd_ap = pd[:, 0:8]
                K.op("dve", lambda e: e.tensor_tensor(dtr[:], pd_ap, sm[:, 0, :], ALU.add), r=[pd, sm], w=[dtr])
                K.op("act", lambda e: e.activation(dtr[:], dtr[:], AF.Exp), r=[dtr], w=[dtr])
                K.op("act", lambda e: e.activation(dtr[:], dtr[:], AF.Ln, bias=1.0), r=[dtr], w=[dtr])
                K.op("dve", lambda e: e.tensor_scalar(dtm[:], dtr[:], valid[:, j:j + 1], None, ALU.mult), r=[dtr, valid], w=[dtm])

            def stage_C(j):
                bind(j)
                conv_dt(j, xw[j % 2], xw[(j + 1) % 2], sh, 3, xbc, j >= NPRE)
                if j == NT - 1:
                    K.dma("sp", convp_o, xbc[125:128, :], r=[xbc])

            def stage_S(j, part):
                bind(j)
                full = j >= NPRE
                ssd_chunk(full, dtm, STf, STb, parts=part)
                if full and part == "c":
                    ssd_post(ysum, None, (j - NPRE) * 128)

            jl = list(range(NT)) if not os.environ.get("KFAST") else list(range(46, 52))
            nj = len(jl)
            stage_F1(jl[0])
            LC, LS = 1, 3
            for it in range(nj + LS):
                if it % 4 == 0:
                    late_dmas(1)
                if it + 1 < nj:
                    stage_F1(jl[it + 1])
                if it >= LS:
                    stage_S(jl[it - LS], "a")
                if LC <= it < nj + LC:
                    stage_C(jl[it - LC])
                if it >= LS:
                    stage_S(jl[it - LS], "b")
                if it < nj:
                    stage_F2(jl[it])
                if it >= LS:
                    stage_S(jl[it - LS], "c")
            late_dmas()
            bind(0)
            state_out(STf, ssmp_o)

            if LEVEL >= 5:
                pd, pz = front(ST, xsin, True)
                pd_ap = pd[:, 0:8]
                K.op("dve", lambda e: e.tensor_tensor(dtr[:], pd_ap, sm[:, 0, :], ALU.add), r=[pd, sm], w=[dtr])
                K.op("act", lambda e: e.activation(dtr[:], dtr[:], AF.Exp), r=[dtr], w=[dtr])
                K.op("act", lambda e: e.activation(dtr[:], dtr[:], AF.Ln, bias=1.0), r=[dtr], w=[dtr])
                K.op("dve", lambda e: e.memset(X7[:], 0.0), w=[X7])
                for bl in range(4):
                    K.dma("sp", X7[7 * bl:7 * bl + 3, :], sconv[bl], w=[X7])
                    K.dma("sp", X7[7 * bl + 3:7 * bl + 7, :], xbc[4 * bl:4 * bl + 4, :], r=[xbc], w=[X7])
                for bl in range(4):
                    K.dma("sp", convs_o[bl], X7[7 * bl + 4:7 * bl + 7, :], r=[X7])
                conv_dt(ST, xw[0], xw[1], shs, 0, X7, True)
                K.op("dve", lambda e: e.memset(yS[:], 0.0), w=[yS])
                STs = K.sb(e2, [128, 512], F32)
                STsb = K.sb(e2, [128, 512], BF16)
                for bl in range(4):
                    K.dma("sp", stl[:], sssm[bl].rearrange("(c p) n -> p c n", p=128), w=[stl])
                    p = npf()
                    for c in range(4):
                        K.op("pe", lambda e, c=c: e.transpose(p[:, c * 128:(c + 1) * 128], stl[:, c, :], identf[:]),
                             r=[stl, identf], w=[p])
                    K.op("act", lambda e: e.copy(STs[:], p[:, :]), r=[p], w=[STs])
                    K.op("dve", lambda e: e.tensor_copy(STsb[:], STs[:]), r=[STs], w=[STsb])
                    K.op("dve", lambda e, bl=bl: e.tensor_scalar(dtm[:], dtr[:], rm[:, bl:bl + 1], None, ALU.mult),
                         r=[dtr, rm], w=[dtm])
                    ssd_chunk(True, dtm, STs, STsb)
                    K.op("dve", lambda e, bl=bl: e.scalar_tensor_tensor(yS[:], ysum[:], rm[:, bl:bl + 1], yS[:], ALU.mult, ALU.add),
                         r=[ysum, rm, yS], w=[yS])
                    state_out(STs, ssms_o[bl])
                ssd_post(yS, None, NOWN * 128)
            K.barrier()

        gfin = cload(bc_rows(gfin_d, D), [128, D])
        KTm_s = K.sb(es, [128, 8, 256], BF16, "KTm_s")
        Vm_s = K.sb(es, [128, 2, D], BF16, "Vm_s")
        wo = wload(es, wb_out, 8, D, name="wo")
        wq = wload(es, wb_xq, 8, D, name="wq")
        wxo = wload(es, wb_xo, 8, D, name="wxo")
        wup = [K.sb(es, [128, 8, 512], BF16, "wup") for _ in range(2)]
        wdn = [K.sb(es, [128, 4, D], BF16, "wdn") for _ in range(2)]
        upv = wb_up.rearrange("(kc p) n -> p kc n", p=128)
        dnv = wb_down.rearrange("(fc p) n -> p fc n", p=128)

        def load_up(ch):
            W = wup[ch % 2]
            K.dma("sp", W[:, 0:4, :], upv[:, 0:4, ch * 512:(ch + 1) * 512], r=[wres[wb_up.name]], w=[W])
            K.dma("sp", W[:, 4:8, :], upv[:, 4:8, ch * 512:(ch + 1) * 512], r=[wres[wb_up.name]], w=[W])

        def load_dn(ch):
            W = wdn[ch % 2]
            K.dma("sp", W[:, :, :], dnv[:, ch * 4:ch * 4 + 4, :], r=[wres[wb_down.name]], w=[W])

        groups = [(g * 4, 4) for g in range(4)] + [(NOWN, 1)]
        if LEVEL < 6:
            groups = []
        elif LEVEL == 6:
            groups = groups[:1]
        elif LEVEL == 7:
            groups = [groups[0], groups[4]]
        for gi_, (t0, ntl) in enumerate(groups):
            n = ntl * 128
            is_s = ntl == 1
            with ExitStack() as eg:
                resid = [K.sb(eg, [128, D], F32, "resid") for _ in range(ntl)]
                hmT = K.sb(eg, [128, 8, n], BF16, "hmT")
                with ExitStack() as e2:
                    wk = dict(junk=K.sb(e2, [128, D], F32), ss=K.sb(e2, [128, 1], F32), rstd=K.sb(e2, [128, 1], F32),
                              xn=K.sb(e2, [128, D], BF16), kb=(K.sb(e2, [128, 2, D], BF16) if is_s else None),
                              xP=[K.sb(e2, [128, 512], BF16, "xP") for _ in range(4)],
                              xrec=[K.sb(e2, [128, 512], F32, "xrec") for _ in range(2)])
                    load_up(0)
                    load_up(1)
                    load_dn(0)
                    load_dn(1)
                    hxT = K.sb(e2, [128, 8, n], BF16)
                    qT = K.sb(e2, [128, 8, n], BF16)
                    OT = K.sb(e2, [128, 8, n], BF16)
                    for tl in range(ntl):
                        src = xsin if is_s else xcat[(NPRE + t0 + tl) * 128:(NPRE + t0 + tl + 1) * 128, :]
                        K.dma("sp", resid[tl][:], src, w=[resid[tl]])
                    for tl in range(ntl):
                        c0 = (t0 + tl) * 128
                        for cg in range(2):
                            ps = npf()
                            for kc in range(8):
                                srcT = attT if kc < 4 else yT
                                K.op("pe", lambda e, kc=kc, cg=cg, ps=ps, srcT=srcT: e.matmul(
                                    ps[:, :], srcT[:, kc % 4, c0:c0 + 128], wo[:, kc, cg * 512:(cg + 1) * 512],
                                    start=(kc == 0), stop=(kc == 7)), r=[srcT, wo], w=[ps])
                            K.op("dve", lambda e, cg=cg, ps=ps, tl=tl: e.tensor_tensor(
                                resid[tl][:, cg * 512:(cg + 1) * 512], ps[:, :], resid[tl][:, cg * 512:(cg + 1) * 512], ALU.add),
                                 r=[ps, resid[tl]], w=[resid[tl]])
                    for tl in range(ntl):
                        norm_T(wk, resid[tl], 1, hxT, hxT[:, :, tl * 128:(tl + 1) * 128])
                    for fc in range(8):
                        ps = npf()
                        for kc in range(8):
                            K.op("pe", lambda e, kc=kc, fc=fc, ps=ps: e.matmul(
                                ps[:, 0:n], wq[:, kc, fc * 128:(fc + 1) * 128], hxT[:, kc, :],
                                start=(kc == 0), stop=(kc == 7)), r=[wq, hxT], w=[ps])
                        K.op("act", lambda e, fc=fc, ps=ps: e.mul(qT[:, fc, :], ps[:, 0:n], 1.0 / 16), r=[ps], w=[qT])
                    if not is_s:
                        xattn(wk, qT, qT[:, :, :], n, KTm_p, Vm_p, OT, OT[:, :, :])
                    else:
                        K.op("dve", lambda e: e.memset(OT[:], 0.0), w=[OT])
                        ktoks = [K.sb(e2, [128, 2, D], F32) for _ in range(2)]
                        vtoks = [K.sb(e2, [128, 2, D], F32) for _ in range(2)]
                        KTm2 = [KTm_s, KTm_s]
                        Vm2 = [Vm_s, Vm_s]

                        def mload(bl):
                            K.dma("sp", ktoks[bl % 2][:], cmk[bl].rearrange("(m p) n -> p m n", p=128), w=[ktoks[bl % 2]])
                            K.dma("sp", vtoks[bl % 2][:], cmv[bl].rearrange("(m p) n -> p m n", p=128), w=[vtoks[bl % 2]])

                        mload(0)
                        mload(1)
                        for bl in range(4):
                            mem_prepare(e2, wk, ktoks[bl % 2], vtoks[bl % 2], KTm2[bl % 2], Vm2[bl % 2])
                            if bl + 2 < 4:
                                mload(bl + 2)
                            xattn(wk, qT, qT[:, :, 4 * bl:4 * bl + 4], 4, KTm2[bl % 2], Vm2[bl % 2], OT, OT[:, :, 4 * bl:4 * bl + 4])
                    for tl in range(ntl):
                        for cg in range(2):
                            ps = npf()
                            for kc in range(8):
                                K.op("pe", lambda e, kc=kc, cg=cg, ps=ps, tl=tl: e.matmul(
                                    ps[:, :], OT[:, kc, tl * 128:(tl + 1) * 128], wxo[:, kc, cg * 512:(cg + 1) * 512],
                                    start=(kc == 0), stop=(kc == 7)), r=[OT, wxo], w=[ps])
                            K.op("dve", lambda e, cg=cg, ps=ps, tl=tl: e.tensor_tensor(
                                resid[tl][:, cg * 512:(cg + 1) * 512], ps[:, :], resid[tl][:, cg * 512:(cg + 1) * 512], ALU.add),
                                 r=[ps, resid[tl]], w=[resid[tl]])
                        norm_T(wk, resid[tl], 2, hmT, hmT[:, :, tl * 128:(tl + 1) * 128])
                    K.barrier()
                with ExitStack() as e2:
                    wk = dict(junk=K.sb(e2, [128, D], F32), ss=K.sb(e2, [128, 1], F32), rstd=K.sb(e2, [128, 1], F32),
                              xn=K.sb(e2, [128, D], BF16))
                    hidT = K.sb(e2, [128, 32, n], BF16)
                    yt = wk["junk"]
                    rls = [K.sb(e2, [128, 512], F32, "rl") for _ in range(2)]
                    for ch in range(8):
                        W = wup[ch % 2]
                        for f4 in range(4):
                            fc = ch * 4 + f4
                            ps = npf()
                            for kc in range(8):
                                K.op("pe", lambda e, kc=kc, f4=f4, ps=ps, W=W: e.matmul(
                                    ps[:, 0:n], W[:, kc, f4 * 128:(f4 + 1) * 128], hmT[:, kc, :],
                                    start=(kc == 0), stop=(kc == 7)), r=[W, hmT], w=[ps])
                            rl = rls[fc % 2]
                            K.op("act", lambda e, ps=ps, rl=rl: e.activation(rl[:, 0:n], ps[:, 0:n], AF.Relu), r=[ps], w=[rl])
                            if fc % 2:
                                K.op("act", lambda e, fc=fc, rl=rl: e.activation(hidT[:, fc, :], rl[:, 0:n], AF.Square), r=[rl], w=[hidT])
                            else:
                                K.op("dve", lambda e, fc=fc, rl=rl: e.tensor_tensor(
                                    hidT[:, fc, :], rl[:, 0:n], rl[:, 0:n], ALU.mult), r=[rl], w=[hidT])
                        if ch + 2 < 8:
                            load_up(ch + 2)
                    for ch in range(8):
                        W = wdn[ch % 2]
                        for tl in range(ntl):
                            for cg in range(2):
                                ps = npf()
                                for f4 in range(4):
                                    K.op("pe", lambda e, f4=f4, cg=cg, ps=ps, tl=tl, W=W, ch=ch: e.matmul(
                                        ps[:, :], hidT[:, ch * 4 + f4, tl * 128:(tl + 1) * 128], W[:, f4, cg * 512:(cg + 1) * 512],
                                        start=(f4 == 0), stop=(f4 == 3)), r=[hidT, W], w=[ps])
                                K.op("dve", lambda e, cg=cg, ps=ps, tl=tl: e.tensor_tensor(
                                    resid[tl][:, cg * 512:(cg + 1) * 512], ps[:, :], resid[tl][:, cg * 512:(cg + 1) * 512], ALU.add),
                                     r=[ps, resid[tl]], w=[resid[tl]])
                        if ch + 2 < 8:
                            load_dn(ch + 2)
                    junk, ss, rstd = wk["junk"], wk["ss"], wk["rstd"]
                    for tl in range(ntl):
                        x = resid[tl]
                        K.op("act", lambda e, x=x: e.activation(junk[:], x[:], AF.Square, accum_out=ss[:, 0:1]), r=[x], w=[junk, ss])
                        K.op("act", lambda e: e.activation(rstd[:], ss[:], AF.Ln, scale=1.0 / D, bias=epsT[:, 0:1]), r=[ss, epsT], w=[rstd])
                        K.op("act", lambda e: e.activation(rstd[:], rstd[:], AF.Exp, scale=-0.5), r=[rstd], w=[rstd])
                        K.op("dve", lambda e, x=x: e.scalar_tensor_tensor(yt[:], x[:], rstd[:, 0:1], gfin[:], ALU.mult, ALU.mult),
                             r=[x, rstd, gfin], w=[yt])
                        dst = ys_o if is_s else y_o[(t0 + tl) * 128:(t0 + tl + 1) * 128, :]
                        K.dma("sp", dst, yt[:], r=[yt])
                    K.barrier()
        K.barrier(full=True)
    return nc


def _mult(delta):
    d = delta
    c = ((d >= 0) & (d <= 128)).astype(np.float32)
    c += ((d >= 0) & (d % 4 == 0) & (d <= 512)).astype(np.float32)
    c += ((d >= 0) & (d % 16 == 0) & (d <= 2048)).astype(np.float32)
    return c


def _consts():
    i = np.arange(128)
    c = {}
    c["c_ident"] = np.eye(128, dtype=np.float32)
    c["c_tri"] = (i[:, None] <= i[None, :]).astype(np.float32)
    c["c_mneg"] = np.where(i[:, None] <= i[None, :], 0.0, NEG).astype(np.float32)
    cm = np.zeros((128, 17 * 128), np.float32)
    for o in range(17):
        cm[:, o * 128:(o + 1) * 128] = _mult(128 * o + i[None, :] - i[:, None])
    c["c_cm"] = cm
    sh = np.zeros((128, 7 * 128), np.float32)
    for t in range(4):
        m = (i[:, None] == i[None, :] - 3 + t)
        sh[:, t * 128:(t + 1) * 128] = m
    for t in range(3):
        m = (i[:, None] - 128 == i[None, :] - 3 + t)
        sh[:, (4 + t) * 128:(5 + t) * 128] = m
    c["c_sh"] = sh
    shs = np.zeros((128, 4 * 128), np.float32)
    for t in range(4):
        for bl in range(4):
            for tt in range(4):
                shs[7 * bl + tt + t, t * 128 + 4 * bl + tt] = 1.0
    c["c_shs"] = shs
    cs = np.zeros((128, 64 * 16), np.float32)
    for bl in range(4):
        for kt in range(16):
            blk = np.zeros((128, 16), np.float32)
            for t in range(4):
                blk[:, 4 * bl + t] = _mult(2048 + t - 128 * kt - i)
            cs[:, (bl * 16 + kt) * 16:(bl * 16 + kt + 1) * 16] = blk
    c["c_cs"] = cs
    cn = np.zeros((128, 16), np.float32)
    for bl in range(4):
        for tk in range(4):
            for tq in range(4):
                if tq >= tk:
                    cn[4 * bl + tk, 4 * bl + tq] = _mult(np.array(tq - tk))
    c["c_cn"] = cn
    cc = np.zeros((128, 4 * 16), np.float32)
    for bl in range(4):
        for t in range(4):
            cc[32 * t:32 * t + 32, bl * 16 + 4 * bl + t] = 1.0
    c["c_cc"] = cc
    rm = np.zeros((128, 4), np.float32)
    for bl in range(4):
        rm[4 * bl:4 * bl + 4, bl] = 1.0
    c["c_rm"] = rm
    return c


_NC = None


def kernel(x_prompt, x_sample, cache_win_k, cache_win_v, state_conv, state_ssm,
           cache_mem_k, cache_mem_v, mem_prompt,
           g_mix, w_in, conv_w, conv_b, dt_bias, a_log, d_skip, g_ssm, w_out,
           g_xatt, g_mem, w_xq, w_mk, w_mv, w_xo, g_mlp, w_up, w_down, g_final):
    global _NC
    f = lambda a: np.ascontiguousarray(np.asarray(a, dtype=np.float32))
    x_prompt, x_sample = f(x_prompt), f(x_sample)
    consts = _consts()
    gc = np.concatenate([f(g)[0].reshape(8, 128).T for g in (g_mix, g_xatt, g_mlp, g_mem)], axis=1)
    shared = dict(w_in=f(w_in)[0], w_out=f(w_out)[0], w_xq=f(w_xq)[0], w_mk=f(w_mk)[0], w_mv=f(w_mv)[0],
                  w_xo=f(w_xo)[0], w_up=f(w_up)[0], w_down=f(w_down)[0], gcols=np.ascontiguousarray(gc),
                  gfin=f(g_final).reshape(1, D), convw=f(conv_w)[0], convb=f(conv_b)[0].reshape(1, D),
                  small=np.stack([f(dt_bias)[0], f(a_log)[0], f(d_skip)[0]]), gssm=f(g_ssm)[0].reshape(1, 512))
    shared.update(consts)
    half = 32
    inv = (10000.0 ** (-np.arange(half, dtype=np.float32) * 2.0 / 64)).astype(np.float32)
    in_maps = []
    for core in range(8):
        b, c = core // 4, core % 4
        start = 2048 * c
        p0 = start - NPRE * 128
        xc = np.zeros((NT * 128, D), np.float32)
        lo = max(p0, 0)
        xc[lo - p0:] = x_prompt[b, lo:start + 2048]
        pos = np.zeros((128, 65), np.float32)
        pos[:, :NT] = p0 + 128 * np.arange(NT)[None, :] + np.arange(128)[:, None]
        pos[:, 64] = -1.0
        pos[:16, 64] = 8192 + (np.arange(16) % 4)
        pp = np.maximum(pos, 0.0).T.astype(np.float32)
        ang = pp[:, :, None] * inv[None, None, :]
        rope = np.concatenate([np.cos(ang), np.sin(ang)], axis=-1).astype(np.float32)
        xs = np.zeros((128, D), np.float32)
        xs[:16] = x_sample[4 * core:4 * core + 4].reshape(16, D)
        m = dict(shared)
        m.update(xcat=xc, xs=xs, pos=pos, rope=np.ascontiguousarray(rope),
                 ck=f(cache_win_k)[0, 4 * core:4 * core + 4].reshape(4, 2048, 512),
                 cv=f(cache_win_v)[0, 4 * core:4 * core + 4].reshape(4, 2048, 512),
                 sconv=f(state_conv)[0, 4 * core:4 * core + 4],
                 sssm=f(state_ssm)[0, 4 * core:4 * core + 4].reshape(4, 512, 128),
                 cmk=f(cache_mem_k)[0, 4 * core:4 * core + 4].reshape(4, 256, D),
                 cmv=f(cache_mem_v)[0, 4 * core:4 * core + 4].reshape(4, 256, D),
                 mem=f(mem_prompt)[b])
        in_maps.append({k: np.ascontiguousarray(v) for k, v in m.items()})
    if _NC is None:
        _NC = build()
    import os
    ncores = int(os.environ.get("KCORES", "8"))
    if ncores < 8:
        c0 = int(os.environ.get("KCORE", "3"))
        res = run_bass_kernel_spmd(_NC, in_maps[c0:c0 + 1], core_ids=[0])
        return res.results[0]
    res = run_bass_kernel_spmd(_NC, in_maps, core_ids=list(range(8)))
    R = res.results
    y_prompt = np.stack([np.concatenate([R[4 * b + c]["y"] for c in range(4)], axis=0) for b in range(2)])
    y_sample = np.concatenate([R[k]["ys"][:16].reshape(4, 4, D) for k in range(8)], axis=0)
    wkp = np.stack([R[4 * b + 3]["wk"].reshape(2048, 8, 64) for b in range(2)])[None]
    wvp = np.stack([R[4 * b + 3]["wv"].reshape(2048, 8, 64) for b in range(2)])[None]
    cvp = np.stack([R[4 * b + 3]["convp"] for b in range(2)])[None]
    ssp = np.stack([R[4 * b + 3]["ssmp"].reshape(8, 64, 128) for b in range(2)])[None]
    mkp = np.stack([R[4 * b]["memk"].reshape(256, 4, 256) for b in range(2)])[None]
    mvp = np.stack([R[4 * b]["memv"].reshape(256, 4, 256) for b in range(2)])[None]
    wks = np.concatenate([R[k]["wks"].reshape(4, 2048, 8, 64) for k in range(8)], axis=0)[None]
    wvs = np.concatenate([R[k]["wvs"].reshape(4, 2048, 8, 64) for k in range(8)], axis=0)[None]
    cvs = np.concatenate([R[k]["convs"] for k in range(8)], axis=0)[None]
    sss = np.concatenate([R[k]["ssms"].reshape(4, 8, 64, 128) for k in range(8)], axis=0)[None]
    return tuple(np.ascontiguousarray(a.astype(np.float32)) for a in
                 (y_prompt, y_sample, wkp, wvp, cvp, ssp, mkp, mvp, wks, wvs, cvs, sss))
```

```python
import numpy as np
from contextlib import ExitStack
import concourse.bass as bass
import concourse.mybir as mybir
from concourse.bass_utils import run_bass_kernel_spmd

F32 = mybir.dt.float32
BF16 = mybir.dt.bfloat16
ALU = mybir.AluOpType
AF = mybir.ActivationFunctionType
AX = mybir.AxisListType

D = 1024
NPRE = 48
NOWN = 16
NT = NPRE + NOWN
ST = 64
NEG = -30000.0
EPS = 1e-6
NDS = 56


class Res:
    __slots__ = ("w", "r")

    def __init__(self):
        self.w = None
        self.r = {}


class TT:
    def __init__(self, h, res=None):
        self.h = h
        self.res = res if res is not None else Res()

    def __getitem__(self, k):
        return self.h[k]


class KB:
    def __init__(self, nc, es):
        self.nc = nc
        self.es = es
        self.eng = {}
        for name, e in (("pe", nc.tensor), ("act", nc.scalar), ("dve", nc.vector),
                        ("pool", nc.gpsimd), ("sp", nc.sync)):
            sem = es.enter_context(nc.semaphore("s_" + name))
            self.eng[name] = dict(e=e, sem=sem, cnt=0, known={})
        self.ds = [dict(sem=es.enter_context(nc.semaphore("d%d" % i)), cnt=0) for i in range(NDS)]
        self.dn = {"sp": 0, "pool": 0}
        self.drange = {"sp": (0, 32), "pool": (32, NDS)}
        self.uid = 0

    def _sem(self, key):
        return self.eng[key]["sem"] if isinstance(key, str) else self.ds[key[1]]["sem"]

    def _wait(self, en, deps):
        E = self.eng[en]
        for key, val in deps:
            if val <= 0:
                continue
            if key == en and en == "pe":
                continue
            if E["known"].get(key, 0) >= val:
                continue
            E["e"].wait_ge(self._sem(key), val)
            E["known"][key] = val

    @staticmethod
    def _deps(r, w):
        deps = []
        for t in r:
            if t.res.w is not None:
                deps.append(t.res.w)
        for t in w:
            if t.res.w is not None:
                deps.append(t.res.w)
            deps.extend(t.res.r.items())
        return deps

    @staticmethod
    def _upd(tok, r, w):
        for t in r:
            t.res.r[tok[0]] = tok[1]
        for t in w:
            t.res.w = tok
            t.res.r = {}

    def op(self, en, fn, r=(), w=()):
        E = self.eng[en]
        self._wait(en, self._deps(r, w))
        ins = fn(E["e"])
        E["cnt"] += 1
        ins.then_inc(E["sem"], 1)
        self._upd((en, E["cnt"]), r, w)

    def dma(self, q, out, in_, r=(), w=()):
        E = self.eng[q]
        lo, hi = self.drange[q]
        i = lo + self.dn[q]
        self.dn[q] = (self.dn[q] + 1) % (hi - lo)
        d = self.ds[i]
        deps = self._deps(r, w)
        deps.append((("d", i), d["cnt"]))
        self._wait(q, deps)
        ins = E["e"].dma_start(out=out, in_=in_)
        d["cnt"] += 16
        ins.then_inc(d["sem"], 16)
        self._upd((("d", i), d["cnt"]), r, w)

    def barrier(self, full=False):
        allv = [(n, e["cnt"]) for n, e in self.eng.items() if n != "sp"]
        allv += [(("d", i), d["cnt"]) for i, d in enumerate(self.ds) if full or i < 32]
        for en in self.eng:
            E = self.eng[en]
            for key, val in allv:
                if val <= 0 or key == en:
                    continue
                if E["known"].get(key, 0) >= val:
                    continue
                E["e"].wait_ge(self._sem(key), val)
                E["known"][key] = val

    def sb(self, es, shape, dt, name=None):
        self.uid += 1
        return TT(es.enter_context(self.nc.sbuf_tensor("%s_%d" % (name or "t", self.uid), list(shape), dt)))


def bc_rows(dram_ap, n):
    return bass.AP(dram_ap.tensor, dram_ap.offset, [[0, 128], [1, n]])


def build():
    import os
    LEVEL = int(os.environ.get("KLEVEL", "99"))
    nc = bass.Bass("TRN2", target_bir_lowering=False)

    def din(name, shape, dt=F32):
        return nc.dram_tensor(name, list(shape), dt, kind="ExternalInput").ap()

    def dout(name, shape, dt=F32):
        return nc.dram_tensor(name, list(shape), dt, kind="ExternalOutput").ap()

    def dscr(name, shape, dt=BF16):
        return nc.dram_tensor(name, list(shape), dt, kind="Internal").ap()

    xcat = din("xcat", [NT * 128, D])
    xsin = din("xs", [128, D])
    posd = din("pos", [128, 65])
    roped = din("rope", [65, 128, 64])
    ck = din("ck", [4, 2048, 512])
    cv = din("cv", [4, 2048, 512])
    sconv = din("sconv", [4, 3, D])
    sssm = din("sssm", [4, 512, 128])
    cmk = din("cmk", [4, 256, D])
    cmv = din("cmv", [4, 256, D])
    memd = din("mem", [256, D])
    w_in = din("w_in", [D, 3080])
    w_out = din("w_out", [D, D])
    w_xq = din("w_xq", [D, D])
    w_mk = din("w_mk", [D, D])
    w_mv = din("w_mv", [D, D])
    w_xo = din("w_xo", [D, D])
    w_up = din("w_up", [D, 4096])
    w_down = din("w_down", [4096, D])
    gcols_d = din("gcols", [128, 32])
    gfin_d = din("gfin", [1, D])
    convw_d = din("convw", [4, D])
    convb_d = din("convb", [1, D])
    small_d = din("small", [3, 8])
    gssm_d = din("gssm", [1, 512])
    c_ident = din("c_ident", [128, 128])
    c_tri = din("c_tri", [128, 128])
    c_mneg = din("c_mneg", [128, 128])
    c_cm = din("c_cm", [128, 17 * 128])
    c_sh = din("c_sh", [128, 7 * 128])
    c_shs = din("c_shs", [128, 4 * 128])
    c_cs = din("c_cs", [128, 64 * 16])
    c_cn = din("c_cn", [128, 16])
    c_cc = din("c_cc", [128, 4 * 16])
    c_rm = din("c_rm", [128, 4])
    y_o = dout("y", [NOWN * 128, D])
    ys_o = dout("ys", [128, D])
    wk_o = dout("wk", [NOWN * 128, 512])
    wv_o = dout("wv", [NOWN * 128, 512])
    convp_o = dout("convp", [3, D])
    ssmp_o = dout("ssmp", [512, 128])
    memk_o = dout("memk", [256, D])
    memv_o = dout("memv", [256, D])
    wks_o = dout("wks", [4, 2048, 512])
    wvs_o = dout("wvs", [4, 2048, 512])
    convs_o = dout("convs", [4, 3, D])
    ssms_o = dout("ssms", [4, 512, 128])
    wb_in = dscr("wb_in", [D, 3080])
    wb_out = dscr("wb_out", [D, D])
    wb_xq = dscr("wb_xq", [D, D])
    wb_mk = dscr("wb_mk", [D, D])
    wb_mv = dscr("wb_mv", [D, D])
    wb_xo = dscr("wb_xo", [D, D])
    wb_up = dscr("wb_up", [D, 4096])
    wb_down = dscr("wb_down", [4096, D])

    with ExitStack() as es:
        K = KB(nc, es)
        wres = {}

        def convert(items):
            for src, dst, rows in items:
                if dst.name not in wres:
                    wres[dst.name] = TT(None)
                step = 512
                for r0 in range(0, rows, step):
                    K.dma("pool", dst[r0:r0 + step, :], src[r0:r0 + step, :], w=[wres[dst.name]])

        wres["in_a"] = TT(None)
        wres["in_b"] = TT(None)
        for r0 in range(0, D, 512):
            K.dma("pool", wb_in[r0:r0 + 512, 0:1536], w_in[r0:r0 + 512, 0:1536], w=[wres["in_a"]])
        convert([(w_mk, wb_mk, D), (w_mv, wb_mv, D)])
        for r0 in range(0, D, 512):
            K.dma("pool", wb_in[r0:r0 + 512, 1536:3080], w_in[r0:r0 + 512, 1536:3080], w=[wres["in_b"]])
        for _w in (wb_out, wb_xq, wb_xo, wb_up, wb_down):
            wres[_w.name] = TT(None)

        late_q = []
        for _src, _dst, _rows in ((w_out, wb_out, D), (w_xq, wb_xq, D), (w_xo, wb_xo, D), (w_up, wb_up, D),
                                  (w_down, wb_down, 4096)):
            for r0 in range(0, _rows, 512):
                late_q.append(lambda _src=_src, _dst=_dst, r0=r0: K.dma(
                    "pool", _dst[r0:r0 + 512, :], _src[r0:r0 + 512, :], w=[wres[_dst.name]]))
        for bl in range(4):
            late_q.append(lambda bl=bl: K.dma("pool", wks_o[bl, 0:2044, :], ck[bl, 4:2048, :]))
            late_q.append(lambda bl=bl: K.dma("pool", wvs_o[bl, 0:2044, :], cv[bl, 4:2048, :]))

        def late_dmas(n=None):
            k = len(late_q) if n is None else min(n, len(late_q))
            for _ in range(k):
                late_q.pop(0)()
        pf = [TT(es.enter_context(nc.psum_tensor("pf%d" % i, [128, 512], F32))) for i in range(6)]
        pb = [TT(es.enter_context(nc.psum_tensor("pb%d" % i, [128, 1024], BF16))) for i in range(2)]
        st_ = dict(pf=0, pb=0)

        def npf():
            st_["pf"] = (st_["pf"] + 1) % 4
            return pf[st_["pf"]]

        def npb():
            st_["pb"] = (st_["pb"] + 1) % 2
            return pb[st_["pb"]]

        def cload(src_ap, shape, dt=F32, q="sp", stk=None):
            stk = stk if stk is not None else es
            t = K.sb(stk, shape, F32, "cst")
            K.dma(q, t[:], src_ap, w=[t])
            if dt == F32:
                return t
            tb = K.sb(stk, shape, dt, "cstb")
            K.op("dve", lambda e: e.tensor_copy(tb[:], t[:]), r=[t], w=[tb])
            return tb

        def cload_bf(stk, specs):
            outs = [K.sb(stk, shape, BF16, "cbf") for _, shape in specs]
            with ExitStack() as tmp:
                for (src_ap, shape), o in zip(specs, outs):
                    t = K.sb(tmp, shape, F32, "cst")
                    K.dma("sp", t[:], src_ap, w=[t])
                    K.op("dve", lambda e, o=o, t=t: e.tensor_copy(o[:], t[:]), r=[t], w=[o])
                K.barrier()
            return outs

        identf = cload(c_ident, [128, 128])
        identb = K.sb(es, [128, 128], BF16, "identb")
        K.op("dve", lambda e: e.tensor_copy(identb[:], identf[:]), r=[identf], w=[identb])
        onesf = K.sb(es, [128, 128], F32, "onesf")
        K.op("dve", lambda e: e.memset(onesf[:], 1.0), w=[onesf])
        epsT = K.sb(es, [128, 1], F32, "epsT")
        K.op("dve", lambda e: e.memset(epsT[:], EPS), w=[epsT])
        onesb = K.sb(es, [128, 128], BF16, "onesb")
        K.op("dve", lambda e: e.memset(onesb[:], 1.0), w=[onesb])
        gcols = cload(gcols_d, [128, 32])
        pos = cload(posd, [128, 65])
        valid = K.sb(es, [128, 65], F32, "valid")
        K.op("dve", lambda e: e.tensor_single_scalar(valid[:], pos[:], 0.0, ALU.is_ge), r=[pos], w=[valid])
        kbias = K.sb(es, [128, 65], F32, "kbias")
        K.op("dve", lambda e: e.tensor_scalar(kbias[:], valid[:], -1.0, -NEG, ALU.add, ALU.mult),
             r=[valid], w=[kbias])
        attT = K.sb(es, [128, 4, (NOWN + 1) * 128], BF16, "attT")
        yT = K.sb(es, [128, 4, (NOWN + 1) * 128], BF16, "yT")
        K.barrier()

        def wload(es2, src_b, kchunks, ncols, col0=0, name="w", q="sp", res=None):
            t = K.sb(es2, [128, kchunks, ncols], BF16, name)
            v = src_b.rearrange("(kc p) n -> p kc n", p=128)
            for kc0 in range(0, kchunks, 4):
                K.dma(q, t[:, kc0:kc0 + 4, :], v[:, kc0:kc0 + 4, col0:col0 + ncols], r=[wres[res or src_b.name]], w=[t])
            return t

        def norm_T(wk, x, gi, dst, dst_ap):
            junk, ss, rstd, xn = wk["junk"], wk["ss"], wk["rstd"], wk["xn"]
            K.op("act", lambda e: e.activation(junk[:], x[:], AF.Square, accum_out=ss[:, 0:1]), r=[x], w=[junk, ss])
            K.op("act", lambda e: e.activation(rstd[:], ss[:], AF.Ln, scale=1.0 / D, bias=epsT[:, 0:1]), r=[ss, epsT], w=[rstd])
            K.op("act", lambda e: e.activation(rstd[:], rstd[:], AF.Exp, scale=-0.5), r=[rstd], w=[rstd])
            K.op("act", lambda e: e.activation(xn[:], x[:], AF.Copy, scale=rstd[:, 0:1]), r=[x, rstd], w=[xn])
            p = npb()
            for j in range(8):
                K.op("pe", lambda e, j=j: e.transpose(p[:, j * 128:(j + 1) * 128], xn[:, j * 128:(j + 1) * 128], identb[:]),
                     r=[xn, identb], w=[p])
            g = gcols[:, gi * 8:(gi + 1) * 8].unsqueeze(2).to_broadcast([128, 8, 128])
            K.op("dve", lambda e: e.tensor_tensor(dst_ap, p[:, :].rearrange("p (k t) -> p k t", t=128), g, ALU.mult),
                 r=[p, gcols], w=[dst])
            return rstd

        def proj(ps, ncols, hT, hT_ap, W, col0):
            for kc in range(8):
                K.op("pe", lambda e, kc=kc: e.matmul(ps[:, 0:ncols], hT_ap[:, kc, :], W[:, kc, col0:col0 + ncols],
                                                       start=(kc == 0), stop=(kc == 7)),
                     r=[hT, W], w=[ps])

        def transpose4(src, src_ap_fn, dst, dst_ap):
            p = npb()
            for j in range(4):
                K.op("pe", lambda e, j=j: e.transpose(p[:, j * 128:(j + 1) * 128], src_ap_fn(j), identb[:]),
                     r=[src, identb], w=[p])
            K.op("act", lambda e: e.copy(dst_ap, p[:, 0:512]), r=[p], w=[dst])

        def mem_prepare(es2, wk, ktok, vtok, KTm, Vm):
            kb = wk["kb"]
            K.op("act", lambda e: e.copy(kb[:], ktok[:]), r=[ktok], w=[kb])
            K.op("pool", lambda e: e.tensor_copy(Vm[:], vtok[:]), r=[vtok], w=[Vm])
            for mb in range(2):
                for half in range(2):
                    p = npb()
                    for j in range(4):
                        c = half * 4 + j
                        K.op("pe", lambda e, j=j, c=c, mb=mb: e.transpose(p[:, j * 128:(j + 1) * 128],
                                                                         kb[:, mb, c * 128:(c + 1) * 128], identb[:]),
                             r=[kb, identb], w=[p])
                    K.op("dve", lambda e, mb=mb, half=half: e.tensor_copy(
                        KTm[:, half * 4:half * 4 + 4, mb * 128:(mb + 1) * 128],
                        p[:, 0:512].rearrange("p (c m) -> p c m", m=128)), r=[p], w=[KTm])

        def xattn(wk, qT, qT_ap, n, KTm, Vm, OT, OT_ap):
            def st1(h):
                Ps = []
                for mb in range(2):
                    s = npf()
                    for dc in range(2):
                        K.op("pe", lambda e, dc=dc, mb=mb, s=s: e.matmul(
                            s[:, 0:n], KTm[:, 2 * h + dc, mb * 128:(mb + 1) * 128], qT_ap[:, 2 * h + dc, :],
                            start=(dc == 0), stop=(dc == 1)), r=[KTm, qT], w=[s])
                    P = wk["xP"][2 * (h % 2) + mb]
                    K.op("act", lambda e, s=s, P=P: e.activation(P[:, 0:n], s[:, 0:n], AF.Exp), r=[s], w=[P])
                    Ps.append(P)
                return Ps

            def st2(h, Ps):
                dn = npf()
                for mb in range(2):
                    K.op("pe", lambda e, mb=mb: e.matmul(dn[:, 0:n], onesb[:], Ps[mb][:, 0:n],
                                                         start=(mb == 0), stop=(mb == 1)), r=[onesb, Ps[mb]], w=[dn])
                rec = wk["xrec"][h % 2]
                K.op("dve", lambda e: e.reciprocal(rec[:, 0:n], dn[:, 0:n]), r=[dn], w=[rec])
                for dc in range(2):
                    o = npf()
                    for mb in range(2):
                        K.op("pe", lambda e, mb=mb, dc=dc, o=o: e.matmul(
                            o[:, 0:n], Vm[:, mb, h * 256 + dc * 128:h * 256 + dc * 128 + 128], Ps[mb][:, 0:n],
                            start=(mb == 0), stop=(mb == 1)), r=[Vm, Ps[mb]], w=[o])
                    K.op("dve", lambda e, dc=dc, o=o: e.tensor_tensor(OT_ap[:, 2 * h + dc, :], o[:, 0:n], rec[:, 0:n], ALU.mult),
                         r=[o, rec], w=[OT])

            prev = None
            for h in range(4):
                Ps = st1(h)
                if prev is not None:
                    st2(*prev)
                prev = (h, Ps)
            st2(*prev)

        KTm_p = K.sb(es, [128, 8, 256], BF16, "KTm_p")
        Vm_p = K.sb(es, [128, 2, D], BF16, "Vm_p")

        with ExitStack() as e2:
          if LEVEL >= 2:
            NB = 3
            LAG = NB - 1
            NR = 18
            junk_ = K.sb(e2, [128, D], F32)
            wks_ = [dict(junk=junk_, ss=K.sb(e2, [128, 1], F32), rstd=K.sb(e2, [128, 1], F32),
                         xn=K.sb(e2, [128, D], BF16)) for _ in range(2)]
            cm, ccs, ccn, ccc = cload_bf(e2, [(c_cm, [128, 17 * 128]), (c_cs, [128, 64 * 16]), (c_cn, [128, 16]),
                                              (c_cc, [128, 4 * 16])])
            wqkv = wload(e2, wb_in, 8, 1536, name="wqkv", res="in_a")
            KT = [K.sb(e2, [128, 4, 128], BF16, "KT") for _ in range(NR)]
            VX = [K.sb(e2, [128, 8, 72], BF16, "VX") for _ in range(NR)]
            for v_ in VX:
                K.op("pool", lambda e, v_=v_: e.memset(v_[:], 1.0), w=[v_])
            xts = [K.sb(e2, [128, D], F32, "xt") for _ in range(2)]
            hTs = [K.sb(e2, [128, 8, 128], BF16) for _ in range(2)]
            css = [K.sb(e2, [128, 64], F32) for _ in range(2)]
            qks = [K.sb(e2, [128, 16, 64], F32) for _ in range(2)]
            ros = [K.sb(e2, [128, 16, 64], F32) for _ in range(2)]
            tas = [K.sb(e2, [128, 16, 32], F32)] * 2
            tbs = [K.sb(e2, [128, 16, 32], F32)] * 2
            qkbs = [K.sb(e2, [128, 16, 64], BF16) for _ in range(2)]
            QTs = [K.sb(e2, [128, 4, 128], BF16) for _ in range(2)]
            vfs = [K.sb(e2, [128, 512], F32) for _ in range(2)]
            Eb = [[K.sb(e2, [128, 512], BF16, "E") for _ in range(2)] for _ in range(NB)]
            Pb = [[K.sb(e2, [128, 512], BF16, "P") for _ in range(2)] for _ in range(NB)]
            rec8 = K.sb(e2, [128, 8], F32)
            att = K.sb(e2, [128, 512], BF16)
            Oacc = [pf[4], pf[5]]

            def qkv_tile(j, xsrc, full, slot, parts="ab"):
                par = j % 2
                xt, hT, cs, qk, ro, ta, tb, qkb, QT, vf = (xts[par], hTs[par], css[par], qks[par], ros[par], tas[par],
                                                           tbs[par], qkbs[par], QTs[par], vfs[par])
                if "a" in parts:
                    K.dma("sp", xt[:], xsrc, w=[xt])
                    K.dma("sp", cs[:], roped[j], w=[cs])
                    norm_T(wks_[par], xt, 0, hT, hT[:, :, :])
                if "b" not in parts:
                    return None, None
                for idx, c0 in ((0, 0), (1, 512)):
                    if idx == 0 and not full:
                        continue
                    ps = npf()
                    proj(ps, 512, hT, hT[:, :, :], wqkv, c0)
                    K.op("act", lambda e, ps=ps, idx=idx: e.copy(
                        qk[:, idx * 8:(idx + 1) * 8, :], ps[:, :].rearrange("p (h d) -> p h d", d=64)), r=[ps], w=[qk])
                ps = npf()
                proj(ps, 512, hT, hT[:, :, :], wqkv, 1024)
                K.op("act", lambda e: e.copy(vf[:], ps[:, :]), r=[ps], w=[vf])
                K.op("dve", lambda e: e.tensor_copy(VX[slot][:, :, 0:64], vf[:].rearrange("p (h d) -> p h d", d=64)),
                     r=[vf], w=[VX[slot]])
                h0 = 0 if full else 8
                nh = 16 - h0
                cosb = cs[:, 0:32].unsqueeze(1).to_broadcast([128, nh, 32])
                sinb = cs[:, 32:64].unsqueeze(1).to_broadcast([128, nh, 32])
                t1 = qk[:, h0:16, 0:32]
                t2 = qk[:, h0:16, 32:64]
                K.op("dve", lambda e: e.tensor_tensor(ta[:, h0:16, :], t1, cosb, ALU.mult), r=[qk, cs], w=[ta])
                K.op("dve", lambda e: e.tensor_tensor(tb[:, h0:16, :], t2, sinb, ALU.mult), r=[qk, cs], w=[tb])
                K.op("dve", lambda e: e.tensor_tensor(ro[:, h0:16, 0:32], ta[:, h0:16, :], tb[:, h0:16, :], ALU.subtract),
                     r=[ta, tb], w=[ro])
                K.op("dve", lambda e: e.tensor_tensor(ta[:, h0:16, :], t2, cosb, ALU.mult), r=[qk, cs], w=[ta])
                K.op("dve", lambda e: e.tensor_tensor(tb[:, h0:16, :], t1, sinb, ALU.mult), r=[qk, cs], w=[tb])
                K.op("dve", lambda e: e.tensor_tensor(ro[:, h0:16, 32:64], ta[:, h0:16, :], tb[:, h0:16, :], ALU.add),
                     r=[ta, tb], w=[ro])
                K.op("act", lambda e: e.copy(qkb[:, 8:16, :], ro[:, 8:16, :]), r=[ro], w=[qkb])
                transpose4(qkb, lambda jj: qkb[:, 8 + 2 * jj:10 + 2 * jj, :].rearrange("p h d -> p (h d)"),
                           KT[slot], KT[slot][:, :, :].rearrange("p c t -> p (c t)"))
                if full:
                    K.op("act", lambda e: e.mul(qkb[:, 0:8, :], ro[:, 0:8, :], 0.125), r=[ro], w=[qkb])
                    transpose4(qkb, lambda jj: qkb[:, 2 * jj:2 * jj + 2, :].rearrange("p h d -> p (h d)"),
                               QT, QT[:, :, :].rearrange("p c t -> p (c t)"))
                return ro, vf

            def unit_front(u, QT, KTt, nq, mask_ap_fn, bias_ap):
                for hg in range(2):
                    s = npf()
                    for hh in range(4):
                        K.op("pe", lambda e, hh=hh, hg=hg, s=s: e.matmul(
                            s[:, hh * nq:(hh + 1) * nq], KTt[hg * 64:hg * 64 + 64, hh, :],
                            QT[hg * 64:hg * 64 + 64, hh, 0:nq], start=True, stop=True), r=[KTt, QT], w=[s])
                    E = Eb[u % NB][hg]
                    if bias_ap is not None:
                        K.op("act", lambda e, E=E, s=s: e.activation(E[:, 0:4 * nq], s[:, 0:4 * nq], AF.Exp, bias=bias_ap),
                             r=[s, kbias], w=[E])
                    else:
                        K.op("act", lambda e, E=E, s=s: e.activation(E[:, 0:4 * nq], s[:, 0:4 * nq], AF.Exp), r=[s], w=[E])
                    P = Pb[u % NB][hg]
                    K.op("dve", lambda e, E=E, P=P: e.tensor_tensor(
                        P[:, 0:4 * nq].rearrange("p (h q) -> p h q", q=nq),
                        E[:, 0:4 * nq].rearrange("p (h q) -> p h q", q=nq), mask_ap_fn(), ALU.mult),
                         r=[E, cm, ccs, ccn, ccc], w=[P])

            def unit_back(u, VXt, nq, first, last):
                for hg in range(2):
                    P = Pb[u % NB][hg]
                    for hh in range(4):
                        h = 2 * hh + hg
                        K.op("pe", lambda e, hh=hh, h=h, hg=hg, P=P: e.matmul(
                            Oacc[hg][0:nq, hh * 128:hh * 128 + 65], P[:, hh * nq:(hh + 1) * nq], VXt[:, h, 0:65],
                            start=(first and hh == 0), stop=(last and hh == 3)), r=[P, VXt], w=[Oacc[hg]])

            def run_units(units, QT, nq):
                n = len(units)
                for i in range(n + LAG):
                    if i < n:
                        prep, ktf, vxf, mfn, bias = units[i]
                        if prep is not None:
                            prep()
                        unit_front(i, QT, ktf(), nq, mfn, bias)
                    k = i - LAG
                    if k >= 0:
                        unit_back(k, units[k][2](), nq, k == 0, k == n - 1)

            def attn_finish(nq, col0):
                for hg in range(2):
                    o3 = Oacc[hg][:, 0:512].rearrange("p (h d) -> p h d", d=128)
                    K.op("dve", lambda e: e.reciprocal(rec8[:, hg * 4:hg * 4 + 4].unsqueeze(2), o3[:, :, 64:65]),
                         r=[Oacc[hg]], w=[rec8])
                    K.op("dve", lambda e: e.tensor_tensor(
                        att[:, :].rearrange("p (pr hf d) -> p pr hf d", hf=2, d=64)[:, :, hg, :], o3[:, :, 0:64],
                        rec8[:, hg * 4:hg * 4 + 4].unsqueeze(2).to_broadcast([128, 4, 64]), ALU.mult),
                         r=[Oacc[hg], rec8], w=[att])
                transpose4(att, lambda jj: att[:, jj * 128:(jj + 1) * 128], attT,
                           attT[:, :, col0:col0 + 128].rearrange("p c t -> p c t"))

            fast = bool(os.environ.get("KFAST"))
            jlist = list(range(32, NT)) if not fast else list(range(46, 52))
            jmin = jlist[0]

            def do_attn(j):
                jj = j - NPRE
                units = []
                for o in range(16, -1, -1):
                    kt = j - o
                    if kt < jmin:
                        continue
                    sl = kt % NR
                    units.append((None, lambda sl=sl: KT[sl], lambda sl=sl: VX[sl],
                                  lambda o=o: cm[:, o * 128:(o + 1) * 128].unsqueeze(1).to_broadcast([128, 4, 128]),
                                  kbias[:, kt:kt + 1]))
                run_units(units, QTs[j % 2], 128)
                attn_finish(128, jj * 128)

            prev_full = None
            qkv_tile(jlist[0], xcat[jlist[0] * 128:(jlist[0] + 1) * 128, :], jlist[0] >= NPRE, jlist[0] % NR, parts="a")
            for ji, j in enumerate(jlist):
                full = j >= NPRE
                if ji + 1 < len(jlist):
                    jn = jlist[ji + 1]
                    qkv_tile(jn, xcat[jn * 128:(jn + 1) * 128, :], jn >= NPRE, jn % NR, parts="a")
                ro, vf = qkv_tile(j, xcat[j * 128:(j + 1) * 128, :], full, j % NR, parts="b")
                if full:
                    late_dmas(1)
                if full:
                    jj = j - NPRE
                    K.dma("sp", wk_o[jj * 128:(jj + 1) * 128, :], ro[:, 8:16, :].rearrange("p h d -> p (h d)"), r=[ro])
                    K.dma("sp", wv_o[jj * 128:(jj + 1) * 128, :], vf[:], r=[vf])
                if prev_full is not None:
                    do_attn(prev_full)
                prev_full = j if full else None
            if prev_full is not None:
                do_attn(prev_full)

            if LEVEL >= 3:
                NBK = NB + 1
                KTs = [K.sb(e2, [128, 4, 128], BF16) for _ in range(NBK)]
                VXs = [K.sb(e2, [128, 8, 72], BF16) for _ in range(NBK)]
                for v_ in VXs:
                    K.op("pool", lambda e, v_=v_: e.memset(v_[:], 1.0), w=[v_])
                ckf = [K.sb(e2, [128, 512], F32, "ckf") for _ in range(3)]
                cvf = [K.sb(e2, [128, 512], F32, "cvf") for _ in range(3)]
                ckb = [K.sb(e2, [128, 512], BF16) for _ in range(2)]
                sslot = 0
                ro, vf = qkv_tile(ST, xsin, True, sslot)
                for bl in range(4):
                    K.dma("sp", wks_o[bl, 2044:2048, :], ro[4 * bl:4 * bl + 4, 8:16, :].rearrange("p h d -> p (h d)"), r=[ro])
                    K.dma("sp", wvs_o[bl, 2044:2048, :], vf[4 * bl:4 * bl + 4, :], r=[vf])
                units = []
                n_ = 0

                def finish_prep(n_, a, b_):
                    kb_ = ckb[n_ % 2]
                    kts_, vxs_ = KTs[n_ % NBK], VXs[n_ % NBK]
                    K.op("pool", lambda e: e.tensor_copy(kb_[:], a[:]), r=[a], w=[kb_])
                    transpose4(kb_, lambda jj: kb_[:, jj * 128:(jj + 1) * 128], kts_,
                               kts_[:, :, :].rearrange("p c t -> p (c t)"))
                    K.op("pool", lambda e: e.tensor_copy(vxs_[:, :, 0:64], b_[:].rearrange("p (h d) -> p h d", d=64)),
                         r=[b_], w=[vxs_])

                for bl in range(4):
                    for g in range(3):
                        def prep(bl=bl, g=g, n_=n_):
                            a, b_ = ckf[n_ % 3], cvf[n_ % 3]
                            for tq in range(4):
                                K.dma("sp", a[32 * tq:32 * tq + 32, :], ck[bl, 512 * g + tq:512 * (g + 1):16, :], w=[a])
                                K.dma("sp", b_[32 * tq:32 * tq + 32, :], cv[bl, 512 * g + tq:512 * (g + 1):16, :], w=[b_])
                            finish_prep(n_, a, b_)
                        units.append((prep, lambda n_=n_: KTs[n_ % NBK], lambda n_=n_: VXs[n_ % NBK],
                                      lambda bl=bl: ccc[:, bl * 16:(bl + 1) * 16].unsqueeze(1).to_broadcast([128, 4, 16]),
                                      None))
                        n_ += 1
                    for kt in range(12, 16):
                        def prep(bl=bl, kt=kt, n_=n_):
                            a, b_ = ckf[n_ % 3], cvf[n_ % 3]
                            K.dma("sp", a[:], ck[bl, kt * 128:(kt + 1) * 128, :], w=[a])
                            K.dma("sp", b_[:], cv[bl, kt * 128:(kt + 1) * 128, :], w=[b_])
                            finish_prep(n_, a, b_)
                        units.append((prep, lambda n_=n_: KTs[n_ % NBK], lambda n_=n_: VXs[n_ % NBK],
                                      lambda bl=bl, kt=kt: ccs[:, (bl * 16 + kt) * 16:(bl * 16 + kt + 1) * 16].unsqueeze(1).to_broadcast([128, 4, 16]),
                                      None))
                        n_ += 1
                units.append((None, lambda: KT[sslot], lambda: VX[sslot],
                              lambda: ccn[:, 0:16].unsqueeze(1).to_broadcast([128, 4, 16]), None))
                run_units(units, QTs[ST % 2], 16)
                attn_finish(16, NOWN * 128)
            K.barrier()

        with ExitStack() as e2:
            wk = dict(junk=K.sb(e2, [128, D], F32), ss=K.sb(e2, [128, 1], F32), rstd=K.sb(e2, [128, 1], F32),
                      xn=K.sb(e2, [128, D], BF16), kb=K.sb(e2, [128, 2, D], BF16))
            wmk = wload(e2, wb_mk, 8, D, name="wmk")
            wmv = wload(e2, wb_mv, 8, D, name="wmv")
            mT = K.sb(e2, [128, 8, 256], BF16)
            ktok = K.sb(e2, [128, 2, D], F32)
            vtok = K.sb(e2, [128, 2, D], F32)
            for mb in range(2):
                xt = K.sb(e2, [128, D], F32)
                K.dma("sp", xt[:], memd[mb * 128:(mb + 1) * 128, :], w=[xt])
                norm_T(wk, xt, 3, mT, mT[:, :, mb * 128:(mb + 1) * 128])
            for mb in range(2):
                for W, tok in ((wmk, ktok), (wmv, vtok)):
                    for cg in range(2):
                        ps = npf()
                        proj(ps, 512, mT, mT[:, :, mb * 128:(mb + 1) * 128], W, cg * 512)
                        K.op("act", lambda e, ps=ps, tok=tok, cg=cg, mb=mb: e.copy(tok[:, mb, cg * 512:(cg + 1) * 512], ps[:, :]),
                             r=[ps], w=[tok])
            K.dma("sp", memk_o.rearrange("(m p) n -> p m n", p=128), ktok[:], r=[ktok])
            K.dma("sp", memv_o.rearrange("(m p) n -> p m n", p=128), vtok[:], r=[vtok])
            mem_prepare(e2, wk, ktok, vtok, KTm_p, Vm_p)
            K.barrier()

        STf = K.sb(es, [128, 512], F32, "STf")
        with ExitStack() as e2:
          if LEVEL >= 4:
            wk = dict(junk=K.sb(e2, [128, D], F32), ss=K.sb(e2, [128, 1], F32), rstd=K.sb(e2, [128, 1], F32),
                      xn=K.sb(e2, [128, D], BF16))
            tri = cload(c_tri, [128, 128], stk=e2)
            mneg = cload(c_mneg, [128, 128], stk=e2)
            sh = cload(c_sh, [128, 7 * 128], BF16, stk=e2)
            shs = cload(c_shs, [128, 4 * 128], BF16, stk=e2)
            rm = cload(c_rm, [128, 4], stk=e2)
            cw = K.sb(e2, [128, 4, D], F32)
            for i in range(4):
                K.dma("sp", cw[:, i, :], bc_rows(convw_d[i:i + 1, :], D), w=[cw])
            cbf = K.sb(e2, [1, D], F32, "cbf")
            K.dma("sp", cbf[:], convb_d, w=[cbf])
            cbrow = K.sb(e2, [1, D], BF16, "cbrow")
            K.op("dve", lambda e: e.tensor_copy(cbrow[:], cbf[:]), r=[cbf], w=[cbrow])
            gs = cload(bc_rows(gssm_d, 512), [128, 512], stk=e2)
            sm = K.sb(e2, [128, 3, 8], F32)
            for i in range(3):
                K.dma("sp", sm[:, i, :], bc_rows(small_d[i:i + 1, :], 8), w=[sm])
            Aneg = K.sb(e2, [128, 8], F32)
            K.op("act", lambda e: e.activation(Aneg[:], sm[:, 1, :], AF.Exp), r=[sm], w=[Aneg])
            K.op("dve", lambda e: e.tensor_scalar(Aneg[:], Aneg[:], -1.0, None, ALU.mult), r=[Aneg], w=[Aneg])
            wz = wload(e2, wb_in, 8, 1544, col0=1536, name="wz", res="in_b")
            xts = [K.sb(e2, [128, D], F32, "xt") for _ in range(2)]
            hTs_ = [K.sb(e2, [128, 8, 128], BF16) for _ in range(2)]
            xbcs_ = [K.sb(e2, [128, D], F32) for _ in range(3)]
            hT, xbc = hTs_[0], xbcs_[0]
            xw = [[K.sb(e2, [128, D], BF16, "xw") for _ in range(4)] for _ in range(2)]
            for t_ in xw[0] + xw[1]:
                K.op("pool", lambda e, t_=t_: e.memset(t_[:], 0.0), w=[t_])
            xcs_ = [K.sb(e2, [128, D], F32) for _ in range(3)]
            xc = xcs_[0]
            xcb = K.sb(e2, [128, 512], BF16)
            dtrs_ = [K.sb(e2, [128, 8], F32) for _ in range(5)]
            dtms_ = [K.sb(e2, [128, 8], F32) for _ in range(5)]
            dtr, dtm = dtrs_[0], dtms_[0]
            av = K.sb(e2, [128, 8], F32)
            acs = K.sb(e2, [128, 8], F32)
            nacs = K.sb(e2, [128, 8], F32)
            toe = K.sb(e2, [128, 8], F32)
            dec = K.sb(e2, [128, 8], F32)
            fs = K.sb(e2, [128, 8], F32)
            xdt = K.sb(e2, [128, 512], BF16)
            xw2 = K.sb(e2, [128, 512], BF16)
            STb = K.sb(e2, [128, 512], BF16)
            abc = K.sb(e2, [128, 8, 128], F32)
            T1 = K.sb(e2, [128, 8, 128], F32)
            Lm = K.sb(e2, [128, 8, 128], F32)
            scT = K.sb(e2, [128, 8, 128], BF16)
            BCT = K.sb(e2, [128, 4, 128], BF16)
            ysum = K.sb(e2, [128, 512], F32)
            ytmp = K.sb(e2, [128, 512], F32)
            yS = K.sb(e2, [128, 512], F32)
            zss_ = [K.sb(e2, [128, 512], F32) for _ in range(5)]
            zs = zss_[0]
            ss2 = K.sb(e2, [128, 2], F32)
            yb = K.sb(e2, [128, 512], BF16)
            X7 = K.sb(e2, [128, D], F32)
            stl = K.sb(e2, [128, 4, 128], F32)
            K.op("dve", lambda e: e.memset(STf[:], 0.0), w=[STf])
            K.op("dve", lambda e: e.memset(STb[:], 0.0), w=[STb])

            def conv_dt(j, cur, prev, shm, nsh_prev, src_x, full):
                full = full or j == NPRE - 1
                ncol = D if full else 768
                for i in range(4):
                    K.op("pool", lambda e, i=i: e.tensor_tensor(cur[i][:, 0:ncol], src_x[:, 0:ncol], cw[:, i, 0:ncol], ALU.mult),
                         r=[src_x, cw], w=[cur[i]])
                for cg in range(2):
                    ps = npf()
                    nw = 512 if (full or cg == 0) else 256
                    for i in range(4):
                        K.op("pe", lambda e, i=i, cg=cg, ps=ps, nw=nw: e.matmul(
                            ps[:, 0:nw], shm[:, i * 128:(i + 1) * 128], cur[i][:, cg * 512:cg * 512 + nw],
                            start=(i == 0), stop=False), r=[shm, cur[i]], w=[ps])
                    for i in range(nsh_prev):
                        K.op("pe", lambda e, i=i, cg=cg, ps=ps, nw=nw: e.matmul(
                            ps[:, 0:nw], shm[:, (4 + i) * 128:(5 + i) * 128], prev[i][:, cg * 512:cg * 512 + nw],
                            start=False, stop=False), r=[shm, prev[i]], w=[ps])
                    K.op("pe", lambda e, cg=cg, ps=ps, nw=nw: e.matmul(
                        ps[:, 0:nw], onesb[0:1, :], cbrow[0:1, cg * 512:cg * 512 + nw], start=False, stop=True),
                         r=[onesb, cbrow], w=[ps])
                    K.op("act", lambda e, cg=cg, ps=ps, nw=nw: e.activation(xc[:, cg * 512:cg * 512 + nw], ps[:, 0:nw], AF.Silu),
                         r=[ps], w=[xc])

            def ssd_chunk(full, dtm_ap_t, ST_f, ST_b, parts="abc"):
                nbc = 512 if full else 256
                if "a" in parts:
                    K.op("dve", lambda e: e.tensor_tensor(av[:], dtm_ap_t[:], Aneg[:], ALU.mult), r=[dtm_ap_t, Aneg], w=[av])
                    p1 = npf()
                    K.op("pe", lambda e: e.matmul(p1[:, 0:8], onesf[:], av[:], start=True, stop=True), r=[onesf, av], w=[p1])
                    K.op("pe", lambda e: e.matmul(p1[:, 8:16], tri[:], av[:], start=True, stop=True), r=[tri, av], w=[p1])
                    K.op("act", lambda e: e.copy(acs[:], p1[:, 8:16]), r=[p1], w=[acs])
                    K.op("dve", lambda e: e.tensor_tensor(toe[:], p1[:, 0:8], acs[:], ALU.subtract), r=[p1, acs], w=[toe])
                    K.op("act", lambda e: e.activation(toe[:], toe[:], AF.Exp), r=[toe], w=[toe])
                    K.op("act", lambda e: e.activation(dec[:], p1[:, 0:8], AF.Exp), r=[p1], w=[dec])
                    xs3 = xc[:, 0:512].rearrange("p (h d) -> p h d", d=64)
                    K.op("dve", lambda e: e.tensor_tensor(xdt[:].rearrange("p (h d) -> p h d", d=64), xs3,
                                                          dtm_ap_t[:].unsqueeze(2).to_broadcast([128, 8, 64]), ALU.mult),
                         r=[xc, dtm_ap_t], w=[xdt])
                    K.op("dve", lambda e: e.tensor_tensor(xw2[:].rearrange("p (h d) -> p h d", d=64),
                                                          xdt[:].rearrange("p (h d) -> p h d", d=64),
                                                          toe[:].unsqueeze(2).to_broadcast([128, 8, 64]), ALU.mult),
                         r=[xdt, toe], w=[xw2])
                    nbc = 512 if full else 256
                    K.op("dve", lambda e: e.tensor_copy(xcb[:, 0:nbc], xc[:, 512:512 + nbc]), r=[xc], w=[xcb])
                if full and "b" in parts:
                    transpose4(xcb, lambda jj: xcb[:, jj * 128:(jj + 1) * 128], BCT, BCT[:, :, :].rearrange("p c t -> p (c t)"))
                    K.op("act", lambda e: e.mul(nacs[:], acs[:], -1.0), r=[acs], w=[nacs])
                    K.op("act", lambda e: e.activation(fs[:], acs[:], AF.Exp), r=[acs], w=[fs])
                    K.op("pool", lambda e: e.tensor_tensor(abc[:], av[:].unsqueeze(2).to_broadcast([128, 8, 128]),
                                                          tri[:].unsqueeze(1).to_broadcast([128, 8, 128]), ALU.mult),
                         r=[av, tri], w=[abc])
                    g_ = npf()
                    for g in range(2):
                        K.op("pe", lambda e, g=g: e.matmul(g_[:, g * 128:(g + 1) * 128], BCT[:, g, :], BCT[:, 2 + g, :],
                                                           start=True, stop=True), r=[BCT], w=[g_])
                    for half in range(2):
                        R = npf()
                        K.op("pe", lambda e, half=half, R=R: e.matmul(
                            R[:, :], onesf[:], abc[:, half * 4:half * 4 + 4, :].rearrange("p h l -> p (h l)"),
                            start=True, stop=True), r=[onesf, abc], w=[R])
                        K.op("dve", lambda e, half=half, R=R: e.tensor_tensor(
                            T1[:, half * 4:half * 4 + 4, :], R[:, :].rearrange("p (h l) -> p h l", l=128),
                            mneg[:].unsqueeze(1).to_broadcast([128, 4, 128]), ALU.add), r=[R, mneg], w=[T1])
                    for h in range(8):
                        K.op("act", lambda e, h=h: e.activation(Lm[:, h, :], T1[:, h, :], AF.Exp, bias=nacs[:, h:h + 1]),
                             r=[T1, nacs], w=[Lm])
                    for g in range(2):
                        K.op("dve", lambda e, g=g: e.tensor_tensor(
                            scT[:, g * 4:g * 4 + 4, :], Lm[:, g * 4:g * 4 + 4, :],
                            g_[:, g * 128:(g + 1) * 128].unsqueeze(1).to_broadcast([128, 4, 128]), ALU.mult),
                             r=[Lm, g_], w=[scT])
                    yd = npf()
                    for h in range(8):
                        K.op("pe", lambda e, h=h: e.matmul(yd[:, h * 64:(h + 1) * 64], scT[:, h, :],
                                                           xdt[:, h * 64:(h + 1) * 64], start=True, stop=True),
                             r=[scT, xdt], w=[yd])
                    K.op("act", lambda e: e.copy(ysum[:], yd[:, :]), r=[yd], w=[ysum])
                if full and "c" in parts:
                    yo = npf()
                    for h in range(8):
                        K.op("pe", lambda e, h=h: e.matmul(yo[:, h * 64:(h + 1) * 64], BCT[:, 2 + h // 4, :],
                                                           ST_b[:, h * 64:(h + 1) * 64], start=True, stop=True),
                             r=[BCT, ST_b], w=[yo])
                    K.op("dve", lambda e: e.tensor_tensor(ytmp[:].rearrange("p (h d) -> p h d", d=64),
                                                          yo[:, :].rearrange("p (h d) -> p h d", d=64),
                                                          fs[:].unsqueeze(2).to_broadcast([128, 8, 64]), ALU.mult),
                         r=[yo, fs], w=[ytmp])
                    K.op("dve", lambda e: e.tensor_tensor(ysum[:], ysum[:], ytmp[:], ALU.add), r=[ysum, ytmp], w=[ysum])
                if "c" in parts:
                    cs_ = npf()
                    for g in range(2):
                        K.op("pe", lambda e, g=g: e.matmul(cs_[:, g * 256:(g + 1) * 256], xcb[:, g * 128:(g + 1) * 128],
                                                           xw2[:, g * 256:(g + 1) * 256], start=True, stop=True),
                             r=[xcb, xw2], w=[cs_])
                    K.op("dve", lambda e: e.tensor_tensor(ST_f[:].rearrange("p (h d) -> p h d", d=64),
                                                          ST_f[:].rearrange("p (h d) -> p h d", d=64),
                                                          dec[:].unsqueeze(2).to_broadcast([128, 8, 64]), ALU.mult),
                         r=[ST_f, dec], w=[ST_f])
                    K.op("dve", lambda e: e.tensor_tensor(ST_f[:], ST_f[:], cs_[:, :], ALU.add), r=[ST_f, cs_], w=[ST_f])
                    K.op("act", lambda e: e.copy(ST_b[:], ST_f[:]), r=[ST_f], w=[ST_b])

            def ssd_post(ysrc, zps, col0):
                K.op("dve", lambda e: e.tensor_tensor(ytmp[:].rearrange("p (h d) -> p h d", d=64),
                                                      xc[:, 0:512].rearrange("p (h d) -> p h d", d=64),
                                                      sm[:, 2, :].unsqueeze(2).to_broadcast([128, 8, 64]), ALU.mult),
                     r=[xc, sm], w=[ytmp])
                K.op("dve", lambda e: e.tensor_tensor(ytmp[:], ytmp[:], ysrc[:], ALU.add), r=[ytmp, ysrc], w=[ytmp])
                K.op("dve", lambda e: e.tensor_tensor(ytmp[:], ytmp[:], zs[:], ALU.mult), r=[ytmp, zs], w=[ytmp])
                for g in range(2):
                    K.op("act", lambda e, g=g: e.activation(zs[:, g * 256:(g + 1) * 256], ytmp[:, g * 256:(g + 1) * 256], AF.Square,
                                                            accum_out=ss2[:, g:g + 1]), r=[ytmp], w=[zs, ss2])
                K.op("act", lambda e: e.activation(ss2[:], ss2[:], AF.Ln, scale=1.0 / 256, bias=epsT[:, 0:1]), r=[ss2, epsT], w=[ss2])
                K.op("act", lambda e: e.activation(ss2[:], ss2[:], AF.Exp, scale=-0.5), r=[ss2], w=[ss2])
                K.op("dve", lambda e: e.tensor_tensor(ytmp[:].rearrange("p (g d) -> p g d", d=256),
                                                      ytmp[:].rearrange("p (g d) -> p g d", d=256),
                                                      ss2[:].unsqueeze(2).to_broadcast([128, 2, 256]), ALU.mult),
                     r=[ytmp, ss2], w=[ytmp])
                K.op("dve", lambda e: e.tensor_tensor(yb[:], ytmp[:], gs[:], ALU.mult), r=[ytmp, gs], w=[yb])
                transpose4(yb, lambda jj: yb[:, jj * 128:(jj + 1) * 128], yT, yT[:, :, col0:col0 + 128].rearrange("p c t -> p c t"))

            def dt_chain(ps8, vcol):
                K.op("dve", lambda e: e.tensor_tensor(dtr[:], ps8, sm[:, 0, :], ALU.add), r=[sm], w=[dtr])
                K.op("act", lambda e: e.activation(dtr[:], dtr[:], AF.Exp), r=[dtr], w=[dtr])
                K.op("act", lambda e: e.activation(dtr[:], dtr[:], AF.Ln, bias=1.0), r=[dtr], w=[dtr])
                K.op("dve", lambda e: e.tensor_scalar(dtm[:], dtr[:], vcol, None, ALU.mult), r=[dtr, valid, rm], w=[dtm])

            def state_out(ST_f, dst):
                p = npf()
                for c in range(4):
                    K.op("pe", lambda e, c=c: e.transpose(p[:, c * 128:(c + 1) * 128], ST_f[:, c * 128:(c + 1) * 128], identf[:]),
                         r=[ST_f, identf], w=[p])
                K.op("act", lambda e: e.copy(stl[:].rearrange("p c n -> p (c n)"), p[:, :]), r=[p], w=[stl])
                K.dma("sp", dst.rearrange("(c p) n -> p c n", p=128), stl[:], r=[stl])

            def front(j, xsrc, full, parts="ab"):
                if "a" in parts:
                    xt = xts[j % 2]
                    K.dma("sp", xt[:], xsrc, w=[xt])
                    norm_T(wk, xt, 0, hT, hT[:, :, :])
                if "b" not in parts:
                    return None, None
                for cg in range(2):
                    ps = npf()
                    nw = 512 if (full or j == NPRE - 1 or cg == 0) else 256
                    proj(ps, nw, hT, hT[:, :, :], wz, 512 + cg * 512)
                    K.op("dve", lambda e, ps=ps, cg=cg, nw=nw: e.tensor_copy(xbc[:, cg * 512:cg * 512 + nw], ps[:, 0:nw]), r=[ps], w=[xbc])
                pd = npf()
                proj(pd, 8, hT, hT[:, :, :], wz, 1536)
                pz = None
                if full:
                    pz = npf()
                    proj(pz, 512, hT, hT[:, :, :], wz, 0)
                    K.op("act", lambda e: e.activation(zs[:], pz[:, :], AF.Silu), r=[pz], w=[zs])
                return pd, pz

            def bind(j):
                nonlocal hT, xbc, xc, dtr, dtm, zs
                hT, xbc, xc = hTs_[j % 2], xbcs_[j % 3], xcs_[j % 3]
                dtr, dtm, zs = dtrs_[j % 5], dtms_[j % 5], zss_[j % 5]

            def stage_F1(j):
                bind(j)
                front(j, xcat[j * 128:(j + 1) * 128, :], j >= NPRE, parts="a")

            def stage_F2(j):
                bind(j)
                full = j >= NPRE
                pd, pz = front(j, xcat[j * 128:(j + 1) * 128, :], full, parts="b")
                pd_ap = pd[:, 0:8]
                K.op("dve", lambda e: e.tensor_tensor(dtr[:], pd_ap, sm[:, 0, :], ALU.add), r=[pd, sm], w=[dtr])
                K.op("act", lambda e: e.activation(dtr[:], dtr[:], AF.Exp), r=[dtr], w=[dtr])
                K.op("act", lambda e: e.activation(dtr[:], dtr[:], AF.Ln, bias=1.0), r=[dtr], w=[dtr])
                K.op("dve", lambda e: e.tensor_scalar(dtm[:], dtr[:], valid[:, j:j + 1], None, ALU.mult), r=[dtr, valid], w=[dtm])

            def stage_C(j):
                bind(j)
                conv_dt(j, xw[j % 2], xw[(j + 1) % 2], sh, 3, xbc, j >= NPRE)
                if j == NT - 1:
                    K.dma("sp", convp_o, xbc[125:128, :], r=[xbc])

            def stage_S(j, part):
                bind(j)
                full = j >= NPRE
                ssd_chunk(full, dtm, STf, STb, parts=part)
                if full and part == "c":
                    ssd_post(ysum, None, (j - NPRE) * 128)

            jl = list(range(NT)) if not os.environ.get("KFAST") else list(range(46, 52))
            nj = len(jl)
            stage_F1(jl[0])
            LC, LS = 1, 3
            for it in range(nj + LS):
                if it % 4 == 0:
                    late_dmas(1)
                if it + 1 < nj:
                    stage_F1(jl[it + 1])
                if it >= LS:
                    stage_S(jl[it - LS], "a")
                if LC <= it < nj + LC:
                    stage_C(jl[it - LC])
                if it >= LS:
                    stage_S(jl[it - LS], "b")
                if it < nj:
                    stage_F2(jl[it])
                if it >= LS:
                    stage_S(jl[it - LS], "c")
            late_dmas()
            bind(0)
            state_out(STf, ssmp_o)

            if LEVEL >= 5:
                pd, pz = front(ST, xsin, True)
                pd_ap = pd[:, 0:8]
                K.op("dve", lambda e: e.tensor_tensor(dtr[:], pd_ap, sm[:, 0, :], ALU.add), r=[pd, sm], w=[dtr])
                K.op("act", lambda e: e.activation(dtr[:], dtr[:], AF.Exp), r=[dtr], w=[dtr])
                K.op("act", lambda e: e.activation(dtr[:], dtr[:], AF.Ln, bias=1.0), r=[dtr], w=[dtr])
                K.op("dve", lambda e: e.memset(X7[:], 0.0), w=[X7])
                for bl in range(4):
                    K.dma("sp", X7[7 * bl:7 * bl + 3, :], sconv[bl], w=[X7])
                    K.dma("sp", X7[7 * bl + 3:7 * bl + 7, :], xbc[4 * bl:4 * bl + 4, :], r=[xbc], w=[X7])
                for bl in range(4):
                    K.dma("sp", convs_o[bl], X7[7 * bl + 4:7 * bl + 7, :], r=[X7])
                conv_dt(ST, xw[0], xw[1], shs, 0, X7, True)
                K.op("dve", lambda e: e.memset(yS[:], 0.0), w=[yS])
                STs = K.sb(e2, [128, 512], F32)
                STsb = K.sb(e2, [128, 512], BF16)
                for bl in range(4):
                    K.dma("sp", stl[:], sssm[bl].rearrange("(c p) n -> p c n", p=128), w=[stl])
                    p = npf()
                    for c in range(4):
                        K.op("pe", lambda e, c=c: e.transpose(p[:, c * 128:(c + 1) * 128], stl[:, c, :], identf[:]),
                             r=[stl, identf], w=[p])
                    K.op("act", lambda e: e.copy(STs[:], p[:, :]), r=[p], w=[STs])
                    K.op("dve", lambda e: e.tensor_copy(STsb[:], STs[:]), r=[STs], w=[STsb])
                    K.op("dve", lambda e, bl=bl: e.tensor_scalar(dtm[:], dtr[:], rm[:, bl:bl + 1], None, ALU.mult),
                         r=[dtr, rm], w=[dtm])
                    ssd_chunk(True, dtm, STs, STsb)
                    K.op("dve", lambda e, bl=bl: e.scalar_tensor_tensor(yS[:], ysum[:], rm[:, bl:bl + 1], yS[:], ALU.mult, ALU.add),
                         r=[ysum, rm, yS], w=[yS])
                    state_out(STs, ssms_o[bl])
                ssd_post(yS, None, NOWN * 128)
            K.barrier()

        gfin = cload(bc_rows(gfin_d, D), [128, D])
        KTm_s = K.sb(es, [128, 8, 256], BF16, "KTm_s")
        Vm_s = K.sb(es, [128, 2, D], BF16, "Vm_s")
        wo = wload(es, wb_out, 8, D, name="wo")
        wq = wload(es, wb_xq, 8, D, name="wq")
        wxo = wload(es, wb_xo, 8, D, name="wxo")
        wup = [K.sb(es, [128, 8, 512], BF16, "wup") for _ in range(2)]
        wdn = [K.sb(es, [128, 4, D], BF16, "wdn") for _ in range(2)]
        upv = wb_up.rearrange("(kc p) n -> p kc n", p=128)
        dnv = wb_down.rearrange("(fc p) n -> p fc n", p=128)

        def load_up(ch):
            W = wup[ch % 2]
            K.dma("sp", W[:, 0:4, :], upv[:, 0:4, ch * 512:(ch + 1) * 512], r=[wres[wb_up.name]], w=[W])
            K.dma("sp", W[:, 4:8, :], upv[:, 4:8, ch * 512:(ch + 1) * 512], r=[wres[wb_up.name]], w=[W])

        def load_dn(ch):
            W = wdn[ch % 2]
            K.dma("sp", W[:, :, :], dnv[:, ch * 4:ch * 4 + 4, :], r=[wres[wb_down.name]], w=[W])

        groups = [(g * 4, 4) for g in range(4)] + [(NOWN, 1)]
        if LEVEL < 6:
            groups = []
        elif LEVEL == 6:
            groups = groups[:1]
        elif LEVEL == 7:
            groups = [groups[0], groups[4]]
        for gi_, (t0, ntl) in enumerate(groups):
            n = ntl * 128
            is_s = ntl == 1
            with ExitStack() as eg:
                resid = [K.sb(eg, [128, D], F32, "resid") for _ in range(ntl)]
                hmT = K.sb(eg, [128, 8, n], BF16, "hmT")
                with ExitStack() as e2:
                    wk = dict(junk=K.sb(e2, [128, D], F32), ss=K.sb(e2, [128, 1], F32), rstd=K.sb(e2, [128, 1], F32),
                              xn=K.sb(e2, [128, D], BF16), kb=(K.sb(e2, [128, 2, D], BF16) if is_s else None),
                              xP=[K.sb(e2, [128, 512], BF16, "xP") for _ in range(4)],
                              xrec=[K.sb(e2, [128, 512], F32, "xrec") for _ in range(2)])
                    load_up(0)
                    load_up(1)
                    load_dn(0)
                    load_dn(1)
                    hxT = K.sb(e2, [128, 8, n], BF16)
                    qT = K.sb(e2, [128, 8, n], BF16)
                    OT = K.sb(e2, [128, 8, n], BF16)
                    for tl in range(ntl):
                        src = xsin if is_s else xcat[(NPRE + t0 + tl) * 128:(NPRE + t0 + tl + 1) * 128, :]
                        K.dma("sp", resid[tl][:], src, w=[resid[tl]])
                    for tl in range(ntl):
                        c0 = (t0 + tl) * 128
                        for cg in range(2):
                            ps = npf()
                            for kc in range(8):
                                srcT = attT if kc < 4 else yT
                                K.op("pe", lambda e, kc=kc, cg=cg, ps=ps, srcT=srcT: e.matmul(
                                    ps[:, :], srcT[:, kc % 4, c0:c0 + 128], wo[:, kc, cg * 512:(cg + 1) * 512],
                                    start=(kc == 0), stop=(kc == 7)), r=[srcT, wo], w=[ps])
                            K.op("dve", lambda e, cg=cg, ps=ps, tl=tl: e.tensor_tensor(
                                resid[tl][:, cg * 512:(cg + 1) * 512], ps[:, :], resid[tl][:, cg * 512:(cg + 1) * 512], ALU.add),
                                 r=[ps, resid[tl]], w=[resid[tl]])
                    for tl in range(ntl):
                        norm_T(wk, resid[tl], 1, hxT, hxT[:, :, tl * 128:(tl + 1) * 128])
                    for fc in range(8):
                        ps = npf()
                        for kc in range(8):
                            K.op("pe", lambda e, kc=kc, fc=fc, ps=ps: e.matmul(
                                ps[:, 0:n], wq[:, kc, fc * 128:(fc + 1) * 128], hxT[:, kc, :],
                                start=(kc == 0), stop=(kc == 7)), r=[wq, hxT], w=[ps])
                        K.op("act", lambda e, fc=fc, ps=ps: e.mul(qT[:, fc, :], ps[:, 0:n], 1.0 / 16), r=[ps], w=[qT])
                    if not is_s:
                        xattn(wk, qT, qT[:, :, :], n, KTm_p, Vm_p, OT, OT[:, :, :])
                    else:
                        K.op("dve", lambda e: e.memset(OT[:], 0.0), w=[OT])
                        ktoks = [K.sb(e2, [128, 2, D], F32) for _ in range(2)]
                        vtoks = [K.sb(e2, [128, 2, D], F32) for _ in range(2)]
                        KTm2 = [KTm_s, KTm_s]
                        Vm2 = [Vm_s, Vm_s]

                        def mload(bl):
                            K.dma("sp", ktoks[bl % 2][:], cmk[bl].rearrange("(m p) n -> p m n", p=128), w=[ktoks[bl % 2]])
                            K.dma("sp", vtoks[bl % 2][:], cmv[bl].rearrange("(m p) n -> p m n", p=128), w=[vtoks[bl % 2]])

                        mload(0)
                        mload(1)
                        for bl in range(4):
                            mem_prepare(e2, wk, ktoks[bl % 2], vtoks[bl % 2], KTm2[bl % 2], Vm2[bl % 2])
                            if bl + 2 < 4:
                                mload(bl + 2)
                            xattn(wk, qT, qT[:, :, 4 * bl:4 * bl + 4], 4, KTm2[bl % 2], Vm2[bl % 2], OT, OT[:, :, 4 * bl:4 * bl + 4])
                    for tl in range(ntl):
                        for cg in range(2):
                            ps = npf()
                            for kc in range(8):
                                K.op("pe", lambda e, kc=kc, cg=cg, ps=ps, tl=tl: e.matmul(
                                    ps[:, :], OT[:, kc, tl * 128:(tl + 1) * 128], wxo[:, kc, cg * 512:(cg + 1) * 512],
                                    start=(kc == 0), stop=(kc == 7)), r=[OT, wxo], w=[ps])
                            K.op("dve", lambda e, cg=cg, ps=ps, tl=tl: e.tensor_tensor(
                                resid[tl][:, cg * 512:(cg + 1) * 512], ps[:, :], resid[tl][:, cg * 512:(cg + 1) * 512], ALU.add),
                                 r=[ps, resid[tl]], w=[resid[tl]])
                    for tl in range(ntl):
                        norm_T(wk, resid[tl], 2, hmT, hmT[:, :, tl * 128:(tl + 1) * 128])
                    K.barrier()
                with ExitStack() as e2:
                    wk = dict(junk=K.sb(e2, [128, D], F32), ss=K.sb(e2, [128, 1], F32), rstd=K.sb(e2, [128, 1], F32),
                              xn=K.sb(e2, [128, D], BF16))
                    hidT = K.sb(e2, [128, 32, n], BF16)
                    yt = wk["junk"]
                    rls = [K.sb(e2, [128, 512], F32, "rl") for _ in range(2)]
                    for ch in range(8):
                        W = wup[ch % 2]
                        for f4 in range(4):
                            fc = ch * 4 + f4
                            ps = npf()
                            for kc in range(8):
                                K.op("pe", lambda e, kc=kc, f4=f4, ps=ps, W=W: e.matmul(
                                    ps[:, 0:n], W[:, kc, f4 * 128:(f4 + 1) * 128], hmT[:, kc, :],
                                    start=(kc == 0), stop=(kc == 7)), r=[W, hmT], w=[ps])
                            rl = rls[fc % 2]
                            K.op("act", lambda e, ps=ps, rl=rl: e.activation(rl[:, 0:n], ps[:, 0:n], AF.Relu), r=[ps], w=[rl])
                            if fc % 2:
                                K.op("act", lambda e, fc=fc, rl=rl: e.activation(hidT[:, fc, :], rl[:, 0:n], AF.Square), r=[rl], w=[hidT])
                            else:
                                K.op("dve", lambda e, fc=fc, rl=rl: e.tensor_tensor(
                                    hidT[:, fc, :], rl[:, 0:n], rl[:, 0:n], ALU.mult), r=[rl], w=[hidT])
                        if ch + 2 < 8:
                            load_up(ch + 2)
                    for ch in range(8):
                        W = wdn[ch % 2]
                        for tl in range(ntl):
                            for cg in range(2):
                                ps = npf()
                                for f4 in range(4):
                                    K.op("pe", lambda e, f4=f4, cg=cg, ps=ps, tl=tl, W=W, ch=ch: e.matmul(
                                        ps[:, :], hidT[:, ch * 4 + f4, tl * 128:(tl + 1) * 128], W[:, f4, cg * 512:(cg + 1) * 512],
                                        start=(f4 == 0), stop=(f4 == 3)), r=[hidT, W], w=[ps])
                                K.op("dve", lambda e, cg=cg, ps=ps, tl=tl: e.tensor_tensor(
                                    resid[tl][:, cg * 512:(cg + 1) * 512], ps[:, :], resid[tl][:, cg * 512:(cg + 1) * 512], ALU.add),
                                     r=[ps, resid[tl]], w=[resid[tl]])
                        if ch + 2 < 8:
                            load_dn(ch + 2)
                    junk, ss, rstd = wk["junk"], wk["ss"], wk["rstd"]
                    for tl in range(ntl):
                        x = resid[tl]
                        K.op("act", lambda e, x=x: e.activation(junk[:], x[:], AF.Square, accum_out=ss[:, 0:1]), r=[x], w=[junk, ss])
                        K.op("act", lambda e: e.activation(rstd[:], ss[:], AF.Ln, scale=1.0 / D, bias=epsT[:, 0:1]), r=[ss, epsT], w=[rstd])
                        K.op("act", lambda e: e.activation(rstd[:], rstd[:], AF.Exp, scale=-0.5), r=[rstd], w=[rstd])
                        K.op("dve", lambda e, x=x: e.scalar_tensor_tensor(yt[:], x[:], rstd[:, 0:1], gfin[:], ALU.mult, ALU.mult),
                             r=[x, rstd, gfin], w=[yt])
                        dst = ys_o if is_s else y_o[(t0 + tl) * 128:(t0 + tl + 1) * 128, :]
                        K.dma("sp", dst, yt[:], r=[yt])
                    K.barrier()
        K.barrier(full=True)
    return nc


def _mult(delta):
    d = delta
    c = ((d >= 0) & (d <= 128)).astype(np.float32)
    c += ((d >= 0) & (d % 4 == 0) & (d <= 512)).astype(np.float32)
    c += ((d >= 0) & (d % 16 == 0) & (d <= 2048)).astype(np.float32)
    return c


def _consts():
    i = np.arange(128)
    c = {}
    c["c_ident"] = np.eye(128, dtype=np.float32)
    c["c_tri"] = (i[:, None] <= i[None, :]).astype(np.float32)
    c["c_mneg"] = np.where(i[:, None] <= i[None, :], 0.0, NEG).astype(np.float32)
    cm = np.zeros((128, 17 * 128), np.float32)
    for o in range(17):
        cm[:, o * 128:(o + 1) * 128] = _mult(128 * o + i[None, :] - i[:, None])
    c["c_cm"] = cm
    sh = np.zeros((128, 7 * 128), np.float32)
    for t in range(4):
        m = (i[:, None] == i[None, :] - 3 + t)
        sh[:, t * 128:(t + 1) * 128] = m
    for t in range(3):
        m = (i[:, None] - 128 == i[None, :] - 3 + t)
        sh[:, (4 + t) * 128:(5 + t) * 128] = m
    c["c_sh"] = sh
    shs = np.zeros((128, 4 * 128), np.float32)
    for t in range(4):
        for bl in range(4):
            for tt in range(4):
                shs[7 * bl + tt + t, t * 128 + 4 * bl + tt] = 1.0
    c["c_shs"] = shs
    cs = np.zeros((128, 64 * 16), np.float32)
    for bl in range(4):
        for kt in range(16):
            blk = np.zeros((128, 16), np.float32)
            for t in range(4):
                blk[:, 4 * bl + t] = _mult(2048 + t - 128 * kt - i)
            cs[:, (bl * 16 + kt) * 16:(bl * 16 + kt + 1) * 16] = blk
    c["c_cs"] = cs
    cn = np.zeros((128, 16), np.float32)
    for bl in range(4):
        for tk in range(4):
            for tq in range(4):
                if tq >= tk:
                    cn[4 * bl + tk, 4 * bl + tq] = _mult(np.array(tq - tk))
    c["c_cn"] = cn
    cc = np.zeros((128, 4 * 16), np.float32)
    for bl in range(4):
        for t in range(4):
            cc[32 * t:32 * t + 32, bl * 16 + 4 * bl + t] = 1.0
    c["c_cc"] = cc
    rm = np.zeros((128, 4), np.float32)
    for bl in range(4):
        rm[4 * bl:4 * bl + 4, bl] = 1.0
    c["c_rm"] = rm
    return c


_NC = None


def kernel(x_prompt, x_sample, cache_win_k, cache_win_v, state_conv, state_ssm,
           cache_mem_k, cache_mem_v, mem_prompt,
           g_mix, w_in, conv_w, conv_b, dt_bias, a_log, d_skip, g_ssm, w_out,
           g_xatt, g_mem, w_xq, w_mk, w_mv, w_xo, g_mlp, w_up, w_down, g_final):
    global _NC
    f = lambda a: np.ascontiguousarray(np.asarray(a, dtype=np.float32))
    x_prompt, x_sample = f(x_prompt), f(x_sample)
    consts = _consts()
    gc = np.concatenate([f(g)[0].reshape(8, 128).T for g in (g_mix, g_xatt, g_mlp, g_mem)], axis=1)
    shared = dict(w_in=f(w_in)[0], w_out=f(w_out)[0], w_xq=f(w_xq)[0], w_mk=f(w_mk)[0], w_mv=f(w_mv)[0],
                  w_xo=f(w_xo)[0], w_up=f(w_up)[0], w_down=f(w_down)[0], gcols=np.ascontiguousarray(gc),
                  gfin=f(g_final).reshape(1, D), convw=f(conv_w)[0], convb=f(conv_b)[0].reshape(1, D),
                  small=np.stack([f(dt_bias)[0], f(a_log)[0], f(d_skip)[0]]), gssm=f(g_ssm)[0].reshape(1, 512))
    shared.update(consts)
    half = 32
    inv = (10000.0 ** (-np.arange(half, dtype=np.float32) * 2.0 / 64)).astype(np.float32)
    in_maps = []
    for core in range(8):
        b, c = core // 4, core % 4
        start = 2048 * c
        p0 = start - NPRE * 128
        xc = np.zeros((NT * 128, D), np.float32)
        lo = max(p0, 0)
        xc[lo - p0:] = x_prompt[b, lo:start + 2048]
        pos = np.zeros((128, 65), np.float32)
        pos[:, :NT] = p0 + 128 * np.arange(NT)[None, :] + np.arange(128)[:, None]
        pos[:, 64] = -1.0
        pos[:16, 64] = 8192 + (np.arange(16) % 4)
        pp = np.maximum(pos, 0.0).T.astype(np.float32)
        ang = pp[:, :, None] * inv[None, None, :]
        rope = np.concatenate([np.cos(ang), np.sin(ang)], axis=-1).astype(np.float32)
        xs = np.zeros((128, D), np.float32)
        xs[:16] = x_sample[4 * core:4 * core + 4].reshape(16, D)
        m = dict(shared)
        m.update(xcat=xc, xs=xs, pos=pos, rope=np.ascontiguousarray(rope),
                 ck=f(cache_win_k)[0, 4 * core:4 * core + 4].reshape(4, 2048, 512),
                 cv=f(cache_win_v)[0, 4 * core:4 * core + 4].reshape(4, 2048, 512),
                 sconv=f(state_conv)[0, 4 * core:4 * core + 4],
                 sssm=f(state_ssm)[0, 4 * core:4 * core + 4].reshape(4, 512, 128),
                 cmk=f(cache_mem_k)[0, 4 * core:4 * core + 4].reshape(4, 256, D),
                 cmv=f(cache_mem_v)[0, 4 * core:4 * core + 4].reshape(4, 256, D),
                 mem=f(mem_prompt)[b])
        in_maps.append({k: np.ascontiguousarray(v) for k, v in m.items()})
    if _NC is None:
        _NC = build()
    import os
    ncores = int(os.environ.get("KCORES", "8"))
    if ncores < 8:
        c0 = int(os.environ.get("KCORE", "3"))
        res = run_bass_kernel_spmd(_NC, in_maps[c0:c0 + 1], core_ids=[0])
        return res.results[0]
    res = run_bass_kernel_spmd(_NC, in_maps, core_ids=list(range(8)))
    R = res.results
    y_prompt = np.stack([np.concatenate([R[4 * b + c]["y"] for c in range(4)], axis=0) for b in range(2)])
    y_sample = np.concatenate([R[k]["ys"][:16].reshape(4, 4, D) for k in range(8)], axis=0)
    wkp = np.stack([R[4 * b + 3]["wk"].reshape(2048, 8, 64) for b in range(2)])[None]
    wvp = np.stack([R[4 * b + 3]["wv"].reshape(2048, 8, 64) for b in range(2)])[None]
    cvp = np.stack([R[4 * b + 3]["convp"] for b in range(2)])[None]
    ssp = np.stack([R[4 * b + 3]["ssmp"].reshape(8, 64, 128) for b in range(2)])[None]
    mkp = np.stack([R[4 * b]["memk"].reshape(256, 4, 256) for b in range(2)])[None]
    mvp = np.stack([R[4 * b]["memv"].reshape(256, 4, 256) for b in range(2)])[None]
    wks = np.concatenate([R[k]["wks"].reshape(4, 2048, 8, 64) for k in range(8)], axis=0)[None]
    wvs = np.concatenate([R[k]["wvs"].reshape(4, 2048, 8, 64) for k in range(8)], axis=0)[None]
    cvs = np.concatenate([R[k]["convs"] for k in range(8)], axis=0)[None]
    sss = np.concatenate([R[k]["ssms"].reshape(4, 8, 64, 128) for k in range(8)], axis=0)[None]
    return tuple(np.ascontiguousarray(a.astype(np.float32)) for a in
                 (y_prompt, y_sample, wkp, wvp, cvp, ssp, mkp, mvp, wks, wvs, cvs, sss))
```

```python
import numpy as np
from contextlib import ExitStack
import concourse.bass as bass
import concourse.mybir as mybir
from concourse.bass_utils import run_bass_kernel_spmd

F32 = mybir.dt.float32
BF16 = mybir.dt.bfloat16
ALU = mybir.AluOpType
AF = mybir.ActivationFunctionType
AX = mybir.AxisListType

D = 1024
NPRE = 48
NOWN = 16
NT = NPRE + NOWN
ST = 64
NEG = -30000.0
EPS = 1e-6
NDS = 56


class Res:
    __slots__ = ("w", "r")

    def __init__(self):
        self.w = None
        self.r = {}


class TT:
    def __init__(self, h, res=None):
        self.h = h
        self.res = res if res is not None else Res()

    def __getitem__(self, k):
        return self.h[k]


class KB:
    def __init__(self, nc, es):
        self.nc = nc
        self.es = es
        self.eng = {}
        for name, e in (("pe", nc.tensor), ("act", nc.scalar), ("dve", nc.vector),
                        ("pool", nc.gpsimd), ("sp", nc.sync)):
            sem = es.enter_context(nc.semaphore("s_" + name))
            self.eng[name] = dict(e=e, sem=sem, cnt=0, known={})
        self.ds = [dict(sem=es.enter_context(nc.semaphore("d%d" % i)), cnt=0) for i in range(NDS)]
        self.dn = {"sp": 0, "pool": 0}
        self.drange = {"sp": (0, 32), "pool": (32, NDS)}
        self.uid = 0

    def _sem(self, key):
        return self.eng[key]["sem"] if isinstance(key, str) else self.ds[key[1]]["sem"]

    def _wait(self, en, deps):
        E = self.eng[en]
        for key, val in deps:
            if val <= 0:
                continue
            if key == en and en == "pe":
                continue
            if E["known"].get(key, 0) >= val:
                continue
            E["e"].wait_ge(self._sem(key), val)
            E["known"][key] = val

    @staticmethod
    def _deps(r, w):
        deps = []
        for t in r:
            if t.res.w is not None:
                deps.append(t.res.w)
        for t in w:
            if t.res.w is not None:
                deps.append(t.res.w)
            deps.extend(t.res.r.items())
        return deps

    @staticmethod
    def _upd(tok, r, w):
        for t in r:
            t.res.r[tok[0]] = tok[1]
        for t in w:
            t.res.w = tok
            t.res.r = {}

    def op(self, en, fn, r=(), w=()):
        E = self.eng[en]
        self._wait(en, self._deps(r, w))
        ins = fn(E["e"])
        E["cnt"] += 1
        ins.then_inc(E["sem"], 1)
        self._upd((en, E["cnt"]), r, w)

    def dma(self, q, out, in_, r=(), w=()):
        E = self.eng[q]
        lo, hi = self.drange[q]
        i = lo + self.dn[q]
        self.dn[q] = (self.dn[q] + 1) % (hi - lo)
        d = self.ds[i]
        deps = self._deps(r, w)
        deps.append((("d", i), d["cnt"]))
        self._wait(q, deps)
        ins = E["e"].dma_start(out=out, in_=in_)
        d["cnt"] += 16
        ins.then_inc(d["sem"], 16)
        self._upd((("d", i), d["cnt"]), r, w)

    def barrier(self, full=False):
        allv = [(n, e["cnt"]) for n, e in self.eng.items() if n != "sp"]
        allv += [(("d", i), d["cnt"]) for i, d in enumerate(self.ds) if full or i < 32]
        for en in self.eng:
            E = self.eng[en]
            for key, val in allv:
                if val <= 0 or key == en:
                    continue
                if E["known"].get(key, 0) >= val:
                    continue
                E["e"].wait_ge(self._sem(key), val)
                E["known"][key] = val

    def sb(self, es, shape, dt, name=None):
        self.uid += 1
        return TT(es.enter_context(self.nc.sbuf_tensor("%s_%d" % (name or "t", self.uid), list(shape), dt)))


def bc_rows(dram_ap, n):
    return bass.AP(dram_ap.tensor, dram_ap.offset, [[0, 128], [1, n]])


def build():
    import os
    LEVEL = int(os.environ.get("KLEVEL", "99"))
    nc = bass.Bass("TRN2", target_bir_lowering=False)

    def din(name, shape, dt=F32):
        return nc.dram_tensor(name, list(shape), dt, kind="ExternalInput").ap()

    def dout(name, shape, dt=F32):
        return nc.dram_tensor(name, list(shape), dt, kind="ExternalOutput").ap()

    def dscr(name, shape, dt=BF16):
        return nc.dram_tensor(name, list(shape), dt, kind="Internal").ap()

    xcat = din("xcat", [NT * 128, D])
    xsin = din("xs", [128, D])
    posd = din("pos", [128, 65])
    roped = din("rope", [65, 128, 64])
    ck = din("ck", [4, 2048, 512])
    cv = din("cv", [4, 2048, 512])
    sconv = din("sconv", [4, 3, D])
    sssm = din("sssm", [4, 512, 128])
    cmk = din("cmk", [4, 256, D])
    cmv = din("cmv", [4, 256, D])
    memd = din("mem", [256, D])
    w_in = din("w_in", [D, 3080])
    w_out = din("w_out", [D, D])
    w_xq = din("w_xq", [D, D])
    w_mk = din("w_mk", [D, D])
    w_mv = din("w_mv", [D, D])
    w_xo = din("w_xo", [D, D])
    w_up = din("w_up", [D, 4096])
    w_down = din("w_down", [4096, D])
    gcols_d = din("gcols", [128, 32])
    gfin_d = din("gfin", [1, D])
    convw_d = din("convw", [4, D])
    convb_d = din("convb", [1, D])
    small_d = din("small", [3, 8])
    gssm_d = din("gssm", [1, 512])
    c_ident = din("c_ident", [128, 128])
    c_tri = din("c_tri", [128, 128])
    c_mneg = din("c_mneg", [128, 128])
    c_cm = din("c_cm", [128, 17 * 128])
    c_sh = din("c_sh", [128, 7 * 128])
    c_shs = din("c_shs", [128, 4 * 128])
    c_cs = din("c_cs", [128, 64 * 16])
    c_cn = din("c_cn", [128, 16])
    c_cc = din("c_cc", [128, 4 * 16])
    c_rm = din("c_rm", [128, 4])
    y_o = dout("y", [NOWN * 128, D])
    ys_o = dout("ys", [128, D])
    wk_o = dout("wk", [NOWN * 128, 512])
    wv_o = dout("wv", [NOWN * 128, 512])
    convp_o = dout("convp", [3, D])
    ssmp_o = dout("ssmp", [512, 128])
    memk_o = dout("memk", [256, D])
    memv_o = dout("memv", [256, D])
    wks_o = dout("wks", [4, 2048, 512])
    wvs_o = dout("wvs", [4, 2048, 512])
    convs_o = dout("convs", [4, 3, D])
    ssms_o = dout("ssms", [4, 512, 128])
    wb_in = dscr("wb_in", [D, 3080])
    wb_out = dscr("wb_out", [D, D])
    wb_xq = dscr("wb_xq", [D, D])
    wb_mk = dscr("wb_mk", [D, D])
    wb_mv = dscr("wb_mv", [D, D])
    wb_xo = dscr("wb_xo", [D, D])
    wb_up = dscr("wb_up", [D, 4096])
    wb_down = dscr("wb_down", [4096, D])

    with ExitStack() as es:
        K = KB(nc, es)
        wres = {}

        def convert(items):
            for src, dst, rows in items:
                if dst.name not in wres:
                    wres[dst.name] = TT(None)
                step = 512
                for r0 in range(0, rows, step):
                    K.dma("pool", dst[r0:r0 + step, :], src[r0:r0 + step, :], w=[wres[dst.name]])

        wres["in_a"] = TT(None)
        wres["in_b"] = TT(None)
        for r0 in range(0, D, 512):
            K.dma("pool", wb_in[r0:r0 + 512, 0:1536], w_in[r0:r0 + 512, 0:1536], w=[wres["in_a"]])
        convert([(w_mk, wb_mk, D), (w_mv, wb_mv, D)])
        for r0 in range(0, D, 512):
            K.dma("pool", wb_in[r0:r0 + 512, 1536:3080], w_in[r0:r0 + 512, 1536:3080], w=[wres["in_b"]])
        for _w in (wb_out, wb_xq, wb_xo, wb_up, wb_down):
            wres[_w.name] = TT(None)

        late_q = []
        for _src, _dst, _rows in ((w_out, wb_out, D), (w_xq, wb_xq, D), (w_xo, wb_xo, D), (w_up, wb_up, D),
                                  (w_down, wb_down, 4096)):
            for r0 in range(0, _rows, 512):
                late_q.append(lambda _src=_src, _dst=_dst, r0=r0: K.dma(
                    "pool", _dst[r0:r0 + 512, :], _src[r0:r0 + 512, :], w=[wres[_dst.name]]))
        for bl in range(4):
            late_q.append(lambda bl=bl: K.dma("pool", wks_o[bl, 0:2044, :], ck[bl, 4:2048, :]))
            late_q.append(lambda bl=bl: K.dma("pool", wvs_o[bl, 0:2044, :], cv[bl, 4:2048, :]))

        def late_dmas(n=None):
            k = len(late_q) if n is None else min(n, len(late_q))
            for _ in range(k):
                late_q.pop(0)()
        pf = [TT(es.enter_context(nc.psum_tensor("pf%d" % i, [128, 512], F32))) for i in range(6)]
        pb = [TT(es.enter_context(nc.psum_tensor("pb%d" % i, [128, 1024], BF16))) for i in range(2)]
        st_ = dict(pf=0, pb=0)

        def npf():
            st_["pf"] = (st_["pf"] + 1) % 4
            return pf[st_["pf"]]

        def npb():
            st_["pb"] = (st_["pb"] + 1) % 2
            return pb[st_["pb"]]

        def cload(src_ap, shape, dt=F32, q="sp", stk=None):
            stk = stk if stk is not None else es
            t = K.sb(stk, shape, F32, "cst")
            K.dma(q, t[:], src_ap, w=[t])
            if dt == F32:
                return t
            tb = K.sb(stk, shape, dt, "cstb")
            K.op("dve", lambda e: e.tensor_copy(tb[:], t[:]), r=[t], w=[tb])
            return tb

        def cload_bf(stk, specs):
            outs = [K.sb(stk, shape, BF16, "cbf") for _, shape in specs]
            with ExitStack() as tmp:
                for (src_ap, shape), o in zip(specs, outs):
                    t = K.sb(tmp, shape, F32, "cst")
                    K.dma("sp", t[:], src_ap, w=[t])
                    K.op("dve", lambda e, o=o, t=t: e.tensor_copy(o[:], t[:]), r=[t], w=[o])
                K.barrier()
            return outs

        identf = cload(c_ident, [128, 128])
        identb = K.sb(es, [128, 128], BF16, "identb")
        K.op("dve", lambda e: e.tensor_copy(identb[:], identf[:]), r=[identf], w=[identb])
        onesf = K.sb(es, [128, 128], F32, "onesf")
        K.op("dve", lambda e: e.memset(onesf[:], 1.0), w=[onesf])
        epsT = K.sb(es, [128, 1], F32, "epsT")
        K.op("dve", lambda e: e.memset(epsT[:], EPS), w=[epsT])
        onesb = K.sb(es, [128, 128], BF16, "onesb")
        K.op("dve", lambda e: e.memset(onesb[:], 1.0), w=[onesb])
        gcols = cload(gcols_d, [128, 32])
        pos = cload(posd, [128, 65])
        valid = K.sb(es, [128, 65], F32, "valid")
        K.op("dve", lambda e: e.tensor_single_scalar(valid[:], pos[:], 0.0, ALU.is_ge), r=[pos], w=[valid])
        kbias = K.sb(es, [128, 65], F32, "kbias")
        K.op("dve", lambda e: e.tensor_scalar(kbias[:], valid[:], -1.0, -NEG, ALU.add, ALU.mult),
             r=[valid], w=[kbias])
        attT = K.sb(es, [128, 4, (NOWN + 1) * 128], BF16, "attT")
        yT = K.sb(es, [128, 4, (NOWN + 1) * 128], BF16, "yT")
        K.barrier()

        def wload(es2, src_b, kchunks, ncols, col0=0, name="w", q="sp", res=None):
            t = K.sb(es2, [128, kchunks, ncols], BF16, name)
            v = src_b.rearrange("(kc p) n -> p kc n", p=128)
            for kc0 in range(0, kchunks, 4):
                K.dma(q, t[:, kc0:kc0 + 4, :], v[:, kc0:kc0 + 4, col0:col0 + ncols], r=[wres[res or src_b.name]], w=[t])
            return t

        def norm_T(wk, x, gi, dst, dst_ap):
            junk, ss, rstd, xn = wk["junk"], wk["ss"], wk["rstd"], wk["xn"]
            K.op("act", lambda e: e.activation(junk[:], x[:], AF.Square, accum_out=ss[:, 0:1]), r=[x], w=[junk, ss])
            K.op("act", lambda e: e.activation(rstd[:], ss[:], AF.Ln, scale=1.0 / D, bias=epsT[:, 0:1]), r=[ss, epsT], w=[rstd])
            K.op("act", lambda e: e.activation(rstd[:], rstd[:], AF.Exp, scale=-0.5), r=[rstd], w=[rstd])
            K.op("act", lambda e: e.activation(xn[:], x[:], AF.Copy, scale=rstd[:, 0:1]), r=[x, rstd], w=[xn])
            p = npb()
            for j in range(8):
                K.op("pe", lambda e, j=j: e.transpose(p[:, j * 128:(j + 1) * 128], xn[:, j * 128:(j + 1) * 128], identb[:]),
                     r=[xn, identb], w=[p])
            g = gcols[:, gi * 8:(gi + 1) * 8].unsqueeze(2).to_broadcast([128, 8, 128])
            K.op("dve", lambda e: e.tensor_tensor(dst_ap, p[:, :].rearrange("p (k t) -> p k t", t=128), g, ALU.mult),
                 r=[p, gcols], w=[dst])
            return rstd

        def proj(ps, ncols, hT, hT_ap, W, col0):
            for kc in range(8):
                K.op("pe", lambda e, kc=kc: e.matmul(ps[:, 0:ncols], hT_ap[:, kc, :], W[:, kc, col0:col0 + ncols],
                                                       start=(kc == 0), stop=(kc == 7)),
                     r=[hT, W], w=[ps])

        def transpose4(src, src_ap_fn, dst, dst_ap):
            p = npb()
            for j in range(4):
                K.op("pe", lambda e, j=j: e.transpose(p[:, j * 128:(j + 1) * 128], src_ap_fn(j), identb[:]),
                     r=[src, identb], w=[p])
            K.op("act", lambda e: e.copy(dst_ap, p[:, 0:512]), r=[p], w=[dst])

        def mem_prepare(es2, wk, ktok, vtok, KTm, Vm):
            kb = wk["kb"]
            K.op("act", lambda e: e.copy(kb[:], ktok[:]), r=[ktok], w=[kb])
            K.op("pool", lambda e: e.tensor_copy(Vm[:], vtok[:]), r=[vtok], w=[Vm])
            for mb in range(2):
                for half in range(2):
                    p = npb()
                    for j in range(4):
                        c = half * 4 + j
                        K.op("pe", lambda e, j=j, c=c, mb=mb: e.transpose(p[:, j * 128:(j + 1) * 128],
                                                                         kb[:, mb, c * 128:(c + 1) * 128], identb[:]),
                             r=[kb, identb], w=[p])
                    K.op("dve", lambda e, mb=mb, half=half: e.tensor_copy(
                        KTm[:, half * 4:half * 4 + 4, mb * 128:(mb + 1) * 128],
                        p[:, 0:512].rearrange("p (c m) -> p c m", m=128)), r=[p], w=[KTm])

        def xattn(wk, qT, qT_ap, n, KTm, Vm, OT, OT_ap):
            def st1(h):
                Ps = []
                for mb in range(2):
                    s = npf()
                    for dc in range(2):
                        K.op("pe", lambda e, dc=dc, mb=mb, s=s: e.matmul(
                            s[:, 0:n], KTm[:, 2 * h + dc, mb * 128:(mb + 1) * 128], qT_ap[:, 2 * h + dc, :],
                            start=(dc == 0), stop=(dc == 1)), r=[KTm, qT], w=[s])
                    P = wk["xP"][2 * (h % 2) + mb]
                    K.op("act", lambda e, s=s, P=P: e.activation(P[:, 0:n], s[:, 0:n], AF.Exp), r=[s], w=[P])
                    Ps.append(P)
                return Ps

            def st2(h, Ps):
                dn = npf()
                for mb in range(2):
                    K.op("pe", lambda e, mb=mb: e.matmul(dn[:, 0:n], onesb[:], Ps[mb][:, 0:n],
                                                         start=(mb == 0), stop=(mb == 1)), r=[onesb, Ps[mb]], w=[dn])
                rec = wk["xrec"][h % 2]
                K.op("dve", lambda e: e.reciprocal(rec[:, 0:n], dn[:, 0:n]), r=[dn], w=[rec])
                for dc in range(2):
                    o = npf()
                    for mb in range(2):
                        K.op("pe", lambda e, mb=mb, dc=dc, o=o: e.matmul(
                            o[:, 0:n], Vm[:, mb, h * 256 + dc * 128:h * 256 + dc * 128 + 128], Ps[mb][:, 0:n],
                            start=(mb == 0), stop=(mb == 1)), r=[Vm, Ps[mb]], w=[o])
                    K.op("dve", lambda e, dc=dc, o=o: e.tensor_tensor(OT_ap[:, 2 * h + dc, :], o[:, 0:n], rec[:, 0:n], ALU.mult),
                         r=[o, rec], w=[OT])

            prev = None
            for h in range(4):
                Ps = st1(h)
                if prev is not None:
                    st2(*prev)
                prev = (h, Ps)
            st2(*prev)

        KTm_p = K.sb(es, [128, 8, 256], BF16, "KTm_p")
        Vm_p = K.sb(es, [128, 2, D], BF16, "Vm_p")

        with ExitStack() as e2:
          if LEVEL >= 2:
            NB = 3
            LAG = NB - 1
            NR = 18
            junk_ = K.sb(e2, [128, D], F32)
            wks_ = [dict(junk=junk_, ss=K.sb(e2, [128, 1], F32), rstd=K.sb(e2, [128, 1], F32),
                         xn=K.sb(e2, [128, D], BF16)) for _ in range(2)]
            cm, ccs, ccn, ccc = cload_bf(e2, [(c_cm, [128, 17 * 128]), (c_cs, [128, 64 * 16]), (c_cn, [128, 16]),
                                              (c_cc, [128, 4 * 16])])
            wqkv = wload(e2, wb_in, 8, 1536, name="wqkv", res="in_a")
            KT = [K.sb(e2, [128, 4, 128], BF16, "KT") for _ in range(NR)]
            VX = [K.sb(e2, [128, 8, 72], BF16, "VX") for _ in range(NR)]
            for v_ in VX:
                K.op("pool", lambda e, v_=v_: e.memset(v_[:], 1.0), w=[v_])
            xts = [K.sb(e2, [128, D], F32, "xt") for _ in range(2)]
            hTs = [K.sb(e2, [128, 8, 128], BF16) for _ in range(2)]
            css = [K.sb(e2, [128, 64], F32) for _ in range(2)]
            qks = [K.sb(e2, [128, 16, 64], F32) for _ in range(2)]
            ros = [K.sb(e2, [128, 16, 64], F32) for _ in range(2)]
            tas = [K.sb(e2, [128, 16, 32], F32)] * 2
            tbs = [K.sb(e2, [128, 16, 32], F32)] * 2
            qkbs = [K.sb(e2, [128, 16, 64], BF16) for _ in range(2)]
            QTs = [K.sb(e2, [128, 4, 128], BF16) for _ in range(2)]
            vfs = [K.sb(e2, [128, 512], F32) for _ in range(2)]
            Eb = [[K.sb(e2, [128, 512], BF16, "E") for _ in range(2)] for _ in range(NB)]
            Pb = [[K.sb(e2, [128, 512], BF16, "P") for _ in range(2)] for _ in range(NB)]
            rec8 = K.sb(e2, [128, 8], F32)
            att = K.sb(e2, [128, 512], BF16)
            Oacc = [pf[4], pf[5]]

            def qkv_tile(j, xsrc, full, slot, parts="ab"):
                par = j % 2
                xt, hT, cs, qk, ro, ta, tb, qkb, QT, vf = (xts[par], hTs[par], css[par], qks[par], ros[par], tas[par],
                                                           tbs[par], qkbs[par], QTs[par], vfs[par])
                if "a" in parts:
                    K.dma("sp", xt[:], xsrc, w=[xt])
                    K.dma("sp", cs[:], roped[j], w=[cs])
                    norm_T(wks_[par], xt, 0, hT, hT[:, :, :])
                if "b" not in parts:
                    return None, None
                for idx, c0 in ((0, 0), (1, 512)):
                    if idx == 0 and not full:
                        continue
                    ps = npf()
                    proj(ps, 512, hT, hT[:, :, :], wqkv, c0)
                    K.op("act", lambda e, ps=ps, idx=idx: e.copy(
                        qk[:, idx * 8:(idx + 1) * 8, :], ps[:, :].rearrange("p (h d) -> p h d", d=64)), r=[ps], w=[qk])
                ps = npf()
                proj(ps, 512, hT, hT[:, :, :], wqkv, 1024)
                K.op("act", lambda e: e.copy(vf[:], ps[:, :]), r=[ps], w=[vf])
                K.op("dve", lambda e: e.tensor_copy(VX[slot][:, :, 0:64], vf[:].rearrange("p (h d) -> p h d", d=64)),
                     r=[vf], w=[VX[slot]])
                h0 = 0 if full else 8
                nh = 16 - h0
                cosb = cs[:, 0:32].unsqueeze(1).to_broadcast([128, nh, 32])
                sinb = cs[:, 32:64].unsqueeze(1).to_broadcast([128, nh, 32])
                t1 = qk[:, h0:16, 0:32]
                t2 = qk[:, h0:16, 32:64]
                K.op("dve", lambda e: e.tensor_tensor(ta[:, h0:16, :], t1, cosb, ALU.mult), r=[qk, cs], w=[ta])
                K.op("dve", lambda e: e.tensor_tensor(tb[:, h0:16, :], t2, sinb, ALU.mult), r=[qk, cs], w=[tb])
                K.op("dve", lambda e: e.tensor_tensor(ro[:, h0:16, 0:32], ta[:, h0:16, :], tb[:, h0:16, :], ALU.subtract),
                     r=[ta, tb], w=[ro])
                K.op("dve", lambda e: e.tensor_tensor(ta[:, h0:16, :], t2, cosb, ALU.mult), r=[qk, cs], w=[ta])
                K.op("dve", lambda e: e.tensor_tensor(tb[:, h0:16, :], t1, sinb, ALU.mult), r=[qk, cs], w=[tb])
                K.op("dve", lambda e: e.tensor_tensor(ro[:, h0:16, 32:64], ta[:, h0:16, :], tb[:, h0:16, :], ALU.add),
                     r=[ta, tb], w=[ro])
                K.op("act", lambda e: e.copy(qkb[:, 8:16, :], ro[:, 8:16, :]), r=[ro], w=[qkb])
                transpose4(qkb, lambda jj: qkb[:, 8 + 2 * jj:10 + 2 * jj, :].rearrange("p h d -> p (h d)"),
                           KT[slot], KT[slot][:, :, :].rearrange("p c t -> p (c t)"))
                if full:
                    K.op("act", lambda e: e.mul(qkb[:, 0:8, :], ro[:, 0:8, :], 0.125), r=[ro], w=[qkb])
                    transpose4(qkb, lambda jj: qkb[:, 2 * jj:2 * jj + 2, :].rearrange("p h d -> p (h d)"),
                               QT, QT[:, :, :].rearrange("p c t -> p (c t)"))
                return ro, vf

            def unit_front(u, QT, KTt, nq, mask_ap_fn, bias_ap):
                for hg in range(2):
                    s = npf()
                    for hh in range(4):
                        K.op("pe", lambda e, hh=hh, hg=hg, s=s: e.matmul(
                            s[:, hh * nq:(hh + 1) * nq], KTt[hg * 64:hg * 64 + 64, hh, :],
                            QT[hg * 64:hg * 64 + 64, hh, 0:nq], start=True, stop=True), r=[KTt, QT], w=[s])
                    E = Eb[u % NB][hg]
                    if bias_ap is not None:
                        K.op("act", lambda e, E=E, s=s: e.activation(E[:, 0:4 * nq], s[:, 0:4 * nq], AF.Exp, bias=bias_ap),
                             r=[s, kbias], w=[E])
                    else:
                        K.op("act", lambda e, E=E, s=s: e.activation(E[:, 0:4 * nq], s[:, 0:4 * nq], AF.Exp), r=[s], w=[E])
                    P = Pb[u % NB][hg]
                    K.op("dve", lambda e, E=E, P=P: e.tensor_tensor(
                        P[:, 0:4 * nq].rearrange("p (h q) -> p h q", q=nq),
                        E[:, 0:4 * nq].rearrange("p (h q) -> p h q", q=nq), mask_ap_fn(), ALU.mult),
                         r=[E, cm, ccs, ccn, ccc], w=[P])

            def unit_back(u, VXt, nq, first, last):
                for hg in range(2):
                    P = Pb[u % NB][hg]
                    for hh in range(4):
                        h = 2 * hh + hg
                        K.op("pe", lambda e, hh=hh, h=h, hg=hg, P=P: e.matmul(
                            Oacc[hg][0:nq, hh * 128:hh * 128 + 65], P[:, hh * nq:(hh + 1) * nq], VXt[:, h, 0:65],
                            start=(first and hh == 0), stop=(last and hh == 3)), r=[P, VXt], w=[Oacc[hg]])

            def run_units(units, QT, nq):
                n = len(units)
                for i in range(n + LAG):
                    if i < n:
                        prep, ktf, vxf, mfn, bias = units[i]
                        if prep is not None:
                            prep()
                        unit_front(i, QT, ktf(), nq, mfn, bias)
                    k = i - LAG
                    if k >= 0:
                        unit_back(k, units[k][2](), nq, k == 0, k == n - 1)

            def attn_finish(nq, col0):
                for hg in range(2):
                    o3 = Oacc[hg][:, 0:512].rearrange("p (h d) -> p h d", d=128)
                    K.op("dve", lambda e: e.reciprocal(rec8[:, hg * 4:hg * 4 + 4].unsqueeze(2), o3[:, :, 64:65]),
                         r=[Oacc[hg]], w=[rec8])
                    K.op("dve", lambda e: e.tensor_tensor(
                        att[:, :].rearrange("p (pr hf d) -> p pr hf d", hf=2, d=64)[:, :, hg, :], o3[:, :, 0:64],
                        rec8[:, hg * 4:hg * 4 + 4].unsqueeze(2).to_broadcast([128, 4, 64]), ALU.mult),
                         r=[Oacc[hg], rec8], w=[att])
                transpose4(att, lambda jj: att[:, jj * 128:(jj + 1) * 128], attT,
                           attT[:, :, col0:col0 + 128].rearrange("p c t -> p c t"))

            fast = bool(os.environ.get("KFAST"))
            jlist = list(range(32, NT)) if not fast else list(range(46, 52))
            jmin = jlist[0]

            def do_attn(j):
                jj = j - NPRE
                units = []
                for o in range(16, -1, -1):
                    kt = j - o
                    if kt < jmin:
                        continue
                    sl = kt % NR
                    units.append((None, lambda sl=sl: KT[sl], lambda sl=sl: VX[sl],
                                  lambda o=o: cm[:, o * 128:(o + 1) * 128].unsqueeze(1).to_broadcast([128, 4, 128]),
                                  kbias[:, kt:kt + 1]))
                run_units(units, QTs[j % 2], 128)
                attn_finish(128, jj * 128)

            prev_full = None
            qkv_tile(jlist[0], xcat[jlist[0] * 128:(jlist[0] + 1) * 128, :], jlist[0] >= NPRE, jlist[0] % NR, parts="a")
            for ji, j in enumerate(jlist):
                full = j >= NPRE
                if ji + 1 < len(jlist):
                    jn = jlist[ji + 1]
                    qkv_tile(jn, xcat[jn * 128:(jn + 1) * 128, :], jn >= NPRE, jn % NR, parts="a")
                ro, vf = qkv_tile(j, xcat[j * 128:(j + 1) * 128, :], full, j % NR, parts="b")
                if full:
                    late_dmas(1)
                if full:
                    jj = j - NPRE
                    K.dma("sp", wk_o[jj * 128:(jj + 1) * 128, :], ro[:, 8:16, :].rearrange("p h d -> p (h d)"), r=[ro])
                    K.dma("sp", wv_o[jj * 128:(jj + 1) * 128, :], vf[:], r=[vf])
                if prev_full is not None:
                    do_attn(prev_full)
                prev_full = j if full else None
            if prev_full is not None:
                do_attn(prev_full)

            if LEVEL >= 3:
                NBK = NB + 1
                KTs = [K.sb(e2, [128, 4, 128], BF16) for _ in range(NBK)]
                VXs = [K.sb(e2, [128, 8, 72], BF16) for _ in range(NBK)]
                for v_ in VXs:
                    K.op("pool", lambda e, v_=v_: e.memset(v_[:], 1.0), w=[v_])
                ckf = [K.sb(e2, [128, 512], F32, "ckf") for _ in range(3)]
                cvf = [K.sb(e2, [128, 512], F32, "cvf") for _ in range(3)]
                ckb = [K.sb(e2, [128, 512], BF16) for _ in range(2)]
                sslot = 0
                ro, vf = qkv_tile(ST, xsin, True, sslot)
                for bl in range(4):
                    K.dma("sp", wks_o[bl, 2044:2048, :], ro[4 * bl:4 * bl + 4, 8:16, :].rearrange("p h d -> p (h d)"), r=[ro])
                    K.dma("sp", wvs_o[bl, 2044:2048, :], vf[4 * bl:4 * bl + 4, :], r=[vf])
                units = []
                n_ = 0

                def finish_prep(n_, a, b_):
                    kb_ = ckb[n_ % 2]
                    kts_, vxs_ = KTs[n_ % NBK], VXs[n_ % NBK]
                    K.op("pool", lambda e: e.tensor_copy(kb_[:], a[:]), r=[a], w=[kb_])
                    transpose4(kb_, lambda jj: kb_[:, jj * 128:(jj + 1) * 128], kts_,
                               kts_[:, :, :].rearrange("p c t -> p (c t)"))
                    K.op("pool", lambda e: e.tensor_copy(vxs_[:, :, 0:64], b_[:].rearrange("p (h d) -> p h d", d=64)),
                         r=[b_], w=[vxs_])

                for bl in range(4):
                    for g in range(3):
                        def prep(bl=bl, g=g, n_=n_):
                            a, b_ = ckf[n_ % 3], cvf[n_ % 3]
                            for tq in range(4):
                                K.dma("sp", a[32 * tq:32 * tq + 32, :], ck[bl, 512 * g + tq:512 * (g + 1):16, :], w=[a])
                                K.dma("sp", b_[32 * tq:32 * tq + 32, :], cv[bl, 512 * g + tq:512 * (g + 1):16, :], w=[b_])
                            finish_prep(n_, a, b_)
                        units.append((prep, lambda n_=n_: KTs[n_ % NBK], lambda n_=n_: VXs[n_ % NBK],
                                      lambda bl=bl: ccc[:, bl * 16:(bl + 1) * 16].unsqueeze(1).to_broadcast([128, 4, 16]),
                                      None))
                        n_ += 1
                    for kt in range(12, 16):
                        def prep(bl=bl, kt=kt, n_=n_):
                            a, b_ = ckf[n_ % 3], cvf[n_ % 3]
                            K.dma("sp", a[:], ck[bl, kt * 128:(kt + 1) * 128, :], w=[a])
                            K.dma("sp", b_[:], cv[bl, kt * 128:(kt + 1) * 128, :], w=[b_])
                            finish_prep(n_, a, b_)
                        units.append((prep, lambda n_=n_: KTs[n_ % NBK], lambda n_=n_: VXs[n_ % NBK],
                                      lambda bl=bl, kt=kt: ccs[:, (bl * 16 + kt) * 16:(bl * 16 + kt + 1) * 16].unsqueeze(1).to_broadcast([128, 4, 16]),
                                      None))
                        n_ += 1
                units.append((None, lambda: KT[sslot], lambda: VX[sslot],
                              lambda: ccn[:, 0:16].unsqueeze(1).to_broadcast([128, 4, 16]), None))
                run_units(units, QTs[ST % 2], 16)
                attn_finish(16, NOWN * 128)
            K.barrier()

        with ExitStack() as e2:
            wk = dict(junk=K.sb(e2, [128, D], F32), ss=K.sb(e2, [128, 1], F32), rstd=K.sb(e2, [128, 1], F32),
                      xn=K.sb(e2, [128, D], BF16), kb=K.sb(e2, [128, 2, D], BF16))
            wmk = wload(e2, wb_mk, 8, D, name="wmk")
            wmv = wload(e2, wb_mv, 8, D, name="wmv")
            mT = K.sb(e2, [128, 8, 256], BF16)
            ktok = K.sb(e2, [128, 2, D], F32)
            vtok = K.sb(e2, [128, 2, D], F32)
            for mb in range(2):
                xt = K.sb(e2, [128, D], F32)
                K.dma("sp", xt[:], memd[mb * 128:(mb + 1) * 128, :], w=[xt])
                norm_T(wk, xt, 3, mT, mT[:, :, mb * 128:(mb + 1) * 128])
            for mb in range(2):
                for W, tok in ((wmk, ktok), (wmv, vtok)):
                    for cg in range(2):
                        ps = npf()
                        proj(ps, 512, mT, mT[:, :, mb * 128:(mb + 1) * 128], W, cg * 512)
                        K.op("act", lambda e, ps=ps, tok=tok, cg=cg, mb=mb: e.copy(tok[:, mb, cg * 512:(cg + 1) * 512], ps[:, :]),
                             r=[ps], w=[tok])
            K.dma("sp", memk_o.rearrange("(m p) n -> p m n", p=128), ktok[:], r=[ktok])
            K.dma("sp", memv_o.rearrange("(m p) n -> p m n", p=128), vtok[:], r=[vtok])
            mem_prepare(e2, wk, ktok, vtok, KTm_p, Vm_p)
            K.barrier()

        STf = K.sb(es, [128, 512], F32, "STf")
        with ExitStack() as e2:
          if LEVEL >= 4:
            wk = dict(junk=K.sb(e2, [128, D], F32), ss=K.sb(e2, [128, 1], F32), rstd=K.sb(e2, [128, 1], F32),
                      xn=K.sb(e2, [128, D], BF16))
            tri = cload(c_tri, [128, 128], stk=e2)
            mneg = cload(c_mneg, [128, 128], stk=e2)
            sh = cload(c_sh, [128, 7 * 128], BF16, stk=e2)
            shs = cload(c_shs, [128, 4 * 128], BF16, stk=e2)
            rm = cload(c_rm, [128, 4], stk=e2)
            cw = K.sb(e2, [128, 4, D], F32)
            for i in range(4):
                K.dma("sp", cw[:, i, :], bc_rows(convw_d[i:i + 1, :], D), w=[cw])
            cbf = K.sb(e2, [1, D], F32, "cbf")
            K.dma("sp", cbf[:], convb_d, w=[cbf])
            cbrow = K.sb(e2, [1, D], BF16, "cbrow")
            K.op("dve", lambda e: e.tensor_copy(cbrow[:], cbf[:]), r=[cbf], w=[cbrow])
            gs = cload(bc_rows(gssm_d, 512), [128, 512], stk=e2)
            sm = K.sb(e2, [128, 3, 8], F32)
            for i in range(3):
                K.dma("sp", sm[:, i, :], bc_rows(small_d[i:i + 1, :], 8), w=[sm])
            Aneg = K.sb(e2, [128, 8], F32)
            K.op("act", lambda e: e.activation(Aneg[:], sm[:, 1, :], AF.Exp), r=[sm], w=[Aneg])
            K.op("dve", lambda e: e.tensor_scalar(Aneg[:], Aneg[:], -1.0, None, ALU.mult), r=[Aneg], w=[Aneg])
            wz = wload(e2, wb_in, 8, 1544, col0=1536, name="wz", res="in_b")
            xts = [K.sb(e2, [128, D], F32, "xt") for _ in range(2)]
            hTs_ = [K.sb(e2, [128, 8, 128], BF16) for _ in range(2)]
            xbcs_ = [K.sb(e2, [128, D], F32) for _ in range(3)]
            hT, xbc = hTs_[0], xbcs_[0]
            xw = [[K.sb(e2, [128, D], BF16, "xw") for _ in range(4)] for _ in range(2)]
            for t_ in xw[0] + xw[1]:
                K.op("pool", lambda e, t_=t_: e.memset(t_[:], 0.0), w=[t_])
            xcs_ = [K.sb(e2, [128, D], F32) for _ in range(3)]
            xc = xcs_[0]
            xcb = K.sb(e2, [128, 512], BF16)
            dtrs_ = [K.sb(e2, [128, 8], F32) for _ in range(4)]
            dtms_ = [K.sb(e2, [128, 8], F32) for _ in range(4)]
            dtr, dtm = dtrs_[0], dtms_[0]
            av = K.sb(e2, [128, 8], F32)
            acs = K.sb(e2, [128, 8], F32)
            nacs = K.sb(e2, [128, 8], F32)
            toe = K.sb(e2, [128, 8], F32)
            dec = K.sb(e2, [128, 8], F32)
            fs = K.sb(e2, [128, 8], F32)
            xdt = K.sb(e2, [128, 512], BF16)
            xw2 = K.sb(e2, [128, 512], BF16)
            STb = K.sb(e2, [128, 512], BF16)
            abc = K.sb(e2, [128, 8, 128], F32)
            T1 = K.sb(e2, [128, 8, 128], F32)
            Lm = K.sb(e2, [128, 8, 128], F32)
            scT = K.sb(e2, [128, 8, 128], BF16)
            BCT = K.sb(e2, [128, 4, 128], BF16)
            ysum = K.sb(e2, [128, 512], F32)
            ytmp = K.sb(e2, [128, 512], F32)
            yS = K.sb(e2, [128, 512], F32)
            zss_ = [K.sb(e2, [128, 512], F32) for _ in range(4)]
            zs = zss_[0]
            ss2 = K.sb(e2, [128, 2], F32)
            ybs_ = [K.sb(e2, [128, 512], BF16) for _ in range(2)]
            yb_state = dict(n=0, pending=[])
            X7 = K.sb(e2, [128, D], F32)
            stl = K.sb(e2, [128, 4, 128], F32)
            K.op("dve", lambda e: e.memset(STf[:], 0.0), w=[STf])
            K.op("dve", lambda e: e.memset(STb[:], 0.0), w=[STb])

            def conv_dt(j, cur, prev, shm, nsh_prev, src_x, full):
                full = full or j == NPRE - 1
                ncol = D if full else 768
                for i in range(4):
                    K.op("pool", lambda e, i=i: e.tensor_tensor(cur[i][:, 0:ncol], src_x[:, 0:ncol], cw[:, i, 0:ncol], ALU.mult),
                         r=[src_x, cw], w=[cur[i]])
                for cg in range(2):
                    ps = npf()
                    nw = 512 if (full or cg == 0) else 256
                    for i in range(4):
                        K.op("pe", lambda e, i=i, cg=cg, ps=ps, nw=nw: e.matmul(
                            ps[:, 0:nw], shm[:, i * 128:(i + 1) * 128], cur[i][:, cg * 512:cg * 512 + nw],
                            start=(i == 0), stop=False), r=[shm, cur[i]], w=[ps])
                    for i in range(nsh_prev):
                        K.op("pe", lambda e, i=i, cg=cg, ps=ps, nw=nw: e.matmul(
                            ps[:, 0:nw], shm[:, (4 + i) * 128:(5 + i) * 128], prev[i][:, cg * 512:cg * 512 + nw],
                            start=False, stop=False), r=[shm, prev[i]], w=[ps])
                    K.op("pe", lambda e, cg=cg, ps=ps, nw=nw: e.matmul(
                        ps[:, 0:nw], onesb[0:1, :], cbrow[0:1, cg * 512:cg * 512 + nw], start=False, stop=True),
                         r=[onesb, cbrow], w=[ps])
                    K.op("act", lambda e, cg=cg, ps=ps, nw=nw: e.activation(xc[:, cg * 512:cg * 512 + nw], ps[:, 0:nw], AF.Silu),
                         r=[ps], w=[xc])

            def ssd_chunk(full, dtm_ap_t, ST_f, ST_b, parts="abc"):
                nbc = 512 if full else 256
                if "a" in parts:
                    K.op("dve", lambda e: e.tensor_tensor(av[:], dtm_ap_t[:], Aneg[:], ALU.mult), r=[dtm_ap_t, Aneg], w=[av])
                    p1 = npf()
                    K.op("pe", lambda e: e.matmul(p1[:, 0:8], onesf[:], av[:], start=True, stop=True), r=[onesf, av], w=[p1])
                    K.op("pe", lambda e: e.matmul(p1[:, 8:16], tri[:], av[:], start=True, stop=True), r=[tri, av], w=[p1])
                    K.op("act", lambda e: e.copy(acs[:], p1[:, 8:16]), r=[p1], w=[acs])
                    K.op("dve", lambda e: e.tensor_tensor(toe[:], p1[:, 0:8], acs[:], ALU.subtract), r=[p1, acs], w=[toe])
                    K.op("act", lambda e: e.activation(toe[:], toe[:], AF.Exp), r=[toe], w=[toe])
                    K.op("act", lambda e: e.activation(dec[:], p1[:, 0:8], AF.Exp), r=[p1], w=[dec])
                    xs3 = xc[:, 0:512].rearrange("p (h d) -> p h d", d=64)
                    K.op("dve", lambda e: e.tensor_tensor(xdt[:].rearrange("p (h d) -> p h d", d=64), xs3,
                                                          dtm_ap_t[:].unsqueeze(2).to_broadcast([128, 8, 64]), ALU.mult),
                         r=[xc, dtm_ap_t], w=[xdt])
                    K.op("dve", lambda e: e.tensor_tensor(xw2[:].rearrange("p (h d) -> p h d", d=64),
                                                          xdt[:].rearrange("p (h d) -> p h d", d=64),
                                                          toe[:].unsqueeze(2).to_broadcast([128, 8, 64]), ALU.mult),
                         r=[xdt, toe], w=[xw2])
                    nbc = 512 if full else 256
                    K.op("dve", lambda e: e.tensor_copy(xcb[:, 0:nbc], xc[:, 512:512 + nbc]), r=[xc], w=[xcb])
                if full and "b" in parts:
                    transpose4(xcb, lambda jj: xcb[:, jj * 128:(jj + 1) * 128], BCT, BCT[:, :, :].rearrange("p c t -> p (c t)"))
                    K.op("act", lambda e: e.mul(nacs[:], acs[:], -1.0), r=[acs], w=[nacs])
                    K.op("act", lambda e: e.activation(fs[:], acs[:], AF.Exp), r=[acs], w=[fs])
                    K.op("pool", lambda e: e.tensor_tensor(abc[:], av[:].unsqueeze(2).to_broadcast([128, 8, 128]),
                                                          tri[:].unsqueeze(1).to_broadcast([128, 8, 128]), ALU.mult),
                         r=[av, tri], w=[abc])
                    g_ = npf()
                    for g in range(2):
                        K.op("pe", lambda e, g=g: e.matmul(g_[:, g * 128:(g + 1) * 128], BCT[:, g, :], BCT[:, 2 + g, :],
                                                           start=True, stop=True), r=[BCT], w=[g_])
                    for half in range(2):
                        R = npf()
                        K.op("pe", lambda e, half=half, R=R: e.matmul(
                            R[:, :], onesf[:], abc[:, half * 4:half * 4 + 4, :].rearrange("p h l -> p (h l)"),
                            start=True, stop=True), r=[onesf, abc], w=[R])
                        K.op("dve", lambda e, half=half, R=R: e.tensor_tensor(
                            T1[:, half * 4:half * 4 + 4, :], R[:, :].rearrange("p (h l) -> p h l", l=128),
                            mneg[:].unsqueeze(1).to_broadcast([128, 4, 128]), ALU.add), r=[R, mneg], w=[T1])
                    for h in range(8):
                        K.op("act", lambda e, h=h: e.activation(Lm[:, h, :], T1[:, h, :], AF.Exp, bias=nacs[:, h:h + 1]),
                             r=[T1, nacs], w=[Lm])
                    for g in range(2):
                        K.op("dve", lambda e, g=g: e.tensor_tensor(
                            scT[:, g * 4:g * 4 + 4, :], Lm[:, g * 4:g * 4 + 4, :],
                            g_[:, g * 128:(g + 1) * 128].unsqueeze(1).to_broadcast([128, 4, 128]), ALU.mult),
                             r=[Lm, g_], w=[scT])
                    yd = npf()
                    for h in range(8):
                        K.op("pe", lambda e, h=h: e.matmul(yd[:, h * 64:(h + 1) * 64], scT[:, h, :],
                                                           xdt[:, h * 64:(h + 1) * 64], start=True, stop=True),
                             r=[scT, xdt], w=[yd])
                    K.op("act", lambda e: e.copy(ysum[:], yd[:, :]), r=[yd], w=[ysum])
                if full and "c" in parts:
                    yo = npf()
                    for h in range(8):
                        K.op("pe", lambda e, h=h: e.matmul(yo[:, h * 64:(h + 1) * 64], BCT[:, 2 + h // 4, :],
                                                           ST_b[:, h * 64:(h + 1) * 64], start=True, stop=True),
                             r=[BCT, ST_b], w=[yo])
                    K.op("dve", lambda e: e.tensor_tensor(ytmp[:].rearrange("p (h d) -> p h d", d=64),
                                                          yo[:, :].rearrange("p (h d) -> p h d", d=64),
                                                          fs[:].unsqueeze(2).to_broadcast([128, 8, 64]), ALU.mult),
                         r=[yo, fs], w=[ytmp])
                    K.op("dve", lambda e: e.tensor_tensor(ysum[:], ysum[:], ytmp[:], ALU.add), r=[ysum, ytmp], w=[ysum])
                if "c" in parts:
                    cs_ = npf()
                    for g in range(2):
                        K.op("pe", lambda e, g=g: e.matmul(cs_[:, g * 256:(g + 1) * 256], xcb[:, g * 128:(g + 1) * 128],
                                                           xw2[:, g * 256:(g + 1) * 256], start=True, stop=True),
                             r=[xcb, xw2], w=[cs_])
                    K.op("dve", lambda e: e.tensor_tensor(ST_f[:].rearrange("p (h d) -> p h d", d=64),
                                                          ST_f[:].rearrange("p (h d) -> p h d", d=64),
                                                          dec[:].unsqueeze(2).to_broadcast([128, 8, 64]), ALU.mult),
                         r=[ST_f, dec], w=[ST_f])
                    K.op("dve", lambda e: e.tensor_tensor(ST_f[:], ST_f[:], cs_[:, :], ALU.add), r=[ST_f, cs_], w=[ST_f])
                    K.op("act", lambda e: e.copy(ST_b[:], ST_f[:]), r=[ST_f], w=[ST_b])

            def flush_yT():
                for yb_, c0_ in yb_state["pending"]:
                    transpose4(yb_, lambda jj, yb_=yb_: yb_[:, jj * 128:(jj + 1) * 128], yT,
                               yT[:, :, c0_:c0_ + 128].rearrange("p c t -> p c t"))
                yb_state["pending"] = []

            def ssd_post(ysrc, zps, col0, defer=False):
                K.op("dve", lambda e: e.tensor_tensor(ytmp[:].rearrange("p (h d) -> p h d", d=64),
                                                      xc[:, 0:512].rearrange("p (h d) -> p h d", d=64),
                                                      sm[:, 2, :].unsqueeze(2).to_broadcast([128, 8, 64]), ALU.mult),
                     r=[xc, sm], w=[ytmp])
                K.op("dve", lambda e: e.tensor_tensor(ytmp[:], ytmp[:], ysrc[:], ALU.add), r=[ytmp, ysrc], w=[ytmp])
                K.op("dve", lambda e: e.tensor_tensor(ytmp[:], ytmp[:], zs[:], ALU.mult), r=[ytmp, zs], w=[ytmp])
                for g in range(2):
                    K.op("act", lambda e, g=g: e.activation(zs[:, g * 256:(g + 1) * 256], ytmp[:, g * 256:(g + 1) * 256], AF.Square,
                                                            accum_out=ss2[:, g:g + 1]), r=[ytmp], w=[zs, ss2])
                K.op("act", lambda e: e.activation(ss2[:], ss2[:], AF.Ln, scale=1.0 / 256, bias=epsT[:, 0:1]), r=[ss2, epsT], w=[ss2])
                K.op("act", lambda e: e.activation(ss2[:], ss2[:], AF.Exp, scale=-0.5), r=[ss2], w=[ss2])
                K.op("dve", lambda e: e.tensor_tensor(ytmp[:].rearrange("p (g d) -> p g d", d=256),
                                                      ytmp[:].rearrange("p (g d) -> p g d", d=256),
                                                      ss2[:].unsqueeze(2).to_broadcast([128, 2, 256]), ALU.mult),
                     r=[ytmp, ss2], w=[ytmp])
                yb = ybs_[yb_state["n"] % 2]
                yb_state["n"] += 1
                K.op("dve", lambda e: e.tensor_tensor(yb[:], ytmp[:], gs[:], ALU.mult), r=[ytmp, gs], w=[yb])
                yb_state["pending"].append((yb, col0))
                if not defer:
                    flush_yT()

            def dt_chain(ps8, vcol):
                K.op("dve", lambda e: e.tensor_tensor(dtr[:], ps8, sm[:, 0, :], ALU.add), r=[sm], w=[dtr])
                K.op("act", lambda e: e.activation(dtr[:], dtr[:], AF.Exp), r=[dtr], w=[dtr])
                K.op("act", lambda e: e.activation(dtr[:], dtr[:], AF.Ln, bias=1.0), r=[dtr], w=[dtr])
                K.op("dve", lambda e: e.tensor_scalar(dtm[:], dtr[:], vcol, None, ALU.mult), r=[dtr, valid, rm], w=[dtm])

            def state_out(ST_f, dst):
                p = npf()
                for c in range(4):
                    K.op("pe", lambda e, c=c: e.transpose(p[:, c * 128:(c + 1) * 128], ST_f[:, c * 128:(c + 1) * 128], identf[:]),
                         r=[ST_f, identf], w=[p])
                K.op("act", lambda e: e.copy(stl[:].rearrange("p c n -> p (c n)"), p[:, :]), r=[p], w=[stl])
                K.dma("sp", dst.rearrange("(c p) n -> p c n", p=128), stl[:], r=[stl])

            def front(j, xsrc, full, parts="ab"):
                if "a" in parts:
                    xt = xts[j % 2]
                    K.dma("sp", xt[:], xsrc, w=[xt])
                    norm_T(wk, xt, 0, hT, hT[:, :, :])
                if "b" not in parts:
                    return None, None
                for cg in range(2):
                    ps = npf()
                    nw = 512 if (full or j == NPRE - 1 or cg == 0) else 256
                    proj(ps, nw, hT, hT[:, :, :], wz, 512 + cg * 512)
                    K.op("dve", lambda e, ps=ps, cg=cg, nw=nw: e.tensor_copy(xbc[:, cg * 512:cg * 512 + nw], ps[:, 0:nw]), r=[ps], w=[xbc])
                pd = npf()
                proj(pd, 8, hT, hT[:, :, :], wz, 1536)
                pz = None
                if full:
                    pz = npf()
                    proj(pz, 512, hT, hT[:, :, :], wz, 0)
                    K.op("act", lambda e: e.activation(zs[:], pz[:, :], AF.Silu), r=[pz], w=[zs])
                return pd, pz

            def bind(j):
                nonlocal hT, xbc, xc, dtr, dtm, zs
                hT, xbc, xc = hTs_[j % 2], xbcs_[j % 3], xcs_[j % 3]
                dtr, dtm, zs = dtrs_[j % 4], dtms_[j % 4], zss_[j % 4]

            def stage_F1(j):
                bind(j)
                front(j, xcat[j * 128:(j + 1) * 128, :], j >= NPRE, parts="a")

            def stage_F2(j):
                bind(j)
                full = j >= NPRE
                pd, pz = front(j, xcat[j * 128:(j + 1) * 128, :], full, parts="b")
                pd_ap = pd[:, 0:8]
                K.op("dve", lambda e: e.tensor_tensor(dtr[:], pd_ap, sm[:, 0, :], ALU.add), r=[pd, sm], w=[dtr])
                K.op("act", lambda e: e.activation(dtr[:], dtr[:], AF.Exp), r=[dtr], w=[dtr])
                K.op("act", lambda e: e.activation(dtr[:], dtr[:], AF.Ln, bias=1.0), r=[dtr], w=[dtr])
                K.op("dve", lambda e: e.tensor_scalar(dtm[:], dtr[:], valid[:, j:j + 1], None, ALU.mult), r=[dtr, valid], w=[dtm])

            def stage_C(j):
                bind(j)
                conv_dt(j, xw[j % 2], xw[(j + 1) % 2], sh, 3, xbc, j >= NPRE)
                if j == NT - 1:
                    K.dma("sp", convp_o, xbc[125:128, :], r=[xbc])

            def stage_S(j, part):
                bind(j)
                full = j >= NPRE
                if part == "c":
                    flush_yT()
                ssd_chunk(full, dtm, STf, STb, parts=part)
                if full and part == "c":
                    ssd_post(ysum, None, (j - NPRE) * 128, defer=True)

            jl = list(range(NT)) if not os.environ.get("KFAST") else list(range(46, 52))
            nj = len(jl)
            stage_F1(jl[0])
            LC, LS = 1, 3
            for it in range(nj + LS):
                if it % 4 == 0:
                    late_dmas(1)
                if it + 1 < nj:
                    stage_F1(jl[it + 1])
                if it >= LS:
                    stage_S(jl[it - LS], "a")
                if LC <= it < nj + LC:
                    stage_C(jl[it - LC])
                if it >= LS:
                    stage_S(jl[it - LS], "b")
                if it < nj:
                    stage_F2(jl[it])
                if it >= LS:
                    stage_S(jl[it - LS], "c")
            flush_yT()
            late_dmas()
            bind(0)
            state_out(STf, ssmp_o)

            if LEVEL >= 5:
                pd, pz = front(ST, xsin, True)
                pd_ap = pd[:, 0:8]
                K.op("dve", lambda e: e.tensor_tensor(dtr[:], pd_ap, sm[:, 0, :], ALU.add), r=[pd, sm], w=[dtr])
                K.op("act", lambda e: e.activation(dtr[:], dtr[:], AF.Exp), r=[dtr], w=[dtr])
                K.op("act", lambda e: e.activation(dtr[:], dtr[:], AF.Ln, bias=1.0), r=[dtr], w=[dtr])
                K.op("dve", lambda e: e.memset(X7[:], 0.0), w=[X7])
                for bl in range(4):
                    K.dma("sp", X7[7 * bl:7 * bl + 3, :], sconv[bl], w=[X7])
                    K.dma("sp", X7[7 * bl + 3:7 * bl + 7, :], xbc[4 * bl:4 * bl + 4, :], r=[xbc], w=[X7])
                for bl in range(4):
                    K.dma("sp", convs_o[bl], X7[7 * bl + 4:7 * bl + 7, :], r=[X7])
                conv_dt(ST, xw[0], xw[1], shs, 0, X7, True)
                K.op("dve", lambda e: e.memset(yS[:], 0.0), w=[yS])
                STs = K.sb(e2, [128, 512], F32)
                STsb = K.sb(e2, [128, 512], BF16)
                for bl in range(4):
                    K.dma("sp", stl[:], sssm[bl].rearrange("(c p) n -> p c n", p=128), w=[stl])
                    p = npf()
                    for c in range(4):
                        K.op("pe", lambda e, c=c: e.transpose(p[:, c * 128:(c + 1) * 128], stl[:, c, :], identf[:]),
                             r=[stl, identf], w=[p])
                    K.op("act", lambda e: e.copy(STs[:], p[:, :]), r=[p], w=[STs])
                    K.op("dve", lambda e: e.tensor_copy(STsb[:], STs[:]), r=[STs], w=[STsb])
                    K.op("dve", lambda e, bl=bl: e.tensor_scalar(dtm[:], dtr[:], rm[:, bl:bl + 1], None, ALU.mult),
                         r=[dtr, rm], w=[dtm])
                    ssd_chunk(True, dtm, STs, STsb)
                    K.op("dve", lambda e, bl=bl: e.scalar_tensor_tensor(yS[:], ysum[:], rm[:, bl:bl + 1], yS[:], ALU.mult, ALU.add),
                         r=[ysum, rm, yS], w=[yS])
                    state_out(STs, ssms_o[bl])
                ssd_post(yS, None, NOWN * 128)
            K.barrier()

        gfin = cload(bc_rows(gfin_d, D), [128, D])
        KTm_s = K.sb(es, [128, 8, 256], BF16, "KTm_s")
        Vm_s = K.sb(es, [128, 2, D], BF16, "Vm_s")
        wo = wload(es, wb_out, 8, D, name="wo")
        wq = wload(es, wb_xq, 8, D, name="wq")
        wxo = wload(es, wb_xo, 8, D, name="wxo")
        wup = [K.sb(es, [128, 8, 512], BF16, "wup") for _ in range(2)]
        wdn = [K.sb(es, [128, 4, D], BF16, "wdn") for _ in range(2)]
        upv = wb_up.rearrange("(kc p) n -> p kc n", p=128)
        dnv = wb_down.rearrange("(fc p) n -> p fc n", p=128)

        def load_up(ch):
            W = wup[ch % 2]
            K.dma("sp", W[:, 0:4, :], upv[:, 0:4, ch * 512:(ch + 1) * 512], r=[wres[wb_up.name]], w=[W])
            K.dma("sp", W[:, 4:8, :], upv[:, 4:8, ch * 512:(ch + 1) * 512], r=[wres[wb_up.name]], w=[W])

        def load_dn(ch):
            W = wdn[ch % 2]
            K.dma("sp", W[:, :, :], dnv[:, ch * 4:ch * 4 + 4, :], r=[wres[wb_down.name]], w=[W])

        groups = [(g * 4, 4) for g in range(4)] + [(NOWN, 1)]
        if LEVEL < 6:
            groups = []
        elif LEVEL == 6:
            groups = groups[:1]
        elif LEVEL == 7:
            groups = [groups[0], groups[4]]
        for gi_, (t0, ntl) in enumerate(groups):
            n = ntl * 128
            is_s = ntl == 1
            with ExitStack() as eg:
                resid = [K.sb(eg, [128, D], F32, "resid") for _ in range(ntl)]
                hmT = K.sb(eg, [128, 8, n], BF16, "hmT")
                with ExitStack() as e2:
                    wk = dict(junk=K.sb(e2, [128, D], F32), ss=K.sb(e2, [128, 1], F32), rstd=K.sb(e2, [128, 1], F32),
                              xn=K.sb(e2, [128, D], BF16), kb=(K.sb(e2, [128, 2, D], BF16) if is_s else None),
                              xP=[K.sb(e2, [128, 512], BF16, "xP") for _ in range(4)],
                              xrec=[K.sb(e2, [128, 512], F32, "xrec") for _ in range(2)])
                    load_up(0)
                    load_up(1)
                    load_dn(0)
                    load_dn(1)
                    hxT = K.sb(e2, [128, 8, n], BF16)
                    qT = K.sb(e2, [128, 8, n], BF16)
                    OT = K.sb(e2, [128, 8, n], BF16)
                    for tl in range(ntl):
                        src = xsin if is_s else xcat[(NPRE + t0 + tl) * 128:(NPRE + t0 + tl + 1) * 128, :]
                        K.dma("sp", resid[tl][:], src, w=[resid[tl]])
                    for tl in range(ntl):
                        c0 = (t0 + tl) * 128
                        for cg in range(2):
                            ps = npf()
                            for kc in range(8):
                                srcT = attT if kc < 4 else yT
                                K.op("pe", lambda e, kc=kc, cg=cg, ps=ps, srcT=srcT: e.matmul(
                                    ps[:, :], srcT[:, kc % 4, c0:c0 + 128], wo[:, kc, cg * 512:(cg + 1) * 512],
                                    start=(kc == 0), stop=(kc == 7)), r=[srcT, wo], w=[ps])
                            K.op("dve", lambda e, cg=cg, ps=ps, tl=tl: e.tensor_tensor(
                                resid[tl][:, cg * 512:(cg + 1) * 512], ps[:, :], resid[tl][:, cg * 512:(cg + 1) * 512], ALU.add),
                                 r=[ps, resid[tl]], w=[resid[tl]])
                    for tl in range(ntl):
                        norm_T(wk, resid[tl], 1, hxT, hxT[:, :, tl * 128:(tl + 1) * 128])
                    for fc in range(8):
                        ps = npf()
                        for kc in range(8):
                            K.op("pe", lambda e, kc=kc, fc=fc, ps=ps: e.matmul(
                                ps[:, 0:n], wq[:, kc, fc * 128:(fc + 1) * 128], hxT[:, kc, :],
                                start=(kc == 0), stop=(kc == 7)), r=[wq, hxT], w=[ps])
                        K.op("act", lambda e, fc=fc, ps=ps: e.mul(qT[:, fc, :], ps[:, 0:n], 1.0 / 16), r=[ps], w=[qT])
                    if not is_s:
                        xattn(wk, qT, qT[:, :, :], n, KTm_p, Vm_p, OT, OT[:, :, :])
                    else:
                        K.op("dve", lambda e: e.memset(OT[:], 0.0), w=[OT])
                        ktoks = [K.sb(e2, [128, 2, D], F32) for _ in range(2)]
                        vtoks = [K.sb(e2, [128, 2, D], F32) for _ in range(2)]
                        KTm2 = [KTm_s, KTm_s]
                        Vm2 = [Vm_s, Vm_s]

                        def mload(bl):
                            K.dma("sp", ktoks[bl % 2][:], cmk[bl].rearrange("(m p) n -> p m n", p=128), w=[ktoks[bl % 2]])
                            K.dma("sp", vtoks[bl % 2][:], cmv[bl].rearrange("(m p) n -> p m n", p=128), w=[vtoks[bl % 2]])

                        mload(0)
                        mload(1)
                        for bl in range(4):
                            mem_prepare(e2, wk, ktoks[bl % 2], vtoks[bl % 2], KTm2[bl % 2], Vm2[bl % 2])
                            if bl + 2 < 4:
                                mload(bl + 2)
                            xattn(wk, qT, qT[:, :, 4 * bl:4 * bl + 4], 4, KTm2[bl % 2], Vm2[bl % 2], OT, OT[:, :, 4 * bl:4 * bl + 4])
                    for tl in range(ntl):
                        for cg in range(2):
                            ps = npf()
                            for kc in range(8):
                                K.op("pe", lambda e, kc=kc, cg=cg, ps=ps, tl=tl: e.matmul(
                                    ps[:, :], OT[:, kc, tl * 128:(tl + 1) * 128], wxo[:, kc, cg * 512:(cg + 1) * 512],
                                    start=(kc == 0), stop=(kc == 7)), r=[OT, wxo], w=[ps])
                            K.op("dve", lambda e, cg=cg, ps=ps, tl=tl: e.tensor_tensor(
                                resid[tl][:, cg * 512:(cg + 1) * 512], ps[:, :], resid[tl][:, cg * 512:(cg + 1) * 512], ALU.add),
                                 r=[ps, resid[tl]], w=[resid[tl]])
                    for tl in range(ntl):
                        norm_T(wk, resid[tl], 2, hmT, hmT[:, :, tl * 128:(tl + 1) * 128])
                    K.barrier()
                with ExitStack() as e2:
                    wk = dict(junk=K.sb(e2, [128, D], F32), ss=K.sb(e2, [128, 1], F32), rstd=K.sb(e2, [128, 1], F32),
                              xn=K.sb(e2, [128, D], BF16))
                    hidT = K.sb(e2, [128, 32, n], BF16)
                    yt = wk["junk"]
                    rls = [K.sb(e2, [128, 512], F32, "rl") for _ in range(2)]
                    for ch in range(8):
                        W = wup[ch % 2]
                        for f4 in range(4):
                            fc = ch * 4 + f4
                            ps = npf()
                            for kc in range(8):
                                K.op("pe", lambda e, kc=kc, f4=f4, ps=ps, W=W: e.matmul(
                                    ps[:, 0:n], W[:, kc, f4 * 128:(f4 + 1) * 128], hmT[:, kc, :],
                                    start=(kc == 0), stop=(kc == 7)), r=[W, hmT], w=[ps])
                            rl = rls[fc % 2]
                            K.op("act", lambda e, ps=ps, rl=rl: e.activation(rl[:, 0:n], ps[:, 0:n], AF.Relu), r=[ps], w=[rl])
                            if fc % 2:
                                K.op("act", lambda e, fc=fc, rl=rl: e.activation(hidT[:, fc, :], rl[:, 0:n], AF.Square), r=[rl], w=[hidT])
                            else:
                                K.op("dve", lambda e, fc=fc, rl=rl: e.tensor_tensor(
                                    hidT[:, fc, :], rl[:, 0:n], rl[:, 0:n], ALU.mult), r=[rl], w=[hidT])
                        if ch + 2 < 8:
                            load_up(ch + 2)
                    for ch in range(8):
                        W = wdn[ch % 2]
                        for tl in range(ntl):
                            for cg in range(2):
                                ps = npf()
                                for f4 in range(4):
                                    K.op("pe", lambda e, f4=f4, cg=cg, ps=ps, tl=tl, W=W, ch=ch: e.matmul(
                                        ps[:, :], hidT[:, ch * 4 + f4, tl * 128:(tl + 1) * 128], W[:, f4, cg * 512:(cg + 1) * 512],
                                        start=(f4 == 0), stop=(f4 == 3)), r=[hidT, W], w=[ps])
                                K.op("dve", lambda e, cg=cg, ps=ps, tl=tl: e.tensor_tensor(
                                    resid[tl][:, cg * 512:(cg + 1) * 512], ps[:, :], resid[tl][:, cg * 512:(cg + 1) * 512], ALU.add),
                                     r=[ps, resid[tl]], w=[resid[tl]])
                        if ch + 2 < 8:
                            load_dn(ch + 2)
                    junk, ss, rstd = wk["junk"], wk["ss"], wk["rstd"]
                    for tl in range(ntl):
                        x = resid[tl]
                        K.op("act", lambda e, x=x: e.activation(junk[:], x[:], AF.Square, accum_out=ss[:, 0:1]), r=[x], w=[junk, ss])
                        K.op("act", lambda e: e.activation(rstd[:], ss[:], AF.Ln, scale=1.0 / D, bias=epsT[:, 0:1]), r=[ss, epsT], w=[rstd])
                        K.op("act", lambda e: e.activation(rstd[:], rstd[:], AF.Exp, scale=-0.5), r=[rstd], w=[rstd])
                        K.op("dve", lambda e, x=x: e.scalar_tensor_tensor(yt[:], x[:], rstd[:, 0:1], gfin[:], ALU.mult, ALU.mult),
                             r=[x, rstd, gfin], w=[yt])
                        dst = ys_o if is_s else y_o[(t0 + tl) * 128:(t0 + tl + 1) * 128, :]
                        K.dma("sp", dst, yt[:], r=[yt])
                    K.barrier()
        K.barrier(full=True)
    return nc


def _mult(delta):
    d = delta
    c = ((d >= 0) & (d <= 128)).astype(np.float32)
    c += ((d >= 0) & (d % 4 == 0) & (d <= 512)).astype(np.float32)
    c += ((d >= 0) & (d % 16 == 0) & (d <= 2048)).astype(np.float32)
    return c


def _consts():
    i = np.arange(128)
    c = {}
    c["c_ident"] = np.eye(128, dtype=np.float32)
    c["c_tri"] = (i[:, None] <= i[None, :]).astype(np.float32)
    c["c_mneg"] = np.where(i[:, None] <= i[None, :], 0.0, NEG).astype(np.float32)
    cm = np.zeros((128, 17 * 128), np.float32)
    for o in range(17):
        cm[:, o * 128:(o + 1) * 128] = _mult(128 * o + i[None, :] - i[:, None])
    c["c_cm"] = cm
    sh = np.zeros((128, 7 * 128), np.float32)
    for t in range(4):
        m = (i[:, None] == i[None, :] - 3 + t)
        sh[:, t * 128:(t + 1) * 128] = m
    for t in range(3):
        m = (i[:, None] - 128 == i[None, :] - 3 + t)
        sh[:, (4 + t) * 128:(5 + t) * 128] = m
    c["c_sh"] = sh
    shs = np.zeros((128, 4 * 128), np.float32)
    for t in range(4):
        for bl in range(4):
            for tt in range(4):
                shs[7 * bl + tt + t, t * 128 + 4 * bl + tt] = 1.0
    c["c_shs"] = shs
    cs = np.zeros((128, 64 * 16), np.float32)
    for bl in range(4):
        for kt in range(16):
            blk = np.zeros((128, 16), np.float32)
            for t in range(4):
                blk[:, 4 * bl + t] = _mult(2048 + t - 128 * kt - i)
            cs[:, (bl * 16 + kt) * 16:(bl * 16 + kt + 1) * 16] = blk
    c["c_cs"] = cs
    cn = np.zeros((128, 16), np.float32)
    for bl in range(4):
        for tk in range(4):
            for tq in range(4):
                if tq >= tk:
                    cn[4 * bl + tk, 4 * bl + tq] = _mult(np.array(tq - tk))
    c["c_cn"] = cn
    cc = np.zeros((128, 4 * 16), np.float32)
    for bl in range(4):
        for t in range(4):
            cc[32 * t:32 * t + 32, bl * 16 + 4 * bl + t] = 1.0
    c["c_cc"] = cc
    rm = np.zeros((128, 4), np.float32)
    for bl in range(4):
        rm[4 * bl:4 * bl + 4, bl] = 1.0
    c["c_rm"] = rm
    return c


_NC = None


def kernel(x_prompt, x_sample, cache_win_k, cache_win_v, state_conv, state_ssm,
           cache_mem_k, cache_mem_v, mem_prompt,
           g_mix, w_in, conv_w, conv_b, dt_bias, a_log, d_skip, g_ssm, w_out,
           g_xatt, g_mem, w_xq, w_mk, w_mv, w_xo, g_mlp, w_up, w_down, g_final):
    global _NC
    f = lambda a: np.ascontiguousarray(np.asarray(a, dtype=np.float32))
    x_prompt, x_sample = f(x_prompt), f(x_sample)
    consts = _consts()
    gc = np.concatenate([f(g)[0].reshape(8, 128).T for g in (g_mix, g_xatt, g_mlp, g_mem)], axis=1)
    shared = dict(w_in=f(w_in)[0], w_out=f(w_out)[0], w_xq=f(w_xq)[0], w_mk=f(w_mk)[0], w_mv=f(w_mv)[0],
                  w_xo=f(w_xo)[0], w_up=f(w_up)[0], w_down=f(w_down)[0], gcols=np.ascontiguousarray(gc),
                  gfin=f(g_final).reshape(1, D), convw=f(conv_w)[0], convb=f(conv_b)[0].reshape(1, D),
                  small=np.stack([f(dt_bias)[0], f(a_log)[0], f(d_skip)[0]]), gssm=f(g_ssm)[0].reshape(1, 512))
    shared.update(consts)
    half = 32
    inv = (10000.0 ** (-np.arange(half, dtype=np.float32) * 2.0 / 64)).astype(np.float32)
    in_maps = []
    for core in range(8):
        b, c = core // 4, core % 4
        start = 2048 * c
        p0 = start - NPRE * 128
        xc = np.zeros((NT * 128, D), np.float32)
        lo = max(p0, 0)
        xc[lo - p0:] = x_prompt[b, lo:start + 2048]
        pos = np.zeros((128, 65), np.float32)
        pos[:, :NT] = p0 + 128 * np.arange(NT)[None, :] + np.arange(128)[:, None]
        pos[:, 64] = -1.0
        pos[:16, 64] = 8192 + (np.arange(16) % 4)
        pp = np.maximum(pos, 0.0).T.astype(np.float32)
        ang = pp[:, :, None] * inv[None, None, :]
        rope = np.concatenate([np.cos(ang), np.sin(ang)], axis=-1).astype(np.float32)
        xs = np.zeros((128, D), np.float32)
        xs[:16] = x_sample[4 * core:4 * core + 4].reshape(16, D)
        m = dict(shared)
        m.update(xcat=xc, xs=xs, pos=pos, rope=np.ascontiguousarray(rope),
                 ck=f(cache_win_k)[0, 4 * core:4 * core + 4].reshape(4, 2048, 512),
                 cv=f(cache_win_v)[0, 4 * core:4 * core + 4].reshape(4, 2048, 512),
                 sconv=f(state_conv)[0, 4 * core:4 * core + 4],
                 sssm=f(state_ssm)[0, 4 * core:4 * core + 4].reshape(4, 512, 128),
                 cmk=f(cache_mem_k)[0, 4 * core:4 * core + 4].reshape(4, 256, D),
                 cmv=f(cache_mem_v)[0, 4 * core:4 * core + 4].reshape(4, 256, D),
                 mem=f(mem_prompt)[b])
        in_maps.append({k: np.ascontiguousarray(v) for k, v in m.items()})
    if _NC is None:
        _NC = build()
    import os
    ncores = int(os.environ.get("KCORES", "8"))
    if ncores < 8:
        c0 = int(os.environ.get("KCORE", "3"))
        res = run_bass_kernel_spmd(_NC, in_maps[c0:c0 + 1], core_ids=[0])
        return res.results[0]
    res = run_bass_kernel_spmd(_NC, in_maps, core_ids=list(range(8)))
    R = res.results
    y_prompt = np.stack([np.concatenate([R[4 * b + c]["y"] for c in range(4)], axis=0) for b in range(2)])
    y_sample = np.concatenate([R[k]["ys"][:16].reshape(4, 4, D) for k in range(8)], axis=0)
    wkp = np.stack([R[4 * b + 3]["wk"].reshape(2048, 8, 64) for b in range(2)])[None]
    wvp = np.stack([R[4 * b + 3]["wv"].reshape(2048, 8, 64) for b in range(2)])[None]
    cvp = np.stack([R[4 * b + 3]["convp"] for b in range(2)])[None]
    ssp = np.stack([R[4 * b + 3]["ssmp"].reshape(8, 64, 128) for b in range(2)])[None]
    mkp = np.stack([R[4 * b]["memk"].reshape(256, 4, 256) for b in range(2)])[None]
    mvp = np.stack([R[4 * b]["memv"].reshape(256, 4, 256) for b in range(2)])[None]
    wks = np.concatenate([R[k]["wks"].reshape(4, 2048, 8, 64) for k in range(8)], axis=0)[None]
    wvs = np.concatenate([R[k]["wvs"].reshape(4, 2048, 8, 64) for k in range(8)], axis=0)[None]
    cvs = np.concatenate([R[k]["convs"] for k in range(8)], axis=0)[None]
    sss = np.concatenate([R[k]["ssms"].reshape(4, 8, 64, 128) for k in range(8)], axis=0)[None]
    return tuple(np.ascontiguousarray(a.astype(np.float32)) for a in
                 (y_prompt, y_sample, wkp, wvp, cvp, ssp, mkp, mvp, wks, wvs, cvs, sss))
```
